# Optimizing a Trainium2 kernel written in Bass

```python
import math
import jax, jax.numpy as jnp
from jax import lax
import numpy as np

D_MODEL = 1024
BATCH = 4
SEQ = 4096
DEPTH = 4

CHUNK = 64
Q_BLOCK = 128
N_MIXERS = 3

MLA_HEADS = 16
QK_NOPE = 64
QK_ROPE = 32
V_HEAD = 64
Q_LORA = 384
KV_LORA = 256
ROPE_THETA = 10000.0

CONV_WIDTH = 31

POOL_WINDOWS = (2, 4, 8, 16)
POOL_GROUPS = len(POOL_WINDOWS)

D_FF = 4 * D_MODEL

NORM_EPS = 1e-6
NEG_INF = -1e30

kernel_name = "hybrid_mla_conformer_pool_trunk"


def _rmsnorm(x, g):
    x32 = x.astype(jnp.float32)
    y = x32 * lax.rsqrt(jnp.mean(x32 * x32, axis=-1, keepdims=True) + NORM_EPS)
    return (y * g.astype(jnp.float32)).astype(x.dtype)


def _layernorm(x, g, b):
    x32 = x.astype(jnp.float32)
    mu = jnp.mean(x32, axis=-1, keepdims=True)
    xc = x32 - mu
    y = xc * lax.rsqrt(jnp.mean(xc * xc, axis=-1, keepdims=True) + NORM_EPS)
    return (y * g.astype(jnp.float32) + b.astype(jnp.float32)).astype(x.dtype)


def _rope_tables(positions):
    inv_freq = ROPE_THETA ** (-jnp.arange(0, QK_ROPE, 2, dtype=jnp.float32) / QK_ROPE)
    ang = positions.astype(jnp.float32)[..., None] * inv_freq
    return jnp.cos(ang), jnp.sin(ang)


def _apply_rope(x, cos, sin):
    half = x.shape[-1] // 2
    x1 = x[..., :half].astype(jnp.float32)
    x2 = x[..., half:].astype(jnp.float32)
    out = jnp.concatenate([x1 * cos - x2 * sin, x1 * sin + x2 * cos], axis=-1)
    return out.astype(x.dtype)


def _chunk_causal_attention(q, k, v):
    B, S, H, Dk = q.shape
    nb = S // Q_BLOCK
    qb = q.reshape(B, nb, Q_BLOCK, H, Dk).swapaxes(0, 1)
    k_chunk = jnp.arange(S) // CHUNK
    scale = 1.0 / math.sqrt(Dk)

    def one_block(args):
        qblk, bi = args
        q_chunk = (bi * Q_BLOCK + jnp.arange(Q_BLOCK)) // CHUNK
        mask = k_chunk[None, :] <= q_chunk[:, None]
        s = jnp.einsum('bqhd,bkhd->bhqk', qblk, k).astype(jnp.float32) * scale
        s = jnp.where(mask[None, None], s, NEG_INF)
        p = jax.nn.softmax(s, axis=-1).astype(v.dtype)
        return jnp.einsum('bhqk,bkhd->bqhd', p, v)

    o = lax.map(one_block, (qb, jnp.arange(nb)))
    return o.swapaxes(0, 1).reshape(B, S, H, v.shape[-1])


def _mla(h, positions, w_dq, q_norm_g, w_uq, w_dkv, kv_norm_g, w_ukv, w_o):
    B, S, _ = h.shape
    cq = _rmsnorm(h @ w_dq, q_norm_g)
    q = (cq @ w_uq).reshape(B, S, MLA_HEADS, QK_NOPE + QK_ROPE)
    q_nope, q_rope = q[..., :QK_NOPE], q[..., QK_NOPE:]
    ckv_all = h @ w_dkv
    ckv = _rmsnorm(ckv_all[..., :KV_LORA], kv_norm_g)
    k_rope = ckv_all[..., KV_LORA:]
    kv = (ckv @ w_ukv).reshape(B, S, MLA_HEADS, QK_NOPE + V_HEAD)
    k_nope, v = kv[..., :QK_NOPE], kv[..., QK_NOPE:]
    cos, sin = _rope_tables(positions)
    q_rope = _apply_rope(q_rope, cos[:, :, None], sin[:, :, None])
    k_rope = _apply_rope(k_rope, cos, sin)
    qf = jnp.concatenate([q_nope, q_rope], axis=-1)
    kf = jnp.concatenate(
        [k_nope, jnp.broadcast_to(k_rope[:, :, None], (B, S, MLA_HEADS, QK_ROPE))], axis=-1)
    o = _chunk_causal_attention(qf, kf, v)
    return o.reshape(B, S, MLA_HEADS * V_HEAD) @ w_o


def _conformer_conv(h, w_pw1, b_pw1, w_dw, b_dw, ln_g, ln_b, w_pw2, b_pw2):
    D = h.shape[-1]
    a = h @ w_pw1 + b_pw1
    u = a[..., :D] * jax.nn.sigmoid(a[..., D:])
    u = lax.conv_general_dilated(
        u, w_dw[:, None, :].astype(u.dtype), window_strides=(1,),
        padding=[(CONV_WIDTH - 1, 0)], dimension_numbers=('NWC', 'WIO', 'NWC'),
        feature_group_count=D) + b_dw
    u = jax.nn.silu(_layernorm(u, ln_g, ln_b))
    return u @ w_pw2 + b_pw2


def _pool_mixer(h, w, b, scale):
    B, S, D = h.shape
    C = D // POOL_GROUPS
    csum = jnp.cumsum(h.astype(jnp.float32), axis=1)
    t = jnp.arange(S)
    pooled = []
    for g, win in enumerate(POOL_WINDOWS):
        cs = csum[..., g * C:(g + 1) * C]
        lag = jnp.pad(cs, ((0, 0), (win, 0), (0, 0)))[:, :S]
        cnt = jnp.minimum(t + 1, win).astype(jnp.float32)
        pooled.append((cs - lag) / cnt[None, :, None])
    p = jnp.concatenate(pooled, axis=-1).astype(h.dtype) - h
    y = jnp.einsum('bsgc,gcd->bsgd', p.reshape(B, S, POOL_GROUPS, C), w) + b
    return y.reshape(B, S, D) * scale


def _sq_relu_mlp(h, w1, w2):
    return jnp.square(jax.nn.relu(h @ w1)) @ w2


def setup_inputs(seed: int = 0) -> dict:
    key = jax.random.key(seed)
    ks = jax.random.split(key, 32)
    D = D_MODEL
    n_mla = (DEPTH + 2) // 3
    n_conv = (DEPTH + 1) // 3
    n_pool = DEPTH // 3
    C = D // POOL_GROUPS

    def nrm(k, shape, fan_in, mult=1.0):
        return jax.random.normal(k, shape, jnp.float32) * (mult * fan_in ** -0.5)

    def gain(k, shape):
        return 1.0 + 0.05 * jax.random.normal(k, shape, jnp.float32)

    def bias(k, shape):
        return 0.02 * jax.random.normal(k, shape, jnp.float32)

    x = jax.random.normal(ks[0], (BATCH, SEQ, D), jnp.float32)
    c = jax.random.normal(ks[1], (BATCH, D), jnp.float32)
    offsets = jax.random.randint(ks[2], (BATCH,), 0, 64, dtype=jnp.int32) * CHUNK
    positions = offsets[:, None] + jnp.arange(SEQ, dtype=jnp.int32)[None, :]
    return {
        "x": x,
        "c": c,
        "positions": positions,
        "ada_w": nrm(ks[3], (DEPTH, D, 6 * D), D, 0.5),
        "ada_b": bias(ks[4], (DEPTH, 6 * D)),
        "norm_g": gain(ks[5], (DEPTH, 4, D)),
        "mla_w_dq": nrm(ks[6], (n_mla, D, Q_LORA), D),
        "mla_q_norm_g": gain(ks[7], (n_mla, Q_LORA)),
        "mla_w_uq": nrm(ks[8], (n_mla, Q_LORA, MLA_HEADS * (QK_NOPE + QK_ROPE)), Q_LORA),
        "mla_w_dkv": nrm(ks[9], (n_mla, D, KV_LORA + QK_ROPE), D),
        "mla_kv_norm_g": gain(ks[10], (n_mla, KV_LORA)),
        "mla_w_ukv": nrm(ks[11], (n_mla, KV_LORA, MLA_HEADS * (QK_NOPE + V_HEAD)), KV_LORA),
        "mla_w_o": nrm(ks[12], (n_mla, MLA_HEADS * V_HEAD, D), MLA_HEADS * V_HEAD),
        "conv_w_pw1": nrm(ks[13], (n_conv, D, 2 * D), D),
        "conv_b_pw1": bias(ks[14], (n_conv, 2 * D)),
        "conv_w_dw": nrm(ks[15], (n_conv, CONV_WIDTH, D), CONV_WIDTH),
        "conv_b_dw": bias(ks[16], (n_conv, D)),
        "conv_ln_g": gain(ks[17], (n_conv, D)),
        "conv_ln_b": bias(ks[18], (n_conv, D)),
        "conv_w_pw2": nrm(ks[19], (n_conv, D, D), D),
        "conv_b_pw2": bias(ks[20], (n_conv, D)),
        "pool_w": nrm(ks[21], (n_pool, POOL_GROUPS, C, C), C),
        "pool_b": bias(ks[22], (n_pool, POOL_GROUPS, C)),
        "pool_scale": gain(ks[23], (n_pool, D)),
        "ffn_w1": nrm(ks[24], (DEPTH, D, D_FF), D),
        "ffn_w2": nrm(ks[25], (DEPTH, D_FF, D), D_FF),
    }


def reference(x, c, positions, ada_w, ada_b, norm_g,
              mla_w_dq, mla_q_norm_g, mla_w_uq, mla_w_dkv, mla_kv_norm_g, mla_w_ukv, mla_w_o,
              conv_w_pw1, conv_b_pw1, conv_w_dw, conv_b_dw, conv_ln_g, conv_ln_b,
              conv_w_pw2, conv_b_pw2,
              pool_w, pool_b, pool_scale,
              ffn_w1, ffn_w2):
    c_act = jax.nn.silu(c)
    for i in range(DEPTH):
        kind = i % N_MIXERS
        j = i // N_MIXERS
        mod = c_act @ ada_w[i] + ada_b[i]
        sh_m, sc_m, gt_m, sh_f, sc_f, gt_f = jnp.split(mod, 6, axis=-1)

        h = _rmsnorm(x, norm_g[i, 0]) * (1.0 + sc_m[:, None]) + sh_m[:, None]
        if kind == 0:
            y = _mla(h, positions, mla_w_dq[j], mla_q_norm_g[j], mla_w_uq[j],
                     mla_w_dkv[j], mla_kv_norm_g[j], mla_w_ukv[j], mla_w_o[j])
        elif kind == 1:
            y = _conformer_conv(h, conv_w_pw1[j], conv_b_pw1[j], conv_w_dw[j], conv_b_dw[j],
                                conv_ln_g[j], conv_ln_b[j], conv_w_pw2[j], conv_b_pw2[j])
        else:
            y = _pool_mixer(h, pool_w[j], pool_b[j], pool_scale[j])
        x = x + gt_m[:, None] * _rmsnorm(y, norm_g[i, 1])

        h = _rmsnorm(x, norm_g[i, 2]) * (1.0 + sc_f[:, None]) + sh_f[:, None]
        y = _sq_relu_mlp(h, ffn_w1[i], ffn_w2[i])
        x = x + gt_f[:, None] * _rmsnorm(y, norm_g[i, 3])
    return x
```

```python
import numpy as np
import ml_dtypes
import concourse.bass as bass
import concourse.mybir as mybir
from concourse.bass_utils import run_bass_kernel_spmd

F32 = mybir.dt.float32
BF16 = mybir.dt.bfloat16
I32 = mybir.dt.int32
AF = mybir.ActivationFunctionType
ALU = mybir.AluOpType
AX = mybir.AxisListType

D = 1024
DC = 8
T = 2048
TB = 512
NTB = 4
DEPTH = 4
H = 16
DFF = 4096
EPS = 1e-6
QL = 384
KVL = 256
NEG = -30000.0
SCALE = 1.0 / float(np.sqrt(96.0))
CW = 31
PWIN = (2, 4, 8, 16)


class Buf:
    __slots__ = ("name", "w", "r")

    def __init__(self, name):
        self.name = name
        self.w = None
        self.r = {}


class KB:
    def __init__(self, nc):
        self.nc = nc
        self.eng = {"pe": nc.tensor, "act": nc.scalar, "dve": nc.vector,
                    "pool": nc.gpsimd, "sp": nc.sync}
        self.sems = {}
        self.cnt = {}
        for e in self.eng:
            self.sems[e] = nc.alloc_semaphore("c_" + e)
            self.cnt[e] = 0
        self.waited = {e: {} for e in self.eng}
        self.pending = {e: [] for e in self.eng}
        self.ndma = 0
        self.nsem = 0
        self.groups = {}

    def dsem(self, name):
        key = "d_" + name
        if key not in self.sems:
            self.sems[key] = self.nc.alloc_semaphore(key)
            self.cnt[key] = 0
        return key

    def _need(self, eng, evs):
        for k, v in evs.items():
            if self.waited[eng].get(k, 0) >= v:
                continue
            self.eng[eng].wait_ge(self.sems[k], v)
            self.waited[eng][k] = v

    def _deps(self, eng, reads, writes):
        evs = {}

        def add(ev, raw):
            if ev is None:
                return
            k, v = ev
            if k == "PENDING":
                assert v == eng and (eng == "pe" or not raw), "dependency on un-flushed op"
                return
            if k == eng:
                if eng in ("pe", "sp"):
                    return
                if not raw:
                    return
            if evs.get(k, 0) < v:
                evs[k] = v
        for b in reads:
            add(b.w, True)
        for b in writes:
            add(b.w, False)
            for k, v in b.r.items():
                add((k, v), False)
        return evs

    def _commit(self, ev, reads, writes):
        for b in reads:
            if b.r.get(ev[0], 0) < ev[1]:
                b.r[ev[0]] = ev[1]
        for b in writes:
            b.w = ev
            b.r = {}

    def op(self, eng, fn, reads=(), writes=(), inc=True):
        reads = list(reads)
        writes = list(writes)
        self._need(eng, self._deps(eng, reads, writes))
        ins = fn(self.eng[eng])
        if inc:
            self.cnt[eng] += 1
            ins.then_inc(self.sems[eng], 1)
            ev = (eng, self.cnt[eng])
            for (r, w) in self.pending[eng]:
                self._commit(ev, r, w)
            self.pending[eng] = []
            self._commit(ev, reads, writes)
        else:
            for b in writes:
                b.w = ("PENDING", eng)
            self.pending[eng].append((reads, writes))
        return ins

    def dma(self, sem, out, in_, reads=(), writes=(), q="sp", hold=False, **kw):
        reads = list(reads)
        writes = list(writes)
        grp = self.groups.setdefault(sem, [])
        if not grp and self.cnt[sem] > 0:
            self._need(q, {sem: self.cnt[sem]})
        self._need(q, self._deps(q, reads, writes))
        self.eng[q].dma_start(out=out, in_=in_, **kw).then_inc(self.sems[sem], 16)
        self.cnt[sem] += 16
        self.ndma += 1
        for b in writes:
            b.w = ("PENDING", "dma")
        grp.append((reads, writes))
        if not hold:
            self.dma_flush(sem)

    def dma_flush(self, sem):
        ev = (sem, self.cnt[sem])
        for (r, w) in self.groups.get(sem, []):
            self._commit(ev, r, w)
        self.groups[sem] = []

    def barrier(self):
        for e in self.eng:
            assert not self.pending[e]
        assert not any(self.groups.values()), "open dma group at barrier"
        evs = {k: v for k, v in self.cnt.items() if v > 0}
        for e in self.eng:
            self._need(e, {k: v for k, v in evs.items() if not (k == e and e in ("pe", "sp"))})

    def final_wait(self):
        evs = {k: v for k, v in self.cnt.items() if v > 0 and k != "sp"}
        self._need("sp", evs)


class Tl:
    def __init__(self, t, name):
        self.t = t
        self.name = name
        self.bufs = {}

    def b(self, key=0):
        if key not in self.bufs:
            self.bufs[key] = Buf("%s.%s" % (self.name, key))
        return self.bufs[key]

    def bs(self, keys):
        return [self.b(k) for k in keys]

    def __getitem__(self, idx):
        return self.t[idx]


def _bf16(a):
    return np.asarray(a, dtype=np.float32).astype(ml_dtypes.bfloat16)


class Prog:
    def __init__(self, layers_p1=(0, 1, 2), layers_p2=(0, 1, 2, 3), debug=None, mixers=True, ffn=True, warm=False):
        self.warm = warm
        self.do_mixers = mixers
        self.do_ffn = ffn
        self.layers_p1 = tuple(layers_p1)
        self.layers_p2 = tuple(layers_p2)
        self.debug = debug
        self.nc = bass.Bass("TRN2", target_bir_lowering=False)
        self.K = KB(self.nc)
        self.uid = 0
        self.build()

    def dram_in(self, name, shape, dt=F32):
        return self.nc.dram_tensor(name, list(shape), dt, kind="ExternalInput").ap()

    def dram_out(self, name, shape, dt=F32):
        return self.nc.dram_tensor(name, list(shape), dt, kind="ExternalOutput").ap()

    def dram_tmp(self, name, shape, dt):
        return self.nc.dram_tensor(name, list(shape), dt).ap()

    def sb(self, name, shape, dt=F32):
        self.uid += 1
        nm = "%s_%d" % (name, self.uid)
        return Tl(self.nc.alloc_sbuf_tensor(nm, list(shape), dt), nm)

    def sbuf_scope(self):
        return _Scope(self)

    def build(self):
        nc, K = self.nc, self.K
        I = {}
        I["xT_prev"] = self.dram_in("xT_prev", [D, T])
        I["xT_own"] = self.dram_in("xT_own", [D, T])
        I["pos_prev"] = self.dram_in("pos_prev", [32, T], I32)
        I["pos_own"] = self.dram_in("pos_own", [32, T], I32)
        I["c"] = self.dram_in("c", [128, DC])
        I["flags"] = self.dram_in("flags", [128, 4])
        I["poolfac"] = self.dram_in("poolfac", [128, 2, 4, 16])
        I["consts"] = self.dram_in("consts", [128, 128 + 8])
        I["ada_w"] = self.dram_in("ada_w", [DEPTH, D, 6 * D])
        I["ada_bT"] = self.dram_in("ada_bT", [128, DEPTH * 48])
        I["norm_gT"] = self.dram_in("norm_gT", [128, DEPTH * 4 * DC])
        I["w_dq"] = self.dram_in("w_dq", [2, D, QL])
        I["qn_gT"] = self.dram_in("qn_gT", [128, 2 * 3])
        I["w_uqA"] = self.dram_in("w_uqA", [2, QL, H * 96])
        I["w_uqB"] = self.dram_in("w_uqB", [2, QL, H * 32])
        I["w_dkvA"] = self.dram_in("w_dkvA", [2, D, 288])
        I["w_dkvB"] = self.dram_in("w_dkvB", [2, D, 32])
        I["kvn_gT"] = self.dram_in("kvn_gT", [128, 2 * 2])
        I["w_ukvK"] = self.dram_in("w_ukvK", [2, KVL, H * 64])
        I["w_ukvV"] = self.dram_in("w_ukvV", [2, KVL, H * 64])
        I["w_o"] = self.dram_in("w_o", [2, D, D])
        I["w_pw1"] = self.dram_in("w_pw1", [D, 2 * D])
        I["b_pw1T"] = self.dram_in("b_pw1T", [128, 16])
        I["w_dwT"] = self.dram_in("w_dwT", [128, DC, CW])
        I["cv_vecT"] = self.dram_in("cv_vecT", [128, 4 * DC])
        I["w_pw2"] = self.dram_in("w_pw2", [D, D])
        I["pool_w"] = self.dram_in("pool_w", [4, 256, 256])
        I["pl_vecT"] = self.dram_in("pl_vecT", [128, 2 * DC])
        I["ffn_w1"] = self.dram_in("ffn_w1", [DEPTH, D, DFF])
        I["ffn_w2"] = self.dram_in("ffn_w2", [DEPTH, DFF, D])
        self.I = I
        self.outT = self.dram_out("outT", [D, T])
        if self.debug:
            self.dbg = self.dram_out("dbg", [D, T])
        self.kcache = self.dram_tmp("kcache", [2, H, 96, 2 * T], BF16)
        self.vcache = self.dram_tmp("vcache", [2, H, 128, 32, 128], BF16)
        self.halo_u = self.dram_tmp("halo_u", [128, DC, 32], BF16)
        self.halo_h = self.dram_tmp("halo_h", [128, DC, 16], F32)
        self.w1s = self.dram_tmp("w1s", [DEPTH, 16, 128, DC * 256], BF16)
        self.w2s = self.dram_tmp("w2s", [DEPTH, DC, 128, 32 * 128], BF16)
        self.ffn_cached = [False] * DEPTH

        self.xT = self.sb("xT", [128, DC, T], F32)
        self.modT = self.sb("modT", [128, DEPTH * 48], F32)
        self.vec = self.sb("vec", [128, DEPTH * 6 * DC], F32)
        self.ngT = self.sb("ngT", [128, DEPTH * 4 * DC], F32)
        self.ident = self.sb("ident", [128, 128 + 8], F32)
        self.identb = self.sb("identb", [128, 128], BF16)
        self.onesb = self.sb("onesb", [128, 128], BF16)
        self.onesb_w = self.sb("onesb_w", [128, TB], BF16)
        self.flags = self.sb("flags", [128, 4], F32)
        self.kmax2 = self.sb("kmax2", [128, 2 * 2 * H], F32)
        self.epsc = self.sb("epsc", [128, 4], F32)
        self.uhalo = self.sb("uhalo", [128, DC, 32], BF16)
        self.hhalo = self.sb("hhalo", [128, DC, 16], F32)
        self.ps = [Tl(nc.alloc_psum_tensor("ps%d" % i, [128, 512], F32), "ps%d" % i) for i in range(8)]
        self.ps_rr = 0

        s_misc = K.dsem("misc")
        K.dma(s_misc, self.ident[:], I["consts"], writes=[self.ident.b()], hold=True)
        K.dma(s_misc, self.flags[:], I["flags"], writes=[self.flags.b()], hold=True)
        K.dma(s_misc, self.ngT[:], I["norm_gT"], writes=[self.ngT.b()], hold=True)
        for nm_, shp_ in (("qn_gT", [128, 6]), ("kvn_gT", [128, 4]), ("b_pw1T", [128, 16]), ("cv_vecT", [128, 4 * DC]),
                          ("pl_vecT", [128, 2 * DC]), ("w_dwT", [128, DC, CW]), ("poolfac", [128, 2, 4, 16])):
            self.sbv(nm_, shp_)
        K.dma_flush(s_misc)
        K.op("dve", lambda e: e.tensor_copy(out=self.identb[:], in_=self.ident[:, 0:128]),
             reads=[self.ident.b()], writes=[self.identb.b()])
        K.op("dve", lambda e: e.memset(self.onesb[:], 1.0), writes=[self.onesb.b()])
        K.op("dve", lambda e: e.memset(self.onesb_w[:], 1.0), writes=[self.onesb_w.b()])
        K.op("dve", lambda e: e.memset(self.epsc[:, 0:1], EPS), writes=[self.epsc.b()])
        K.op("dve", lambda e: e.memset(self.kmax2[:], 0.0), writes=[self.kmax2.b()])

        self.prologue_mod()

        for pss, layers in ((0, self.layers_p1), (1, self.layers_p2)):
            if not layers:
                continue
            self.load_x(pss)
            for li in layers:
                self.layer(pss, li)
            if pss == 0 and 3 in self.layers_p2 and self.do_mixers:
                self.mla(0, 3, kv_only=True)
        self.store_out()
        K.final_wait()

    def psum(self, pool):
        lo, n = pool
        if not isinstance(self.ps_rr, dict):
            self.ps_rr = {}
        r = self.ps_rr.get(pool, 0)
        self.ps_rr[pool] = r + 1
        return self.ps[lo + (r % n)]

    def prologue_mod(self):
        nc, K, I = self.nc, self.K, self.I
        with self.sbuf_scope() as S:
            cT = S.sb("cT", [128, DC], F32)
            cact = S.sb("cact", [128, DC], F32)
            abT = S.sb("abT", [128, DEPTH * 48], F32)
            stg = [S.sb("adastg%d" % i, [128, DC, 512], F32) for i in range(2)]
            sm = K.dsem("pm")
            ss = [K.dsem("ada%d" % i) for i in range(2)]
            K.dma(sm, cT[:], I["c"], writes=[cT.b()], hold=True)
            K.dma(sm, abT[:], I["ada_bT"], writes=[abT.b()])
            K.op("act", lambda e: e.activation(out=cact[:], in_=cT[:], func=AF.Silu),
                 reads=[cT.b()], writes=[cact.b()])
            pm = self.ps[0]
            n = 0
            for li in range(DEPTH):
                wv = I["ada_w"][li].rearrange("(kc p) n -> p kc n", p=128)
                for pc in range(12):
                    st = stg[n % 2]
                    K.dma(ss[n % 2], st[:], wv[:, :, pc * 512:(pc + 1) * 512], writes=[st.b()])
                    for oc in range(4):
                        col = li * 48 + pc * 4 + oc
                        for kc in range(DC):
                            last = (kc == DC - 1) and (oc == 3)
                            K.op("pe", lambda e, st=st, oc=oc, kc=kc, col=col: e.matmul(
                                pm[:, col:col + 1], lhsT=st[:, kc, oc * 128:(oc + 1) * 128],
                                rhs=cact[:, kc:kc + 1], start=(kc == 0), stop=(kc == DC - 1)),
                                reads=[st.b(), cact.b()], writes=[pm.b()], inc=last)
                    n += 1
            K.op("dve", lambda e: e.tensor_tensor(out=self.modT[:], in0=pm[:, 0:DEPTH * 48], in1=abT[:], op=ALU.add),
                 reads=[pm.b(), abT.b()], writes=[self.modT.b()])
            for li in range(DEPTH):
                m = lambda j: self.modT[:, li * 48 + j * DC: li * 48 + (j + 1) * DC]
                g = lambda j: self.ngT[:, (li * 4 + j) * DC:(li * 4 + j + 1) * DC]
                v = lambda j: self.vec[:, (li * 6 + j) * DC:(li * 6 + j + 1) * DC]
                rd = [self.modT.b(), self.ngT.b()]
                wr = [self.vec.b()]
                K.op("dve", lambda e: e.scalar_tensor_tensor(out=v(0), in0=m(1), scalar=1.0, in1=g(0), op0=ALU.add, op1=ALU.mult), reads=rd, writes=wr)
                K.op("dve", lambda e: e.tensor_copy(out=v(1), in_=m(0)), reads=rd, writes=wr)
                K.op("dve", lambda e: e.tensor_tensor(out=v(2), in0=m(2), in1=g(1), op=ALU.mult), reads=rd, writes=wr)
                K.op("dve", lambda e: e.scalar_tensor_tensor(out=v(3), in0=m(4), scalar=1.0, in1=g(2), op0=ALU.add, op1=ALU.mult), reads=rd, writes=wr)
                K.op("dve", lambda e: e.tensor_copy(out=v(4), in_=m(3)), reads=rd, writes=wr)
                K.op("dve", lambda e: e.tensor_tensor(out=v(5), in0=m(5), in1=g(3), op=ALU.mult), reads=rd, writes=wr)

    def lvec(self, li, j, c):
        o = (li * 6 + j) * DC + c
        return self.vec[:, o:o + 1]

    def xb(self, tb):
        return self.xT.b(tb)

    def load_x(self, pss):
        K = self.K
        src = self.I["xT_prev" if pss == 0 else "xT_own"].rearrange("(c p) t -> p c t", p=128)
        sem = K.dsem("ldx")
        for tb in range(NTB):
            K.dma(sem, self.xT[:, :, tb * TB:(tb + 1) * TB], src[:, :, tb * TB:(tb + 1) * TB], writes=[self.xb(tb)], hold=(tb < NTB - 1))

    def store_out(self):
        K = self.K
        dst = self.outT.rearrange("(c p) t -> p c t", p=128)
        sem = K.dsem("stx")
        for tb in range(NTB):
            K.dma(sem, dst[:, :, tb * TB:(tb + 1) * TB], self.xT[:, :, tb * TB:(tb + 1) * TB], reads=[self.xb(tb)], hold=(tb < NTB - 1))

    def sq_ops(self, sq, c, src_ap, src_bufs, eng):
        K = self.K
        if eng == "act":
            K.op("act", lambda e: e.activation(out=sq[:, c, :], in_=src_ap, func=AF.Square),
                 reads=src_bufs, writes=[sq.b(c)])
        else:
            K.op(eng, lambda e: e.tensor_tensor(out=sq[:, c, :], in0=src_ap, in1=src_ap, op=ALU.mult),
                 reads=src_bufs, writes=[sq.b(c)])

    def norm_ws(self, S, nset=1, pn=True, width=TB):
        ws = {"sets": [], "i": 0, "pn": [], "j": 0}
        for k in range(nset):
            ws["sets"].append((S.sb("ws_sq%d" % k, [128, DC, width], BF16), S.sb("ws_ln%d" % k, [128, width], F32),
                               S.sb("ws_rs%d" % k, [128, width], F32)))
        if pn:
            ws["pn"] = [S.sb("ws_pn%d" % k, [128, width], F32) for k in range(3)]
        return ws

    def ws_set(self, ws):
        st = ws["sets"][ws["i"] % len(ws["sets"])]
        ws["i"] += 1
        return st

    def rstd_from_sq(self, ws_set, sq, nchunk, n, width, kparts=128):
        K = self.K
        _, tmp, rstd = ws_set
        pp = self.psum((6, 2))
        for c in range(nchunk):
            K.op("pe", lambda e, c=c: e.matmul(pp[:, 0:width], lhsT=self.onesb[0:kparts, :], rhs=sq[0:kparts, c, 0:width],
                                                 start=(c == 0), stop=(c == nchunk - 1)),
                 reads=[sq.b(c), self.onesb.b()], writes=[pp.b()], inc=(c == nchunk - 1))
        K.op("act", lambda e: e.activation(out=tmp[:, 0:width], in_=pp[:, 0:width], func=AF.Ln, bias=self.epsc[:, 0:1], scale=1.0 / n),
             reads=[pp.b(), self.epsc.b()], writes=[tmp.b()])
        K.op("act", lambda e: e.activation(out=rstd[:, 0:width], in_=tmp[:, 0:width], func=AF.Exp, scale=-0.5),
             reads=[tmp.b()], writes=[rstd.b()])
        return rstd

    def prenorm(self, ws, li, sub, tb, hT, hcol0, hkey):
        K = self.K
        st = self.ws_set(ws)
        sq = st[0]
        xs = slice(tb * TB, (tb + 1) * TB)
        for c in range(DC):
            self.sq_ops(sq, c, self.xT[:, c, xs], [self.xb(tb)], "dve" if c % 4 == 3 else "act")
        rstd = self.rstd_from_sq(st, sq, DC, D, TB)
        ja, jb = (0, 1) if sub == 0 else (3, 4)
        for c in range(DC):
            tmp = ws["pn"][ws["j"] % 3]
            ws["j"] += 1
            K.op("dve", lambda e, c=c, tmp=tmp: e.tensor_tensor(out=tmp[:], in0=self.xT[:, c, xs], in1=rstd[:], op=ALU.mult),
                 reads=[self.xb(tb), rstd.b()], writes=[tmp.b()])
            if c % 4 != 0:
                K.op("act", lambda e, c=c, tmp=tmp: e.activation(out=hT[:, c, hcol0:hcol0 + TB], in_=tmp[:], func=AF.Identity,
                                                                  bias=self.lvec(li, jb, c), scale=self.lvec(li, ja, c)),
                     reads=[tmp.b(), self.vec.b()], writes=[hT.b((hkey, c))])
            else:
                K.op("dve", lambda e, c=c, tmp=tmp: e.tensor_scalar(out=hT[:, c, hcol0:hcol0 + TB], in0=tmp[:],
                                                                     scalar1=self.lvec(li, ja, c), scalar2=self.lvec(li, jb, c),
                                                                     op0=ALU.mult, op1=ALU.add),
                     reads=[tmp.b(), self.vec.b()], writes=[hT.b((hkey, c))])

    def postnorm_residual(self, ws, li, sub, tb, yb):
        K = self.K
        st = self.ws_set(ws)
        sq = st[0]
        for c in range(DC):
            self.sq_ops(sq, c, yb[:, c, :], [yb.b(c)], "act")
        rstd = self.rstd_from_sq(st, sq, DC, D, TB)
        xs = slice(tb * TB, (tb + 1) * TB)
        jg = 2 if sub == 0 else 5
        for c in range(DC):
            K.op("dve", lambda e, c=c: e.tensor_tensor(out=yb[:, c, :], in0=yb[:, c, :], in1=rstd[:], op=ALU.mult),
                 reads=[yb.b(c), rstd.b()], writes=[yb.b(c)])
            K.op("dve", lambda e, c=c: e.scalar_tensor_tensor(out=self.xT[:, c, xs], in0=yb[:, c, :], scalar=self.lvec(li, jg, c),
                                                              in1=self.xT[:, c, xs], op0=ALU.mult, op1=ALU.add),
                 reads=[yb.b(c), self.vec.b(), self.xb(tb)], writes=[self.xb(tb)])

    def cast(self, out_ap, in_ap, rbufs, wbufs):
        K = self.K
        n = getattr(self, "_ncast", 0)
        self._ncast = n + 1
        if n % 2 == 0:
            K.op("dve", lambda e: e.tensor_copy(out=out_ap, in_=in_ap), reads=rbufs, writes=wbufs)
        else:
            K.op("act", lambda e: e.activation(out=out_ap, in_=in_ap, func=AF.Identity), reads=rbufs, writes=wbufs)

    def evac(self, eng, out_ap, pp, wbufs):
        K = self.K
        if eng == "act":
            K.op("act", lambda e: e.activation(out=out_ap, in_=pp[:], func=AF.Identity), reads=[pp.b()], writes=wbufs)
        else:
            K.op("dve", lambda e: e.tensor_copy(out=out_ap, in_=pp[:]), reads=[pp.b()], writes=wbufs)

    def ffn(self, pss, li):
        K, I = self.K, self.I
        w1v = I["ffn_w1"][li].rearrange("(kc p) n -> p kc n", p=128)
        w2v = I["ffn_w2"][li].rearrange("(hc p) n -> p hc n", p=128)
        ssem = [K.dsem("fw%d" % i) for i in range(2)]
        stsem = [K.dsem("fws%d" % i) for i in range(2)]
        for sbk in range(2):
            first = not self.ffn_cached[li]
            self.ffn_cached[li] = True
            with self.sbuf_scope() as S:
                hid = S.sb("hid", [128, 32, 2 * TB], BF16)
                with self.sbuf_scope() as S1:
                    hT = S1.sb("hT", [128, DC, 2 * TB], BF16)
                    nwb = 2 if first else 4
                    stg = [S1.sb("fstg%d" % i, [128, DC, 256], F32) for i in range(2)] if first else None
                    ws = self.norm_ws(S1, nset=1 if first else 2)
                    w1b = [S1.sb("w1b%d" % i, [128, DC, 256], BF16) for i in range(nwb)]
                    lsem4 = [K.dsem("fwl%d" % i) for i in range(4)]
                    rl = [S1.sb("rl%d" % i, [128, TB], BF16) for i in range(3)]
                    for t2 in range(2):
                        self.prenorm(ws, li, 1, sbk * 2 + t2, hT, t2 * TB, t2)
                    nrl = 0
                    for pc in range(16):
                        wb = w1b[pc % nwb]
                        st = stg[pc % 2] if first else None
                        cb = self._cb("w1s", li, pc, 0)
                        if first:
                            K.dma(ssem[pc % 2], st[:], w1v[:, :, pc * 256:(pc + 1) * 256], writes=[st.b()])
                            self.cast(wb[:], st[:], [st.b()], [wb.b()])
                            K.dma(stsem[pc % 2], self.w1s[li, pc].rearrange("p (k n) -> p k n", k=DC), wb[:], reads=[wb.b()], writes=[cb])
                        else:
                            K.dma(lsem4[pc % 4], wb[:], self.w1s[li, pc].rearrange("p (k n) -> p k n", k=DC), reads=[cb], writes=[wb.b()])
                        for hc2 in range(2):
                            hc = pc * 2 + hc2
                            for t2 in range(2):
                                pp = self.psum((0, 4))
                                for kc in range(DC):
                                    K.op("pe", lambda e, wb=wb, kc=kc, hc2=hc2, t2=t2, pp=pp: e.matmul(
                                        pp[:], lhsT=wb[:, kc, hc2 * 128:(hc2 + 1) * 128], rhs=hT[:, kc, t2 * TB:(t2 + 1) * TB],
                                        start=(kc == 0), stop=(kc == DC - 1)),
                                        reads=[wb.b(), hT.b((t2, kc))], writes=[pp.b()], inc=(kc == DC - 1))
                                r = rl[nrl % 3]
                                nrl += 1
                                K.op("act", lambda e, pp=pp, r=r: e.activation(out=r[:], in_=pp[:], func=AF.Relu),
                                     reads=[pp.b()], writes=[r.b()])
                                K.op("dve", lambda e, r=r, hc=hc, t2=t2: e.tensor_tensor(out=hid[:, hc, t2 * TB:(t2 + 1) * TB], in0=r[:], in1=r[:], op=ALU.mult),
                                     reads=[r.b()], writes=[hid.b((hc, t2))])
                with self.sbuf_scope() as S1:
                    stg = [S1.sb("gstg%d" % i, [128, 8, 128], F32) for i in range(2)] if first else None
                    w2b = [S1.sb("w2b%d" % i, [128, 32, 128], BF16) for i in range(2)]
                    yb = [S1.sb("yb%d" % i, [128, DC, TB], F32) for i in range(2)]
                    ws = self.norm_ws(S1, nset=1, pn=False)
                    npc = 0
                    for oc in range(DC):
                        wb = w2b[oc % 2]
                        cb = self._cb("w2s", li, oc, 0)
                        if first:
                            for hh in range(4):
                                st = stg[npc % 2]
                                K.dma(ssem[npc % 2], st[:], w2v[:, hh * 8:(hh + 1) * 8, oc * 128:(oc + 1) * 128], writes=[st.b()])
                                self.cast(wb[:, hh * 8:(hh + 1) * 8, :], st[:], [st.b()], [wb.b(hh)])
                                npc += 1
                            K.dma(stsem[oc % 2], self.w2s[li, oc].rearrange("p (k n) -> p k n", k=32), wb[:], reads=wb.bs(range(4)), writes=[cb])
                        else:
                            K.dma(ssem[oc % 2], wb[:], self.w2s[li, oc].rearrange("p (k n) -> p k n", k=32), reads=[cb], writes=wb.bs(range(4)))
                        for t2 in range(2):
                            pp = self.psum((0, 4))
                            for hc in range(32):
                                K.op("pe", lambda e, wb=wb, hc=hc, t2=t2, pp=pp: e.matmul(
                                    pp[:], lhsT=wb[:, hc, :], rhs=hid[:, hc, t2 * TB:(t2 + 1) * TB],
                                    start=(hc == 0), stop=(hc == 31)),
                                    reads=[wb.b(hc // 8), hid.b((hc, t2))], writes=[pp.b()], inc=(hc == 31))
                            self.evac("act" if (oc + t2) % 2 == 0 else "dve", yb[t2][:, oc, :], pp, [yb[t2].b(oc)])
                    for t2 in range(2):
                        self.postnorm_residual(ws, li, 1, sbk * 2 + t2, yb[t2])

    def layer(self, pss, li):
        kind = li % 3
        if self.do_mixers:
            if kind == 0:
                self.mla(pss, li)
            elif kind == 1:
                self.conv(pss, li)
            else:
                self.pool(pss, li)
        if self.do_ffn:
            self.ffn(pss, li)

    def load_w_bf16(self, dst, src3, stg, sems, key=0, cache=None):
        K = self.K
        A, N = src3.shape[1], src3.shape[2]
        if cache is not None:
            if not hasattr(self, "_wc"):
                self._wc = {}
            if cache in self._wc:
                scr, cb = self._wc[cache]
                self._wcn = getattr(self, "_wcn", 0) + 1
                K.dma(K.dsem("wcl%d" % (self._wcn % 4)), dst[:, 0:A, :], scr.rearrange("p (a n) -> p a n", n=N), reads=[cb], writes=[dst.b(key)])
                return
        cap = stg[0].t.shape[-1]
        per = max(1, cap // N)
        assert N <= cap
        a0 = 0
        n = getattr(self, "_lw", 0)
        while a0 < A:
            a1 = min(A, a0 + per)
            st = stg[n % len(stg)]
            view = st[:, 0:(a1 - a0) * N].rearrange("p (a n) -> p a n", n=N)
            K.dma(sems[n % len(stg)], view, src3[:, a0:a1, :], writes=[st.b()])
            self.cast(dst[:, a0:a1, :], view, [st.b()], [dst.b(key)])
            n += 1
            a0 = a1
        self._lw = n
        if cache is not None:
            scr = self.dram_tmp("wc_" + cache, [128, A * N], BF16)
            cb = Buf("wc_" + cache)
            self._wc[cache] = (scr, cb)
            K.dma(K.dsem("wcs%d" % (len(self._wc) % 2)), scr.rearrange("p (a n) -> p a n", n=N), dst[:, 0:A, :], reads=[dst.b(key)], writes=[cb])

    def rope_tables(self, S, pss, C32, S32):
        K = self.K
        pos = self.I["pos_prev" if pss == 0 else "pos_own"]
        TWO_PI = 2.0 * np.pi
        C1 = 6.28125
        C2 = TWO_PI - C1
        pi_ = S.sb("posi", [32, T], I32)
        ang = S.sb("ang", [32, T], F32)
        ki = S.sb("ki", [32, T], I32)
        kf = S.sb("kf", [32, T], F32)
        r = S.sb("rr", [32, T], F32)
        sem = K.dsem("misc")
        K.dma(sem, pi_[:], pos, writes=[pi_.b()])
        K.op("dve", lambda e: e.tensor_copy(out=ang[:], in_=pi_[:]), reads=[pi_.b()], writes=[ang.b()])
        K.op("dve", lambda e: e.tensor_scalar(out=ang[:], in0=ang[:], scalar1=self.ident[0:32, 128:129], scalar2=None, op0=ALU.mult),
             reads=[ang.b(), self.ident.b()], writes=[ang.b()])
        for which in range(2):
            off = 0.0 if which == 0 else 0.5 * np.pi
            K.op("dve", lambda e: e.tensor_scalar(out=ki[:], in0=ang[:], scalar1=float(off), scalar2=float(1.0 / TWO_PI), op0=ALU.add, op1=ALU.mult),
                 reads=[ang.b()], writes=[ki.b()])
            K.op("dve", lambda e: e.tensor_copy(out=kf[:], in_=ki[:]), reads=[ki.b()], writes=[kf.b()])
            K.op("dve", lambda e: e.scalar_tensor_tensor(out=r[:], in0=kf[:], scalar=float(-C1), in1=ang[:], op0=ALU.mult, op1=ALU.add),
                 reads=[kf.b(), ang.b()], writes=[r.b()])
            K.op("dve", lambda e: e.scalar_tensor_tensor(out=r[:], in0=kf[:], scalar=float(-C2), in1=r[:], op0=ALU.mult, op1=ALU.add),
                 reads=[kf.b(), r.b()], writes=[r.b()])
            K.op("dve", lambda e: e.tensor_scalar(out=r[:], in0=r[:], scalar1=float(off), scalar2=3.1415925, op0=ALU.add, op1=ALU.min),
                 reads=[r.b()], writes=[r.b()])
            K.op("dve", lambda e: e.tensor_scalar(out=r[:], in0=r[:], scalar1=-3.1415925, scalar2=None, op0=ALU.max),
                 reads=[r.b()], writes=[r.b()])
            if which == 0:
                for lo in (0, 64):
                    K.op("act", lambda e, lo=lo: e.activation(out=S32[lo:lo + 32, :], in_=r[:], func=AF.Sin, scale=self.ident[0:32, 129:130]),
                         reads=[r.b(), self.ident.b()], writes=[S32.b()])
            else:
                for lo in (0, 64):
                    K.op("act", lambda e, lo=lo: e.activation(out=C32[lo:lo + 32, :], in_=r[:], func=AF.Sin),
                         reads=[r.b()], writes=[C32.b()])

    def mla(self, pss, li, kv_only=False):
        K, I = self.K, self.I
        j = li // 3
        NKB_OWN = T // 128
        NK = T * (pss + 1)
        NKB = NK // 128
        prevb = self.flags[:, 0:1]
        wsem = [K.dsem("mw%d" % i) for i in range(2)]
        csems = [K.dsem("cache_st%d" % i) for i in range(4)]
        lsem = [K.dsem("cache_ld%d" % i) for i in range(2)]
        with self.sbuf_scope() as SM:
            cqn = SM.sb("cqn", [128, 3, T], BF16) if not kv_only else None
            C32 = SM.sb("C32", [96, T], BF16)
            S32 = SM.sb("S32", [96, T], BF16)
            ao = SM.sb("ao", [128, DC, T], BF16) if not kv_only else None
            with self.sbuf_scope() as SK:
                ckvn = SK.sb("ckvn", [128, 2, T], BF16)
                krope = SK.sb("krope", [96, T], BF16)
                with self.sbuf_scope() as S1:
                    with self.sbuf_scope() as SR:
                        self.rope_tables(SR, pss, C32, S32)
                    stg = [S1.sb("mstg%d" % i, [128, 1024], F32) for i in range(2)]
                    wdq = S1.sb("wdq", [128, DC, QL], BF16)
                    wdkv = S1.sb("wdkv", [128, DC, 320], BF16)
                    self.load_w_bf16(wdq, I["w_dq"][j].rearrange("(kc p) n -> p kc n", p=128), stg, wsem, cache="wdq%d" % j)
                    dkA = I["w_dkvA"][j].rearrange("(kc p) n -> p kc n", p=128)
                    dkB = I["w_dkvB"][j].rearrange("(kc p) n -> p kc n", p=128)
                    self.load_w_bf16(_Vn(wdkv, 0, 288), dkA, stg, wsem, cache="wdkvA%d" % j)
                    self.load_w_bf16(_Vn(wdkv, 288, 320), dkB, stg, wsem, cache="wdkvB%d" % j)
                    gq = self.sbv("qn_gT", [128, 6])
                    gkv = self.sbv("kvn_gT", [128, 4])
                    hTs = [S1.sb("mhT%d" % i, [128, DC, TB], BF16) for i in range(2)]
                    ws = self.norm_ws(S1, nset=1)
                    sqqs = [S1.sb("sqq%d" % i, [128, 3, TB], BF16) for i in range(1)] * 2
                    sqks = [S1.sb("sqk%d" % i, [128, 2, TB], BF16) for i in range(1)] * 2
                    kr1 = [S1.sb("kr1_%d" % i, [32, TB], F32) for i in range(1)] * 2
                    kr2 = [S1.sb("kr2_%d" % i, [32, TB], F32) for i in range(1)] * 2
                    for tb in range(NTB):
                        ts = slice(tb * TB, (tb + 1) * TB)
                        hT = hTs[tb % 2]
                        self.prenorm(ws, li, 0, tb, hT, 0, 0)
                        if True:
                            sqq = sqqs[tb % 2]
                            pq = []
                            for oc in range(0 if kv_only else 3):
                                pp = self.psum((0, 4))
                                pq.append(pp)
                                for kc in range(DC):
                                    K.op("pe", lambda e, pp=pp, oc=oc, kc=kc: e.matmul(pp[:], lhsT=wdq[:, kc, oc * 128:(oc + 1) * 128], rhs=hT[:, kc, :],
                                                                                         start=(kc == 0), stop=(kc == DC - 1)),
                                         reads=[wdq.b(), hT.b((0, kc))], writes=[pp.b()], inc=(kc == DC - 1))
                                self.sq_ops(sqq, oc, pp[:], [pp.b()], "act")
                            rstd = self.rstd_from_sq(self.ws_set(ws), sqq, 3, QL, TB) if not kv_only else None
                            for oc in range(0 if kv_only else 3):
                                K.op("dve", lambda e, oc=oc: e.scalar_tensor_tensor(out=cqn[:, oc, ts], in0=pq[oc][:], scalar=gq[:, j * 3 + oc:j * 3 + oc + 1],
                                                                                      in1=rstd[:], op0=ALU.mult, op1=ALU.mult),
                                     reads=[pq[oc].b(), rstd.b(), gq.b()], writes=[cqn.b(tb)])
                            sqk = sqks[tb % 2]
                            pk = []
                            for oc in range(2):
                                pp = self.psum((0, 4))
                                pk.append(pp)
                                for kc in range(DC):
                                    K.op("pe", lambda e, pp=pp, oc=oc, kc=kc: e.matmul(pp[:], lhsT=wdkv[:, kc, oc * 128:(oc + 1) * 128], rhs=hT[:, kc, :],
                                                                                         start=(kc == 0), stop=(kc == DC - 1)),
                                         reads=[wdkv.b(), hT.b((0, kc))], writes=[pp.b()], inc=(kc == DC - 1))
                                self.sq_ops(sqk, oc, pp[:], [pp.b()], "act")
                            rstd2 = self.rstd_from_sq(self.ws_set(ws), sqk, 2, KVL, TB)
                            for oc in range(2):
                                K.op("dve", lambda e, oc=oc: e.scalar_tensor_tensor(out=ckvn[:, oc, ts], in0=pk[oc][:], scalar=gkv[:, j * 2 + oc:j * 2 + oc + 1],
                                                                                      in1=rstd2[:], op0=ALU.mult, op1=ALU.mult),
                                     reads=[pk[oc].b(), rstd2.b(), gkv.b()], writes=[ckvn.b(tb)])
                            pr = []
                            for w in range(2):
                                pp = self.psum((4, 2))
                                pr.append(pp)
                                for kc in range(DC):
                                    K.op("pe", lambda e, pp=pp, w=w, kc=kc: e.matmul(pp[0:32, :], lhsT=wdkv[:, kc, 256 + 32 * w:288 + 32 * w], rhs=hT[:, kc, :],
                                                                                       start=(kc == 0), stop=(kc == DC - 1)),
                                         reads=[wdkv.b(), hT.b((0, kc))], writes=[pp.b()], inc=(kc == DC - 1))
                            t1 = kr1[tb % 2]
                            t2 = kr2[tb % 2]
                            K.op("dve", lambda e: e.tensor_tensor(out=t1[:], in0=pr[0][0:32, :], in1=C32[0:32, ts], op=ALU.mult),
                                 reads=[pr[0].b(), C32.b()], writes=[t1.b()])
                            K.op("dve", lambda e: e.tensor_tensor(out=t2[:], in0=pr[1][0:32, :], in1=S32[0:32, ts], op=ALU.mult),
                                 reads=[pr[1].b(), S32.b()], writes=[t2.b()])
                            K.op("dve", lambda e: e.tensor_tensor(out=krope[64:96, ts], in0=t1[:], in1=t2[:], op=ALU.add),
                                 reads=[t1.b(), t2.b()], writes=[krope.b(tb)])
                with self.sbuf_scope() as S1:
                    stg = [S1.sb("mstg%d" % i, [128, 1024], F32) for i in range(2)]
                    wk = S1.sb("wukvK", [128, 2, H * 64], BF16)
                    wv = S1.sb("wukvV", [128, 2, H * 64], BF16)
                    for hh in range(0, H, 8):
                        self.load_w_bf16(_Vn(wk, hh * 64, (hh + 8) * 64), I["w_ukvK"][j].rearrange("(kc p) n -> p kc n", p=128)[:, :, hh * 64:(hh + 8) * 64], stg, wsem, key=("k", hh), cache="wk%d_%d" % (j, hh))
                    for hh in range(0, H, 8):
                        self.load_w_bf16(_Vn(wv, hh * 64, (hh + 8) * 64), I["w_ukvV"][j].rearrange("(kc p) n -> p kc n", p=128)[:, :, hh * 64:(hh + 8) * 64], stg, wsem, key=("v", hh), cache="wv%d_%d" % (j, hh))
                    khs = [S1.sb("kh%d" % i, [96, T], BF16) for i in range(2)]
                    vhs = [S1.sb("vh%d" % i, [128, NKB_OWN, 128], BF16) for i in range(2)]
                    sqkhs = [S1.sb("sqkh%d" % i, [96, T], BF16) for i in range(2)]
                    mx4s = [S1.sb("mx4_%d" % i, [128, 4], F32) for i in range(2)]
                    for v in vhs:
                        K.op("pool", lambda e, v=v: e.memset(v[:, :, 64:128], 1.0), writes=[v.b("ones")])
                    for h in range(H):
                        kh, vh = khs[h % 2], vhs[h % 2]
                        sqkh, mx4 = sqkhs[h % 2], mx4s[h % 2]
                        K.op("act", lambda e, kh=kh: e.activation(out=kh[64:96, :], in_=krope[64:96, :], func=AF.Identity),
                             reads=krope.bs(range(NTB)), writes=[kh.b("r")])
                        for tb in range(NTB):
                            ts = slice(tb * TB, (tb + 1) * TB)
                            pp = self.psum((0, 4))
                            for c in range(2):
                                K.op("pe", lambda e, pp=pp, c=c: e.matmul(pp[0:64, :], lhsT=wk[:, c, h * 64:(h + 1) * 64], rhs=ckvn[:, c, ts],
                                                                            start=(c == 0), stop=(c == 1)),
                                     reads=[wk.b(("k", 0)), wk.b(("k", 8)), ckvn.b(tb)], writes=[pp.b()], inc=(c == 1))
                            if tb % 2 == 0:
                                K.op("dve", lambda e, pp=pp: e.tensor_copy(out=kh[0:64, ts], in_=pp[0:64, :]), reads=[pp.b()], writes=[kh.b(tb)])
                            else:
                                K.op("act", lambda e, pp=pp: e.activation(out=kh[0:64, ts], in_=pp[0:64, :], func=AF.Identity), reads=[pp.b()], writes=[kh.b(tb)])
                        K.dma(csems[(2 * h) % 4], self.kcache[j, h, :, pss * T:(pss + 1) * T], kh[:], reads=kh.bs(["r", 0, 1, 2, 3]), writes=[self.kcb(j, h, pss)])
                        for g in range(2):
                            pp = self.psum((0, 4))
                            for kb in range(8):
                                blk = g * 8 + kb
                                for c in range(2):
                                    K.op("pe", lambda e, pp=pp, kb=kb, blk=blk, c=c: e.matmul(pp[:, kb * 64:(kb + 1) * 64], lhsT=ckvn[:, c, blk * 128:(blk + 1) * 128],
                                                                                                rhs=wv[:, c, h * 64:(h + 1) * 64], start=(c == 0), stop=(c == 1)),
                                         reads=[wv.b(("v", 0)), wv.b(("v", 8)), ckvn.b(blk // 4)], writes=[pp.b()], inc=(kb == 7 and c == 1))
                            src = pp[:, :].rearrange("p (k d) -> p k d", d=64)
                            if g == 0:
                                K.op("dve", lambda e, src=src: e.tensor_copy(out=vh[:, 0:8, 0:64], in_=src), reads=[pp.b()], writes=[vh.b(0)])
                            else:
                                K.op("act", lambda e, src=src: e.activation(out=vh[:, 8:16, 0:64], in_=src, func=AF.Identity), reads=[pp.b()], writes=[vh.b(1)])
                        K.dma(csems[(2 * h + 1) % 4], self.vcache[j, h, :, pss * NKB_OWN:(pss + 1) * NKB_OWN, :], vh[:], reads=vh.bs(["ones", 0, 1]), writes=[self.vcb(j, h, pss)])
                        K.op("dve", lambda e: e.tensor_tensor(out=sqkh[:], in0=kh[:], in1=kh[:], op=ALU.mult), reads=kh.bs(["r", 0, 1, 2, 3]), writes=[sqkh.b()])
                        for tb in range(NTB):
                            ts = slice(tb * TB, (tb + 1) * TB)
                            pp = self.psum((6, 2))
                            K.op("pe", lambda e, pp=pp: e.matmul(pp[:], lhsT=self.onesb[0:96, :], rhs=sqkh[0:96, ts], start=True, stop=True),
                                 reads=[sqkh.b(), self.onesb.b()], writes=[pp.b()])
                            K.op("dve", lambda e, pp=pp, tb=tb: e.tensor_reduce(out=mx4[:, tb:tb + 1], in_=pp[:], axis=AX.X, op=ALU.max),
                                 reads=[pp.b()], writes=[mx4.b()])
                        kcol = (j * 2 + pss) * H + h
                        K.op("dve", lambda e: e.tensor_reduce(out=self.kmax2[:, kcol:kcol + 1], in_=mx4[:], axis=AX.X, op=ALU.max),
                             reads=[mx4.b()], writes=[self.kmax2.b()])
            if kv_only:
                return
            with self.sbuf_scope() as S1:
                wqa = S1.sb("wuqA", [128, 3, H * 96], BF16)
                wqb = S1.sb("wuqB", [128, 3, H * 32], BF16)
                with self.sbuf_scope() as SG:
                    stg = [SG.sb("mstg%d" % i, [128, 1024], F32) for i in range(2)]
                    for hh in range(0, H, 8):
                        self.load_w_bf16(_Vn(wqa, hh * 96, (hh + 8) * 96), I["w_uqA"][j].rearrange("(kc p) n -> p kc n", p=128)[:, :, hh * 96:(hh + 8) * 96], stg, wsem, cache="wqa%d_%d" % (j, hh))
                    self.load_w_bf16(wqb, I["w_uqB"][j].rearrange("(kc p) n -> p kc n", p=128), stg, wsem, cache="wqb%d" % j)
                khf = [S1.sb("khf%d" % i, [96, NK], BF16) for i in range(2)]
                vhf = [S1.sb("vhf%d" % i, [128, NKB, 128], BF16) for i in range(2)]
                qhs = [S1.sb("qh%d" % i, [96, T], BF16) for i in range(2)]
                sqqh = S1.sb("sqqh", [96, T], BF16)
                mxq = S1.sb("mxq", [128, 4], F32)
                sc = [S1.sb("attsc%d" % i, [128, 8], F32) for i in range(2)]
                pts = [S1.sb("pt%d" % i, [128, TB], BF16) for i in range(4)]
                ptd = [S1.sb("ptd%d" % i, [128, TB], BF16) for i in range(4)]
                r1 = [S1.sb("qr1_%d" % i, [96, TB], F32) for i in range(2)]
                r2 = [S1.sb("qr2_%d" % i, [96, TB], F32) for i in range(2)]
                rec = [S1.sb("rec%d" % i, [64, TB], F32) for i in range(2)]
                for dj in range(4):
                    K.op("pool", lambda e, dj=dj: e.memset(ptd[dj][:], 0.0), writes=[ptd[dj].b()])
                cnt = {"pt": 0, "rec": 0}

                def prep(h):
                    kf_, vf_, qh, s_ = khf[h % 2], vhf[h % 2], qhs[h % 2], sc[h % 2]
                    K.dma(lsem[h % 2], kf_[:], self.kcache[j, h, :, 0:NK], reads=[self.kcb(j, h, p_) for p_ in range(pss + 1)], writes=[kf_.b()], hold=True)
                    K.dma(lsem[h % 2], vf_[:], self.vcache[j, h, :, 0:NKB, :], reads=[self.vcb(j, h, p_) for p_ in range(pss + 1)], writes=[vf_.b()])
                    yield
                    for tb in range(NTB):
                        ts = slice(tb * TB, (tb + 1) * TB)
                        pa = self.psum((6, 1))
                        for c in range(3):
                            K.op("pe", lambda e, c=c: e.matmul(pa[0:96, :], lhsT=wqa[:, c, h * 96:(h + 1) * 96], rhs=cqn[:, c, ts], start=(c == 0), stop=(c == 2)),
                                 reads=[wqa.b(), cqn.b(tb)], writes=[pa.b()], inc=(c == 2))
                        pb = self.psum((7, 1))
                        for c in range(3):
                            K.op("pe", lambda e, c=c: e.matmul(pb[0:32, :], lhsT=wqb[:, c, h * 32:(h + 1) * 32], rhs=cqn[:, c, ts], start=(c == 0), stop=(c == 2)),
                                 reads=[wqb.b(), cqn.b(tb)], writes=[pb.b()], inc=(c == 2))
                        a1, a2 = r1[tb % 2], r2[tb % 2]
                        K.op("dve", lambda e: e.tensor_tensor(out=a1[64:96, :], in0=pa[64:96, :], in1=C32[64:96, ts], op=ALU.mult), reads=[pa.b(), C32.b()], writes=[a1.b()])
                        K.op("dve", lambda e: e.tensor_tensor(out=a2[64:96, :], in0=pb[0:32, :], in1=S32[0:32, ts], op=ALU.mult), reads=[pb.b(), S32.b()], writes=[a2.b()])
                        K.op("dve", lambda e: e.tensor_tensor(out=qh[64:96, ts], in0=a1[64:96, :], in1=a2[64:96, :], op=ALU.add), reads=[a1.b(), a2.b()], writes=[qh.b((tb, "r"))])
                        K.op("dve", lambda e: e.tensor_copy(out=qh[0:64, ts], in_=pa[0:64, :]), reads=[pa.b()], writes=[qh.b((tb, "n"))])
                        yield
                    K.op("dve", lambda e: e.tensor_tensor(out=sqqh[:], in0=qh[:], in1=qh[:], op=ALU.mult), reads=qh.bs([(t_, x_) for t_ in range(NTB) for x_ in "rn"]), writes=[sqqh.b()])
                    for tb in range(NTB):
                        ts = slice(tb * TB, (tb + 1) * TB)
                        pp = self.psum((6, 1))
                        K.op("pe", lambda e: e.matmul(pp[:], lhsT=self.onesb[0:96, :], rhs=sqqh[0:96, ts], start=True, stop=True),
                             reads=[sqqh.b(), self.onesb.b()], writes=[pp.b()])
                        K.op("dve", lambda e: e.tensor_reduce(out=mxq[:, tb:tb + 1], in_=pp[:], axis=AX.X, op=ALU.max), reads=[pp.b()], writes=[mxq.b()])
                    yield
                    K.op("dve", lambda e: e.tensor_reduce(out=s_[:, 0:1], in_=mxq[:], axis=AX.X, op=ALU.max), reads=[mxq.b()], writes=[s_.b()])
                    k0 = (j * 2 + 0) * H + h
                    k1 = (j * 2 + pss) * H + h
                    K.op("dve", lambda e: e.tensor_tensor(out=s_[:, 1:2], in0=self.kmax2[:, k0:k0 + 1], in1=self.kmax2[:, k1:k1 + 1], op=ALU.max),
                         reads=[self.kmax2.b()], writes=[s_.b()])
                    K.op("dve", lambda e: e.scalar_tensor_tensor(out=s_[:, 2:3], in0=s_[:, 0:1], scalar=1e-12, in1=s_[:, 1:2], op0=ALU.max, op1=ALU.mult),
                         reads=[s_.b()], writes=[s_.b()])
                    K.op("dve", lambda e: e.tensor_scalar(out=s_[:, 2:3], in0=s_[:, 2:3], scalar1=1e-12, scalar2=None, op0=ALU.max), reads=[s_.b()], writes=[s_.b()])
                    K.op("act", lambda e: e.activation(out=s_[:, 3:4], in_=s_[:, 2:3], func=AF.Ln), reads=[s_.b()], writes=[s_.b()])
                    K.op("act", lambda e: e.activation(out=s_[:, 4:5], in_=s_[:, 3:4], func=AF.Exp, scale=0.5), reads=[s_.b()], writes=[s_.b()])
                    K.op("dve", lambda e: e.tensor_scalar(out=s_[:, 5:6], in0=s_[:, 4:5], scalar1=float(-SCALE), scalar2=None, op0=ALU.mult), reads=[s_.b()], writes=[s_.b()])
                    K.op("dve", lambda e: e.tensor_tensor(out=s_[:, 6:7], in0=s_[:, 5:6], in1=prevb, op=ALU.add), reads=[s_.b(), self.flags.b()], writes=[s_.b()])

                LA = 3

                def attn(h):
                    kf_, vf_, qh, s_ = khf[h % 2], vhf[h % 2], qhs[h % 2], sc[h % 2]
                    items = []
                    for qb in range(NTB):
                        blocks = []
                        if pss == 1:
                            for kb in range(NKB_OWN):
                                blocks.append((kb, 6, None))
                        for kbo in range(4 * (qb + 1)):
                            blocks.append((pss * NKB_OWN + kbo, 5, (kbo - 4 * qb) if kbo >= 4 * qb else None))
                        for i_, (blk, bcol, dj) in enumerate(blocks):
                            items.append((qb, blk, bcol, dj, i_ == 0, i_ == len(blocks) - 1))
                    n = len(items)
                    sts = [None] * n
                    accs = {}
                    gen = prep(h + 1) if h + 1 < H else iter(())
                    every = max(1, n // 8)

                    def qk(i):
                        qb, blk, bcol, dj, first, last = items[i]
                        c0 = 0 if dj is None else 128 * dj
                        st = self.psum((0, 4))
                        sts[i] = st
                        K.op("pe", lambda e: e.matmul(st[:, c0:TB], lhsT=kf_[0:96, blk * 128:(blk + 1) * 128], rhs=qh[0:96, qb * TB + c0:(qb + 1) * TB],
                                                      start=True, stop=True),
                             reads=[kf_.b(), qh.b((qb, "r")), qh.b((qb, "n"))], writes=[st.b()])
                    for i in range(min(LA, n)):
                        qk(i)
                    for i in range(n):
                        if i + LA < n:
                            qk(i + LA)
                        if i % every == every - 1:
                            next(gen, None)
                        qb, blk, bcol, dj, first, last = items[i]
                        st = sts[i]
                        if first:
                            accs[qb] = self.psum((4, 2))
                        acc = accs[qb]
                        if dj is None:
                            pt = pts[cnt["pt"] % len(pts)]
                            cnt["pt"] += 1
                            c0 = 0
                            K.op("act", lambda e: e.activation(out=pt[:], in_=st[:], func=AF.Exp, bias=s_[:, bcol:bcol + 1], scale=float(SCALE)),
                                 reads=[st.b(), s_.b()], writes=[pt.b()])
                        else:
                            pt = ptd[dj]
                            c0 = 128 * dj
                            K.op("act", lambda e: e.activation(out=pt[0:64, c0:TB], in_=st[0:64, c0:TB], func=AF.Exp, bias=s_[0:64, bcol:bcol + 1], scale=float(SCALE)),
                                 reads=[st.b(), s_.b()], writes=[pt.b()])
                            K.op("act", lambda e: e.activation(out=pt[64:128, c0 + 64:TB], in_=st[64:128, c0 + 64:TB], func=AF.Exp, bias=s_[64:128, bcol:bcol + 1], scale=float(SCALE)),
                                 reads=[st.b(), s_.b()], writes=[pt.b()])
                        K.op("pe", lambda e: e.matmul(acc[:, c0:TB], lhsT=vf_[:, blk, :], rhs=pt[:, c0:TB], start=first, stop=last),
                             reads=[vf_.b(), pt.b()], writes=[acc.b()], inc=last)
                        if self.warm and not last:
                            K.op("pe", lambda e: e.matmul(self.ps[7][64:128, :], lhsT=self.onesb[:, 0:64], rhs=self.onesb_w[:, :], start=True, stop=True),
                                 reads=[self.onesb.b(), self.onesb_w.b()], inc=False)
                        if last:
                            rc = rec[cnt["rec"] % 2]
                            cnt["rec"] += 1
                            K.op("dve", lambda e: e.reciprocal(out=rc[0:64, :], in_=acc[64:128, :]), reads=[acc.b()], writes=[rc.b()])
                            po = (h % 2) * 64
                            K.op("dve", lambda e: e.tensor_tensor(out=ao[po:po + 64, h // 2, qb * TB:(qb + 1) * TB], in0=acc[0:64, :], in1=rc[0:64, :], op=ALU.mult),
                                 reads=[acc.b(), rc.b()], writes=[ao.b((h // 2, qb))])
                    for _ in gen:
                        pass

                for _ in prep(0):
                    pass
                for h in range(H):
                    attn(h)
            with self.sbuf_scope() as S1:
                stg = [S1.sb("mstg%d" % i, [128, 1024], F32) for i in range(2)]
                wo = S1.sb("wo", [128, DC, D], BF16)
                self.load_w_bf16(wo, I["w_o"][j].rearrange("(kc p) n -> p kc n", p=128), stg, wsem, cache="wo%d" % j)
                ybs = [S1.sb("myb%d" % i, [128, DC, TB], F32) for i in range(2)]
                ws = self.norm_ws(S1, nset=2, pn=False)
                for tb in range(NTB):
                    yb = ybs[tb % 2]
                    for oc in range(DC):
                        pp = self.psum((0, 4))
                        for kc in range(DC):
                            K.op("pe", lambda e, pp=pp, oc=oc, kc=kc: e.matmul(pp[:], lhsT=wo[:, kc, oc * 128:(oc + 1) * 128], rhs=ao[:, kc, tb * TB:(tb + 1) * TB],
                                                                                 start=(kc == 0), stop=(kc == DC - 1)),
                                 reads=[wo.b(), ao.b((kc, tb))], writes=[pp.b()], inc=(kc == DC - 1))
                        self.evac("act" if oc % 2 == 0 else "dve", yb[:, oc, :], pp, [yb.b(oc)])
                    self.postnorm_residual(ws, li, 0, tb, yb)

    def kcb(self, j, h, p_):
        return self._cb("k", j, h, p_)

    def vcb(self, j, h, p_):
        return self._cb("v", j, h, p_)

    def _cb(self, kind, j, h, p_):
        if not hasattr(self, "_cbufs"):
            self._cbufs = {}
        key = (kind, j, h, p_)
        if key not in self._cbufs:
            self._cbufs[key] = Buf("cache%s" % (key,))
        return self._cbufs[key]

    def sbv(self, name, shape):
        if not hasattr(self, "_sbv"):
            self._sbv = {}
        if name not in self._sbv:
            assert shape is not None
            t = self.sb(name, shape, F32)
            self.K.dma(self.K.dsem("misc"), t[:], self.I[name], writes=[t.b()], hold=True)
            self._sbv[name] = t
        return self._sbv[name]


    def conv(self, pss, li):
        K, I = self.K, self.I
        wsem = [K.dsem("mw%d" % i) for i in range(2)]
        b1 = self.sbv("b_pw1T", None)
        cv = self.sbv("cv_vecT", None)
        wdw = self.sbv("w_dwT", None)
        uh = self.uhalo
        hflag = self.flags[:, 1:2]
        if pss == 0:
            K.op("dve", lambda e: e.memset(uh[:], 0.0), writes=[uh.b()])
        else:
            K.op("dve", lambda e: e.tensor_scalar(out=uh[:], in0=uh[:], scalar1=hflag, scalar2=None, op0=ALU.mult),
                 reads=[uh.b(), self.flags.b()], writes=[uh.b()])
        w1v = I["w_pw1"].rearrange("(kc p) n -> p kc n", p=128)
        w2v = I["w_pw2"].rearrange("(kc p) n -> p kc n", p=128)
        W = 2 * TB
        for sbk in range(2):
            with self.sbuf_scope() as SV:
                vT = SV.sb("vT", [128, DC, W], F32)
                with self.sbuf_scope() as SU:
                    uT = SU.sb("uT", [128, DC, 32 + W], BF16)
                    K.op("pool", lambda e: e.tensor_copy(out=uT[:, :, 0:32], in_=uh[:]), reads=[uh.b()], writes=[uT.b("h")])
                    with self.sbuf_scope() as S1:
                        w1 = S1.sb("wpw1", [128, DC, 2 * D], BF16)
                        with self.sbuf_scope() as SG:
                            stg = [SG.sb("cstg%d" % i, [128, 1024], F32) for i in range(2)]
                            for q4 in range(4):
                                self.load_w_bf16(_Vn(w1, q4 * 512, (q4 + 1) * 512), w1v[:, :, q4 * 512:(q4 + 1) * 512], stg, wsem, key=q4, cache="pw1_%d" % q4)
                        hT = S1.sb("chT", [128, DC, TB], BF16)
                        sg = [S1.sb("sig%d" % i, [128, TB], F32) for i in range(2)]
                        ws = self.norm_ws(S1, nset=1)
                        for t2 in range(2):
                            tb = sbk * 2 + t2
                            self.prenorm(ws, li, 0, tb, hT, 0, 0)
                            for oc in range(DC):
                                pa = self.psum((0, 4))
                                pb = self.psum((0, 4))
                                for (pp, off) in ((pa, 0), (pb, D)):
                                    for kc in range(DC):
                                        K.op("pe", lambda e, pp=pp, off=off, kc=kc: e.matmul(pp[:], lhsT=w1[:, kc, off + oc * 128:off + (oc + 1) * 128], rhs=hT[:, kc, :],
                                                                                               start=(kc == 0), stop=(kc == DC - 1)),
                                             reads=[w1.b((off + oc * 128) // 512), hT.b((0, kc))], writes=[pp.b()], inc=(kc == DC - 1))
                                sgt = sg[oc % 2]
                                K.op("act", lambda e: e.activation(out=sgt[:], in_=pb[:], func=AF.Sigmoid, bias=b1[:, DC + oc:DC + oc + 1]),
                                     reads=[pb.b(), b1.b()], writes=[sgt.b()])
                                K.op("dve", lambda e: e.scalar_tensor_tensor(out=uT[:, oc, 32 + t2 * TB:32 + (t2 + 1) * TB], in0=pa[:], scalar=b1[:, oc:oc + 1],
                                                                              in1=sgt[:], op0=ALU.add, op1=ALU.mult),
                                     reads=[pa.b(), sgt.b(), b1.b()], writes=[uT.b((oc, t2))])
                    K.op("pool", lambda e: e.tensor_copy(out=uh[:], in_=uT[:, :, W:W + 32]), reads=uT.bs([(c, 1) for c in range(DC)]), writes=[uh.b()])
                    with self.sbuf_scope() as S1:
                        dgs = [S1.sb("dg%d" % i, [128, CW, 128], BF16) for i in range(2)]
                        for c in range(DC):
                            dg = dgs[c % 2]
                            for jt in range(CW):
                                if jt % 2 == 0:
                                    K.op("act", lambda e, jt=jt: e.activation(out=dg[:, jt, :], in_=self.identb[:], func=AF.Copy, scale=wdw[:, c, jt:jt + 1]),
                                         reads=[self.identb.b(), wdw.b()], writes=[dg.b(jt % 2)])
                                else:
                                    K.op("dve", lambda e, jt=jt: e.tensor_scalar(out=dg[:, jt, :], in0=self.identb[:], scalar1=wdw[:, c, jt:jt + 1], scalar2=None, op0=ALU.mult),
                                         reads=[self.identb.b(), wdw.b()], writes=[dg.b(jt % 2)])
                            for t2 in range(2):
                                pp = self.psum((0, 4))
                                for jt in range(CW):
                                    c0 = t2 * TB + 2 + jt
                                    K.op("pe", lambda e, jt=jt, c0=c0: e.matmul(pp[:], lhsT=dg[:, jt, :], rhs=uT[:, c, c0:c0 + TB], start=(jt == 0), stop=(jt == CW - 1)),
                                         reads=[dg.b(jt % 2), uT.b("h"), uT.b((c, 0)), uT.b((c, 1))], writes=[pp.b()], inc=(jt == CW - 1))
                                if (c + t2) % 2 == 0:
                                    K.op("act", lambda e: e.activation(out=vT[:, c, t2 * TB:(t2 + 1) * TB], in_=pp[:], func=AF.Identity, bias=cv[:, c:c + 1]),
                                         reads=[pp.b(), cv.b()], writes=[vT.b((c, t2))])
                                else:
                                    K.op("dve", lambda e: e.tensor_scalar(out=vT[:, c, t2 * TB:(t2 + 1) * TB], in0=pp[:], scalar1=cv[:, c:c + 1], scalar2=None, op0=ALU.add),
                                         reads=[pp.b(), cv.b()], writes=[vT.b((c, t2))])
                with self.sbuf_scope() as S1:
                    w2 = S1.sb("wpw2", [128, DC, D], BF16)
                    with self.sbuf_scope() as SG:
                        stg = [SG.sb("cstg%d" % i, [128, 1024], F32) for i in range(2)]
                        self.load_w_bf16(w2, w2v, stg, wsem, cache="pw2")
                    for t2 in range(2):
                        tb = sbk * 2 + t2
                        vs = slice(t2 * TB, (t2 + 1) * TB)
                        with self.sbuf_scope() as S2:
                            yb = S2.sb("cyb", [128, DC, TB], F32)
                            with self.sbuf_scope() as S3:
                                vb = S3.sb("vb", [128, DC, TB], BF16)
                                sqv = S3.sb("sqv", [128, DC, TB], BF16)
                                for c in range(DC):
                                    K.op("act", lambda e, c=c: e.activation(out=vb[:, c, :], in_=vT[:, c, vs], func=AF.Identity), reads=[vT.b((c, t2))], writes=[vb.b(c)])
                                    self.sq_ops(sqv, c, vT[:, c, vs], [vT.b((c, t2))], "dve")
                                p1 = self.psum((6, 2))
                                p2 = self.psum((6, 2))
                                for (pp, src) in ((p1, vb), (p2, sqv)):
                                    for c in range(DC):
                                        K.op("pe", lambda e, pp=pp, src=src, c=c: e.matmul(pp[:], lhsT=self.onesb[:], rhs=src[:, c, :], start=(c == 0), stop=(c == DC - 1)),
                                             reads=[src.b(c), self.onesb.b()], writes=[pp.b()], inc=(c == DC - 1))
                                m = S3.sb("lnm", [128, TB], F32)
                                msq = S3.sb("lnmsq", [128, TB], F32)
                                var = S3.sb("lnvar", [128, TB], F32)
                                lt = S3.sb("lnlt", [128, TB], F32)
                                rstd = S3.sb("lnrstd", [128, TB], F32)
                                nmr = S3.sb("lnnmr", [128, TB], F32)
                                K.op("act", lambda e: e.activation(out=m[:], in_=p1[:], func=AF.Identity, scale=1.0 / D), reads=[p1.b()], writes=[m.b()])
                                K.op("dve", lambda e: e.tensor_tensor(out=msq[:], in0=m[:], in1=m[:], op=ALU.mult), reads=[m.b()], writes=[msq.b()])
                                K.op("dve", lambda e: e.scalar_tensor_tensor(out=var[:], in0=p2[:], scalar=1.0 / D, in1=msq[:], op0=ALU.mult, op1=ALU.subtract),
                                     reads=[p2.b(), msq.b()], writes=[var.b()])
                                K.op("dve", lambda e: e.tensor_scalar(out=var[:], in0=var[:], scalar1=0.0, scalar2=None, op0=ALU.max), reads=[var.b()], writes=[var.b()])
                                K.op("act", lambda e: e.activation(out=lt[:], in_=var[:], func=AF.Ln, bias=self.epsc[:, 0:1]), reads=[var.b(), self.epsc.b()], writes=[lt.b()])
                                K.op("act", lambda e: e.activation(out=rstd[:], in_=lt[:], func=AF.Exp, scale=-0.5), reads=[lt.b()], writes=[rstd.b()])
                                K.op("dve", lambda e: e.scalar_tensor_tensor(out=nmr[:], in0=m[:], scalar=-1.0, in1=rstd[:], op0=ALU.mult, op1=ALU.mult),
                                     reads=[m.b(), rstd.b()], writes=[nmr.b()])
                                sT = S3.sb("sT", [128, DC, TB], BF16)
                                tt = [S3.sb("lntt%d" % i, [128, TB], F32) for i in range(4)]
                                for c in range(DC):
                                    ta, tb_ = tt[(2 * c) % 4], tt[(2 * c + 1) % 4]
                                    K.op("dve", lambda e, c=c, ta=ta: e.tensor_tensor(out=ta[:], in0=vT[:, c, vs], in1=rstd[:], op=ALU.mult),
                                         reads=[vT.b((c, t2)), rstd.b()], writes=[ta.b()])
                                    K.op("dve", lambda e, ta=ta, tb_=tb_: e.tensor_tensor(out=tb_[:], in0=ta[:], in1=nmr[:], op=ALU.add),
                                         reads=[ta.b(), nmr.b()], writes=[tb_.b()])
                                    K.op("act", lambda e, c=c, tb_=tb_: e.activation(out=sT[:, c, :], in_=tb_[:], func=AF.Silu, scale=cv[:, DC + c:DC + c + 1], bias=cv[:, 2 * DC + c:2 * DC + c + 1]),
                                         reads=[tb_.b(), cv.b()], writes=[sT.b(c)])
                                for oc in range(DC):
                                    pp = self.psum((0, 4))
                                    for kc in range(DC):
                                        K.op("pe", lambda e, pp=pp, oc=oc, kc=kc: e.matmul(pp[:], lhsT=w2[:, kc, oc * 128:(oc + 1) * 128], rhs=sT[:, kc, :], start=(kc == 0), stop=(kc == DC - 1)),
                                             reads=[w2.b(), sT.b(kc)], writes=[pp.b()], inc=(kc == DC - 1))
                                    if oc % 2 == 0:
                                        K.op("act", lambda e, pp=pp, oc=oc: e.activation(out=yb[:, oc, :], in_=pp[:], func=AF.Identity, bias=cv[:, 3 * DC + oc:3 * DC + oc + 1]),
                                             reads=[pp.b(), cv.b()], writes=[yb.b(oc)])
                                    else:
                                        K.op("dve", lambda e, pp=pp, oc=oc: e.tensor_scalar(out=yb[:, oc, :], in0=pp[:], scalar1=cv[:, 3 * DC + oc:3 * DC + oc + 1], scalar2=None, op0=ALU.add),
                                             reads=[pp.b(), cv.b()], writes=[yb.b(oc)])
                            with self.sbuf_scope() as S3:
                                self.postnorm_residual(self.norm_ws(S3, nset=1, pn=False), li, 0, tb, yb)

    def pool(self, pss, li):
        K, I = self.K, self.I
        wsem = [K.dsem("mw%d" % i) for i in range(2)]
        pv = self.sbv("pl_vecT", None)
        pf = self.sbv("poolfac", None)
        hh = self.hhalo
        hflag = self.flags[:, 1:2]
        if pss == 0:
            K.op("dve", lambda e: e.memset(hh[:], 0.0), writes=[hh.b()])
        else:
            K.op("dve", lambda e: e.tensor_scalar(out=hh[:], in0=hh[:], scalar1=hflag, scalar2=None, op0=ALU.mult),
                 reads=[hh.b(), self.flags.b()], writes=[hh.b()])
        with self.sbuf_scope() as S0:
            pw = S0.sb("poolw", [128, 8, 256], BF16)
            with self.sbuf_scope() as SG:
                stg = [SG.sb("pstg%d" % i, [128, 1024], F32) for i in range(2)]
                self.load_w_bf16(pw, I["pool_w"].rearrange("g (kc p) n -> p (g kc) n", p=128), stg, wsem, cache="poolw")
            WW = 16 + TB
            pws = self.norm_ws(S0, nset=2)
            for tb in range(NTB):
                with self.sbuf_scope() as S1:
                    hp = S1.sb("hp", [128, DC, WW], F32)
                    K.op("pool", lambda e: e.tensor_copy(out=hp[:, :, 0:16], in_=hh[:]), reads=[hh.b()], writes=[hp.b("h")])
                    self.prenorm(pws, li, 0, tb, _Vc(hp, 16), 0, 0)
                    K.op("pool", lambda e: e.tensor_copy(out=hh[:], in_=hp[:, :, TB:TB + 16]), reads=hp.bs([(0, c) for c in range(DC)]), writes=[hh.b()])
                    pT = S1.sb("ppT", [128, DC, TB], BF16)
                    yb = S1.sb("pyb", [128, DC, TB], F32)
                    sa = [S1.sb("psa%d" % i, [128, WW], F32) for i in range(4)]
                    for c in range(DC):
                        g = c // 2
                        eng = "dve"
                        bufs = sa[0:2] if c % 2 == 0 else sa[2:4]
                        cur_ap = lambda lo, hi, c=c: hp[:, c, lo:hi]
                        cur_b = [hp.b("h"), hp.b((0, c))]
                        lo = 0
                        for lvl in range(g + 1):
                            sh = 1 << lvl
                            dst = bufs[lvl % 2]
                            nlo = lo + sh
                            K.op(eng, lambda e, dst=dst, cur_ap=cur_ap, nlo=nlo, sh=sh: e.tensor_tensor(out=dst[:, nlo:WW], in0=cur_ap(nlo, WW), in1=cur_ap(nlo - sh, WW - sh), op=ALU.add),
                                 reads=cur_b, writes=[dst.b()])
                            cur_ap = (lambda lo_, hi_, dst=dst: dst[:, lo_:hi_])
                            cur_b = [dst.b()]
                            lo = nlo
                        if tb == 0:
                            K.op(eng, lambda e, cur_ap=cur_ap: e.tensor_tensor(out=cur_ap(16, 32), in0=cur_ap(16, 32), in1=pf[:, pss, g, :], op=ALU.mult),
                                 reads=cur_b + [pf.b()], writes=cur_b)
                        K.op("dve", lambda e, cur_ap=cur_ap, c=c, g=g: e.scalar_tensor_tensor(out=pT[:, c, :], in0=cur_ap(16, WW), scalar=1.0 / PWIN[g], in1=hp[:, c, 16:WW],
                                                                                               op0=ALU.mult, op1=ALU.subtract),
                             reads=cur_b + [hp.b((0, c))], writes=[pT.b(c)])
                    for g in range(4):
                        for o2 in range(2):
                            oc = 2 * g + o2
                            pp = self.psum((0, 4))
                            for k2 in range(2):
                                K.op("pe", lambda e, pp=pp, g=g, o2=o2, k2=k2: e.matmul(pp[:], lhsT=pw[:, 2 * g + k2, o2 * 128:(o2 + 1) * 128], rhs=pT[:, 2 * g + k2, :],
                                                                                          start=(k2 == 0), stop=(k2 == 1)),
                                     reads=[pw.b(), pT.b(2 * g + k2)], writes=[pp.b()], inc=(k2 == 1))
                            K.op("dve", lambda e, pp=pp, oc=oc: e.tensor_scalar(out=yb[:, oc, :], in0=pp[:], scalar1=pv[:, oc:oc + 1], scalar2=pv[:, DC + oc:DC + oc + 1],
                                                                                 op0=ALU.add, op1=ALU.mult),
                                 reads=[pp.b(), pv.b()], writes=[yb.b(oc)])
                    self.postnorm_residual(pws, li, 0, tb, yb)


class _Vc:
    def __init__(self, tl, off):
        self.tl, self.off = tl, off

    def __getitem__(self, idx):
        p, c, n = idx
        return self.tl.t[p, c, n.start + self.off:n.stop + self.off]

    def b(self, key=0):
        return self.tl.b(key)


class _Vn:
    def __init__(self, tl, lo, hi):
        self.tl, self.lo, self.hi = tl, lo, hi

    def __getitem__(self, idx):
        p, a, n = idx
        return self.tl.t[p, a, self.lo:self.hi]

    def b(self, key=0):
        return self.tl.b(key)


class _Scope:
    def __init__(self, prog):
        self.p = prog
        self.cms = []

    def __enter__(self):
        return self

    def sb(self, name, shape, dt=F32):
        self.p.uid += 1
        nm = "%s_%d" % (name, self.p.uid)
        cm = self.p.nc.sbuf_tensor(nm, list(shape), dt)
        t = cm.__enter__()
        self.cms.append(cm)
        return Tl(t, nm)

    def __exit__(self, *a):
        self.p.K.barrier()
        for cm in reversed(self.cms):
            cm.__exit__(None, None, None)
        return False


def _t128(v, n):
    return np.ascontiguousarray(np.asarray(v, np.float32).reshape(n, 128).T)


def make_in_maps(inp):
    x = np.asarray(inp["x"], np.float32)
    B = x.shape[0]
    pos = np.asarray(inp["positions"]).astype(np.int32)
    shared = {}
    shared["ada_w"] = np.ascontiguousarray(inp["ada_w"], np.float32)
    shared["ada_bT"] = np.concatenate([_t128(inp["ada_b"][i], 48) for i in range(DEPTH)], axis=1)
    shared["norm_gT"] = np.concatenate([_t128(inp["norm_g"][i, j], DC) for i in range(DEPTH) for j in range(4)], axis=1)
    shared["w_dq"] = np.ascontiguousarray(inp["mla_w_dq"], np.float32)
    shared["qn_gT"] = np.concatenate([_t128(inp["mla_q_norm_g"][j], 3) for j in range(2)], axis=1)
    wuq = np.asarray(inp["mla_w_uq"], np.float32).reshape(2, QL, H, 96)
    shared["w_uqA"] = np.ascontiguousarray(wuq.reshape(2, QL, H * 96))
    shared["w_uqB"] = np.ascontiguousarray(np.concatenate([wuq[..., 80:96], wuq[..., 64:80]], axis=-1).reshape(2, QL, H * 32))
    wdkv = np.asarray(inp["mla_w_dkv"], np.float32)
    shared["w_dkvA"] = np.ascontiguousarray(wdkv)
    shared["w_dkvB"] = np.ascontiguousarray(np.concatenate([wdkv[..., 272:288], wdkv[..., 256:272]], axis=-1))
    shared["kvn_gT"] = np.concatenate([_t128(inp["mla_kv_norm_g"][j], 2) for j in range(2)], axis=1)
    wukv = np.asarray(inp["mla_w_ukv"], np.float32).reshape(2, KVL, H, 128)
    shared["w_ukvK"] = np.ascontiguousarray(wukv[..., 0:64].reshape(2, KVL, H * 64))
    shared["w_ukvV"] = np.ascontiguousarray(wukv[..., 64:128].reshape(2, KVL, H * 64))
    shared["w_o"] = np.ascontiguousarray(inp["mla_w_o"], np.float32)
    shared["w_pw1"] = np.ascontiguousarray(inp["conv_w_pw1"][0], np.float32)
    shared["b_pw1T"] = _t128(inp["conv_b_pw1"][0], 16)
    shared["w_dwT"] = np.ascontiguousarray(np.asarray(inp["conv_w_dw"][0], np.float32).T.reshape(DC, 128, CW).transpose(1, 0, 2))
    shared["cv_vecT"] = np.concatenate([_t128(inp[k][0], DC) for k in ("conv_b_dw", "conv_ln_g", "conv_ln_b", "conv_b_pw2")], axis=1)
    shared["w_pw2"] = np.ascontiguousarray(inp["conv_w_pw2"][0], np.float32)
    shared["pool_w"] = np.ascontiguousarray(inp["pool_w"][0], np.float32)
    shared["pl_vecT"] = np.concatenate([_t128(np.asarray(inp["pool_b"][0]).reshape(-1), DC), _t128(inp["pool_scale"][0], DC)], axis=1)
    shared["ffn_w1"] = np.ascontiguousarray(inp["ffn_w1"], np.float32)
    shared["ffn_w2"] = np.ascontiguousarray(inp["ffn_w2"], np.float32)
    consts = np.zeros((128, 136), np.float32)
    consts[:, :128] = np.eye(128, dtype=np.float32)
    invf = (10000.0 ** (-np.arange(0, 32, 2, dtype=np.float32) / np.float32(32.0))).astype(np.float32)
    consts[0:16, 128] = invf
    consts[16:32, 128] = invf
    consts[0:16, 129] = -1.0
    consts[16:32, 129] = 1.0
    shared["consts"] = consts
    fac_start = np.ones((4, 16), np.float32)
    for g, w in enumerate(PWIN):
        for t in range(16):
            fac_start[g, t] = float(w) / float(min(t + 1, w))
    maps = []
    for core in range(2 * B):
        b, half = core // 2, core % 2
        m = dict(shared)
        own = x[b, half * T:(half + 1) * T]
        m["xT_own"] = np.ascontiguousarray(own.T)
        m["pos_own"] = np.ascontiguousarray(np.broadcast_to(pos[b, half * T:(half + 1) * T].reshape(1, T), (32, T)))
        if half == 1:
            m["xT_prev"] = np.ascontiguousarray(x[b, 0:T].T)
            m["pos_prev"] = np.ascontiguousarray(np.broadcast_to(pos[b, 0:T].reshape(1, T), (32, T)))
        else:
            m["xT_prev"] = np.zeros((D, T), np.float32)
            m["pos_prev"] = np.zeros((32, T), np.int32)
        m["c"] = _t128(inp["c"][b], DC)
        fl = np.zeros((128, 4), np.float32)
        fl[:, 0] = 0.0 if half == 1 else NEG
        fl[:, 1] = 1.0 if half == 1 else 0.0
        m["flags"] = fl
        pf = np.ones((2, 4, 16), np.float32)
        pf[0] = fac_start
        if half == 0:
            pf[1] = fac_start
        m["poolfac"] = np.ascontiguousarray(np.broadcast_to(pf[None], (128, 2, 4, 16)))
        maps.append(m)
    return maps


_PROG = {}


def get_prog(**kw):
    key = tuple(sorted((k, str(v)) for k, v in kw.items()))
    if key not in _PROG:
        _PROG[key] = Prog(**kw)
    return _PROG[key]


def kernel(**inputs):
    x = np.asarray(inputs["x"])
    B, S, _ = x.shape
    prog = get_prog()
    maps = make_in_maps(inputs)
    res = run_bass_kernel_spmd(prog.nc, maps, core_ids=list(range(8)))
    out = np.empty((B, S, D), np.float32)
    for core in range(8):
        b, half = core // 2, core % 2
        out[b, half * T:(half + 1) * T] = np.asarray(res.results[core]["outT"]).T
    return out
```

```python
import numpy as np
import ml_dtypes
import concourse.bass as bass
import concourse.mybir as mybir
from concourse.bass_utils import run_bass_kernel_spmd

F32 = mybir.dt.float32
BF16 = mybir.dt.bfloat16
I32 = mybir.dt.int32
AF = mybir.ActivationFunctionType
ALU = mybir.AluOpType
AX = mybir.AxisListType

D = 1024
DC = 8
T = 2048
TB = 512
NTB = 4
DEPTH = 4
H = 16
DFF = 4096
EPS = 1e-6
QL = 384
KVL = 256
NEG = -30000.0
SCALE = 1.0 / float(np.sqrt(96.0))
CW = 31
PWIN = (2, 4, 8, 16)


class Buf:
    __slots__ = ("name", "w", "r")

    def __init__(self, name):
        self.name = name
        self.w = None
        self.r = {}


class KB:
    def __init__(self, nc):
        self.nc = nc
        self.eng = {"pe": nc.tensor, "act": nc.scalar, "dve": nc.vector,
                    "pool": nc.gpsimd, "sp": nc.sync}
        self.sems = {}
        self.cnt = {}
        for e in self.eng:
            self.sems[e] = nc.alloc_semaphore("c_" + e)
            self.cnt[e] = 0
        self.waited = {e: {} for e in self.eng}
        self.pending = {e: [] for e in self.eng}
        self.ndma = 0
        self.nsem = 0
        self.groups = {}

    def dsem(self, name):
        key = "d_" + name
        if key not in self.sems:
            self.sems[key] = self.nc.alloc_semaphore(key)
            self.cnt[key] = 0
        return key

    def _need(self, eng, evs):
        for k, v in evs.items():
            if self.waited[eng].get(k, 0) >= v:
                continue
            self.eng[eng].wait_ge(self.sems[k], v)
            self.waited[eng][k] = v

    def _deps(self, eng, reads, writes):
        evs = {}

        def add(ev, raw):
            if ev is None:
                return
            k, v = ev
            if k == "PENDING":
                assert v == eng and (eng == "pe" or not raw), "dependency on un-flushed op"
                return
            if k == eng:
                if eng in ("pe", "sp"):
                    return
                if not raw:
                    return
            if evs.get(k, 0) < v:
                evs[k] = v
        for b in reads:
            add(b.w, True)
        for b in writes:
            add(b.w, False)
            for k, v in b.r.items():
                add((k, v), False)
        return evs

    def _commit(self, ev, reads, writes):
        for b in reads:
            if b.r.get(ev[0], 0) < ev[1]:
                b.r[ev[0]] = ev[1]
        for b in writes:
            b.w = ev
            b.r = {}

    def op(self, eng, fn, reads=(), writes=(), inc=True):
        reads = list(reads)
        writes = list(writes)
        self._need(eng, self._deps(eng, reads, writes))
        ins = fn(self.eng[eng])
        if inc:
            self.cnt[eng] += 1
            ins.then_inc(self.sems[eng], 1)
            ev = (eng, self.cnt[eng])
            for (r, w) in self.pending[eng]:
                self._commit(ev, r, w)
            self.pending[eng] = []
            self._commit(ev, reads, writes)
        else:
            for b in writes:
                b.w = ("PENDING", eng)
            self.pending[eng].append((reads, writes))
        return ins

    def dma(self, sem, out, in_, reads=(), writes=(), q="sp", hold=False, **kw):
        reads = list(reads)
        writes = list(writes)
        grp = self.groups.setdefault(sem, [])
        if not grp and self.cnt[sem] > 0:
            self._need(q, {sem: self.cnt[sem]})
        self._need(q, self._deps(q, reads, writes))
        self.eng[q].dma_start(out=out, in_=in_, **kw).then_inc(self.sems[sem], 16)
        self.cnt[sem] += 16
        self.ndma += 1
        for b in writes:
            b.w = ("PENDING", "dma")
        grp.append((reads, writes))
        if not hold:
            self.dma_flush(sem)

    def dma_flush(self, sem):
        ev = (sem, self.cnt[sem])
        for (r, w) in self.groups.get(sem, []):
            self._commit(ev, r, w)
        self.groups[sem] = []

    def barrier(self):
        for e in self.eng:
            assert not self.pending[e]
        assert not any(self.groups.values()), "open dma group at barrier"
        evs = {k: v for k, v in self.cnt.items() if v > 0}
        for e in self.eng:
            self._need(e, {k: v for k, v in evs.items() if not (k == e and e in ("pe", "sp"))})

    def final_wait(self):
        evs = {k: v for k, v in self.cnt.items() if v > 0 and k != "sp"}
        self._need("sp", evs)


class Tl:
    def __init__(self, t, name):
        self.t = t
        self.name = name
        self.bufs = {}

    def b(self, key=0):
        if key not in self.bufs:
            self.bufs[key] = Buf("%s.%s" % (self.name, key))
        return self.bufs[key]

    def bs(self, keys):
        return [self.b(k) for k in keys]

    def __getitem__(self, idx):
        return self.t[idx]


def _bf16(a):
    return np.asarray(a, dtype=np.float32).astype(ml_dtypes.bfloat16)


class Prog:
    def __init__(self, layers_p1=(0, 1, 2), layers_p2=(0, 1, 2, 3), debug=None, mixers=True, ffn=True, warm=False):
        self.warm = warm
        self.do_mixers = mixers
        self.do_ffn = ffn
        self.layers_p1 = tuple(layers_p1)
        self.layers_p2 = tuple(layers_p2)
        self.debug = debug
        self.nc = bass.Bass("TRN2", target_bir_lowering=False)
        self.K = KB(self.nc)
        self.uid = 0
        self.build()

    def dram_in(self, name, shape, dt=F32):
        return self.nc.dram_tensor(name, list(shape), dt, kind="ExternalInput").ap()

    def dram_out(self, name, shape, dt=F32):
        return self.nc.dram_tensor(name, list(shape), dt, kind="ExternalOutput").ap()

    def dram_tmp(self, name, shape, dt):
        return self.nc.dram_tensor(name, list(shape), dt).ap()

    def sb(self, name, shape, dt=F32):
        self.uid += 1
        nm = "%s_%d" % (name, self.uid)
        return Tl(self.nc.alloc_sbuf_tensor(nm, list(shape), dt), nm)

    def sbuf_scope(self):
        return _Scope(self)

    def build(self):
        nc, K = self.nc, self.K
        I = {}
        I["xT_prev"] = self.dram_in("xT_prev", [D, T])
        I["xT_own"] = self.dram_in("xT_own", [D, T])
        I["pos_prev"] = self.dram_in("pos_prev", [32, T], I32)
        I["pos_own"] = self.dram_in("pos_own", [32, T], I32)
        I["c"] = self.dram_in("c", [128, DC])
        I["flags"] = self.dram_in("flags", [128, 4])
        I["poolfac"] = self.dram_in("poolfac", [128, 2, 4, 16])
        I["consts"] = self.dram_in("consts", [128, 128 + 8])
        I["ada_wT"] = self.dram_in("ada_wT", [DEPTH, 6 * D, D])
        I["c_rep"] = self.dram_in("c_rep", [128, D])
        I["ada_bT"] = self.dram_in("ada_bT", [128, DEPTH * 48])
        I["norm_gT"] = self.dram_in("norm_gT", [128, DEPTH * 4 * DC])
        I["w_dq"] = self.dram_in("w_dq", [2, D, QL])
        I["qn_gT"] = self.dram_in("qn_gT", [128, 2 * 3])
        I["w_uqA"] = self.dram_in("w_uqA", [2, QL, H * 96])
        I["w_uqB"] = self.dram_in("w_uqB", [2, QL, H * 32])
        I["w_dkvA"] = self.dram_in("w_dkvA", [2, D, 288])
        I["w_dkvB"] = self.dram_in("w_dkvB", [2, D, 32])
        I["kvn_gT"] = self.dram_in("kvn_gT", [128, 2 * 2])
        I["w_ukvK"] = self.dram_in("w_ukvK", [2, KVL, H * 64])
        I["w_ukvV"] = self.dram_in("w_ukvV", [2, KVL, H * 64])
        I["w_o"] = self.dram_in("w_o", [2, D, D])
        I["w_pw1"] = self.dram_in("w_pw1", [D, 2 * D])
        I["b_pw1T"] = self.dram_in("b_pw1T", [128, 16])
        I["w_dwT"] = self.dram_in("w_dwT", [128, DC, CW])
        I["cv_vecT"] = self.dram_in("cv_vecT", [128, 4 * DC])
        I["w_pw2"] = self.dram_in("w_pw2", [D, D])
        I["pool_w"] = self.dram_in("pool_w", [4, 256, 256])
        I["pl_vecT"] = self.dram_in("pl_vecT", [128, 2 * DC])
        I["ffn_w1"] = self.dram_in("ffn_w1", [DEPTH, D, DFF])
        I["ffn_w2"] = self.dram_in("ffn_w2", [DEPTH, DFF, D])
        self.I = I
        self.outT = self.dram_out("outT", [D, T])
        if self.debug:
            self.dbg = self.dram_out("dbg", [D, T])
        self.kcache = self.dram_tmp("kcache", [2, H, 96, 2 * T], BF16)
        self.vcache = self.dram_tmp("vcache", [2, H, 128, 32, 128], BF16)
        self.halo_u = self.dram_tmp("halo_u", [128, DC, 32], BF16)
        self.halo_h = self.dram_tmp("halo_h", [128, DC, 16], F32)
        self.w1s = self.dram_tmp("w1s", [DEPTH, 16, 128, DC * 256], BF16)
        self.w2s = self.dram_tmp("w2s", [DEPTH, DC, 128, 32 * 128], BF16)
        self.ffn_cached = [False] * DEPTH

        self.xT = self.sb("xT", [128, DC, T], F32)
        self.modT = self.sb("modT", [128, DEPTH * 48], F32)
        self.vec = self.sb("vec", [128, DEPTH * 6 * DC], F32)
        self.ngT = self.sb("ngT", [128, DEPTH * 4 * DC], F32)
        self.abT = self.sb("abT", [128, DEPTH * 48], F32)
        self.ident = self.sb("ident", [128, 128 + 8], F32)
        self.identb = self.sb("identb", [128, 128], BF16)
        self.onesb = self.sb("onesb", [128, 128], BF16)
        self.onesb_w = self.sb("onesb_w", [128, TB], BF16)
        self.flags = self.sb("flags", [128, 4], F32)
        self.kmax2 = self.sb("kmax2", [128, 2 * 2 * H], F32)
        self.epsc = self.sb("epsc", [128, 4], F32)
        self.uhalo = self.sb("uhalo", [128, DC, 32], BF16)
        self.hhalo = self.sb("hhalo", [128, DC, 16], F32)
        self.ps = [Tl(nc.alloc_psum_tensor("ps%d" % i, [128, 512], F32), "ps%d" % i) for i in range(8)]
        self.ps_rr = 0

        s_misc = K.dsem("misc")
        K.dma(s_misc, self.ident[:], I["consts"], writes=[self.ident.b()], hold=True)
        K.dma(s_misc, self.flags[:], I["flags"], writes=[self.flags.b()], hold=True)
        K.dma(s_misc, self.ngT[:], I["norm_gT"], writes=[self.ngT.b()], hold=True)
        K.dma(s_misc, self.abT[:], I["ada_bT"], writes=[self.abT.b()], hold=True)
        for nm_, shp_ in (("qn_gT", [128, 6]), ("kvn_gT", [128, 4]), ("b_pw1T", [128, 16]), ("cv_vecT", [128, 4 * DC]),
                          ("pl_vecT", [128, 2 * DC]), ("w_dwT", [128, DC, CW]), ("poolfac", [128, 2, 4, 16])):
            self.sbv(nm_, shp_)
        K.dma_flush(s_misc)
        K.op("dve", lambda e: e.tensor_copy(out=self.identb[:], in_=self.ident[:, 0:128]),
             reads=[self.ident.b()], writes=[self.identb.b()])
        K.op("dve", lambda e: e.memset(self.onesb[:], 1.0), writes=[self.onesb.b()])
        K.op("dve", lambda e: e.memset(self.onesb_w[:], 1.0), writes=[self.onesb_w.b()])
        K.op("dve", lambda e: e.memset(self.epsc[:, 0:1], EPS), writes=[self.epsc.b()])
        K.op("dve", lambda e: e.memset(self.kmax2[:], 0.0), writes=[self.kmax2.b()])

        self.prologue_mod()

        for pss, layers in ((0, self.layers_p1), (1, self.layers_p2)):
            if not layers:
                continue
            self.load_x(pss)
            for li in layers:
                self.layer(pss, li)
            if pss == 0 and 3 in self.layers_p2 and self.do_mixers:
                self.mla(0, 3, kv_only=True)
        self.store_out()
        K.final_wait()

    def psum(self, pool):
        lo, n = pool
        if not isinstance(self.ps_rr, dict):
            self.ps_rr = {}
        r = self.ps_rr.get(pool, 0)
        self.ps_rr[pool] = r + 1
        return self.ps[lo + (r % n)]

    def ada_gen(self, S, layers, nbuf=3):
        K, I = self.K, self.I
        crep = S.sb("crep", [128, D], F32)
        junk = S.sb("adajunk", [128, D], F32)
        stg = [S.sb("adastg%d" % i, [128, D], F32) for i in range(nbuf)]
        ss = [K.dsem("ada%d" % i) for i in range(nbuf)]
        K.dma(K.dsem("pm"), crep[:], I["c_rep"], writes=[crep.b()])
        K.op("act", lambda e: e.activation(out=crep[:], in_=crep[:], func=AF.Silu), reads=[crep.b()], writes=[crep.b()])
        n = 0
        for li in layers:
            wv = I["ada_wT"][li].rearrange("(j p) k -> j p k", p=128)
            for jc in range(48):
                st = stg[n % nbuf]
                col = li * 48 + jc
                K.dma(ss[n % nbuf], st[:], wv[jc], writes=[st.b()])
                K.op("dve", lambda e, st=st, col=col: e.scalar_tensor_tensor(out=junk[:], in0=st[:], scalar=1.0, in1=crep[:], op0=ALU.mult, op1=ALU.mult,
                                                                             accum_out=self.modT[:, col:col + 1]),
                     reads=[st.b(), crep.b()], writes=[junk.b(), self.modT.b()])
                n += 1
                yield
            c0 = li * 48
            K.op("dve", lambda e: e.tensor_tensor(out=self.modT[:, c0:c0 + 48], in0=self.modT[:, c0:c0 + 48], in1=self.abT[:, c0:c0 + 48], op=ALU.add),
                 reads=[self.modT.b(), self.abT.b()], writes=[self.modT.b()])
            m = lambda j: self.modT[:, li * 48 + j * DC: li * 48 + (j + 1) * DC]
            g = lambda j: self.ngT[:, (li * 4 + j) * DC:(li * 4 + j + 1) * DC]
            v = lambda j: self.vec[:, (li * 6 + j) * DC:(li * 6 + j + 1) * DC]
            rd = [self.modT.b(), self.ngT.b()]
            wr = [self.vec.b()]
            K.op("dve", lambda e: e.scalar_tensor_tensor(out=v(0), in0=m(1), scalar=1.0, in1=g(0), op0=ALU.add, op1=ALU.mult), reads=rd, writes=wr)
            K.op("dve", lambda e: e.tensor_copy(out=v(1), in_=m(0)), reads=rd, writes=wr)
            K.op("dve", lambda e: e.tensor_tensor(out=v(2), in0=m(2), in1=g(1), op=ALU.mult), reads=rd, writes=wr)
            K.op("dve", lambda e: e.scalar_tensor_tensor(out=v(3), in0=m(4), scalar=1.0, in1=g(2), op0=ALU.add, op1=ALU.mult), reads=rd, writes=wr)
            K.op("dve", lambda e: e.tensor_copy(out=v(4), in_=m(3)), reads=rd, writes=wr)
            K.op("dve", lambda e: e.tensor_tensor(out=v(5), in0=m(5), in1=g(3), op=ALU.mult), reads=rd, writes=wr)
            yield

    def prologue_mod(self):
        first = (self.layers_p1 + self.layers_p2)[0]
        rest = [l for l in sorted(set(self.layers_p1 + self.layers_p2)) if l != first]
        with self.sbuf_scope() as S:
            for _ in self.ada_gen(S, [first], nbuf=4):
                pass
        self.ada_rest = rest
        self.ada_lazy_ok = self.do_mixers and first % 3 == 0 and bool(rest)
        if rest and not self.ada_lazy_ok:
            with self.sbuf_scope() as S:
                for _ in self.ada_gen(S, rest, nbuf=4):
                    pass
            self.ada_rest = []

    def lvec(self, li, j, c):
        o = (li * 6 + j) * DC + c
        return self.vec[:, o:o + 1]

    def xb(self, tb):
        return self.xT.b(tb)

    def load_x(self, pss):
        K = self.K
        src = self.I["xT_prev" if pss == 0 else "xT_own"].rearrange("(c p) t -> p c t", p=128)
        sem = K.dsem("ldx")
        for tb in range(NTB):
            K.dma(sem, self.xT[:, :, tb * TB:(tb + 1) * TB], src[:, :, tb * TB:(tb + 1) * TB], writes=[self.xb(tb)], hold=(tb < NTB - 1))

    def store_out(self):
        K = self.K
        dst = self.outT.rearrange("(c p) t -> p c t", p=128)
        sem = K.dsem("stx")
        for tb in range(NTB):
            K.dma(sem, dst[:, :, tb * TB:(tb + 1) * TB], self.xT[:, :, tb * TB:(tb + 1) * TB], reads=[self.xb(tb)], hold=(tb < NTB - 1))

    def sq_ops(self, sq, c, src_ap, src_bufs, eng):
        K = self.K
        if eng == "act":
            K.op("act", lambda e: e.activation(out=sq[:, c, :], in_=src_ap, func=AF.Square),
                 reads=src_bufs, writes=[sq.b(c)])
        else:
            K.op(eng, lambda e: e.tensor_tensor(out=sq[:, c, :], in0=src_ap, in1=src_ap, op=ALU.mult),
                 reads=src_bufs, writes=[sq.b(c)])

    def norm_ws(self, S, nset=1, pn=True, width=TB):
        ws = {"sets": [], "i": 0, "pn": [], "j": 0}
        for k in range(nset):
            ws["sets"].append((S.sb("ws_sq%d" % k, [128, DC, width], BF16), S.sb("ws_ln%d" % k, [128, width], F32),
                               S.sb("ws_rs%d" % k, [128, width], F32)))
        if pn:
            ws["pn"] = [S.sb("ws_pn%d" % k, [128, width], F32) for k in range(3)]
        return ws

    def ws_set(self, ws):
        st = ws["sets"][ws["i"] % len(ws["sets"])]
        ws["i"] += 1
        return st

    def rstd_from_sq(self, ws_set, sq, nchunk, n, width, kparts=128):
        K = self.K
        _, tmp, rstd = ws_set
        pp = self.psum((6, 2))
        for c in range(nchunk):
            K.op("pe", lambda e, c=c: e.matmul(pp[:, 0:width], lhsT=self.onesb[0:kparts, :], rhs=sq[0:kparts, c, 0:width],
                                                 start=(c == 0), stop=(c == nchunk - 1)),
                 reads=[sq.b(c), self.onesb.b()], writes=[pp.b()], inc=(c == nchunk - 1))
        K.op("act", lambda e: e.activation(out=tmp[:, 0:width], in_=pp[:, 0:width], func=AF.Ln, bias=self.epsc[:, 0:1], scale=1.0 / n),
             reads=[pp.b(), self.epsc.b()], writes=[tmp.b()])
        K.op("act", lambda e: e.activation(out=rstd[:, 0:width], in_=tmp[:, 0:width], func=AF.Exp, scale=-0.5),
             reads=[tmp.b()], writes=[rstd.b()])
        return rstd

    def prenorm(self, ws, li, sub, tb, hT, hcol0, hkey):
        K = self.K
        st = self.ws_set(ws)
        sq = st[0]
        xs = slice(tb * TB, (tb + 1) * TB)
        for c in range(DC):
            self.sq_ops(sq, c, self.xT[:, c, xs], [self.xb(tb)], "dve" if c % 4 == 3 else "act")
        rstd = self.rstd_from_sq(st, sq, DC, D, TB)
        ja, jb = (0, 1) if sub == 0 else (3, 4)
        for c in range(DC):
            tmp = ws["pn"][ws["j"] % 3]
            ws["j"] += 1
            K.op("dve", lambda e, c=c, tmp=tmp: e.tensor_tensor(out=tmp[:], in0=self.xT[:, c, xs], in1=rstd[:], op=ALU.mult),
                 reads=[self.xb(tb), rstd.b()], writes=[tmp.b()])
            if c % 4 != 0:
                K.op("act", lambda e, c=c, tmp=tmp: e.activation(out=hT[:, c, hcol0:hcol0 + TB], in_=tmp[:], func=AF.Identity,
                                                                  bias=self.lvec(li, jb, c), scale=self.lvec(li, ja, c)),
                     reads=[tmp.b(), self.vec.b()], writes=[hT.b((hkey, c))])
            else:
                K.op("dve", lambda e, c=c, tmp=tmp: e.tensor_scalar(out=hT[:, c, hcol0:hcol0 + TB], in0=tmp[:],
                                                                     scalar1=self.lvec(li, ja, c), scalar2=self.lvec(li, jb, c),
                                                                     op0=ALU.mult, op1=ALU.add),
                     reads=[tmp.b(), self.vec.b()], writes=[hT.b((hkey, c))])

    def postnorm_residual(self, ws, li, sub, tb, yb):
        K = self.K
        st = self.ws_set(ws)
        sq = st[0]
        for c in range(DC):
            self.sq_ops(sq, c, yb[:, c, :], [yb.b(c)], "act")
        rstd = self.rstd_from_sq(st, sq, DC, D, TB)
        xs = slice(tb * TB, (tb + 1) * TB)
        jg = 2 if sub == 0 else 5
        for c in range(DC):
            K.op("dve", lambda e, c=c: e.tensor_tensor(out=yb[:, c, :], in0=yb[:, c, :], in1=rstd[:], op=ALU.mult),
                 reads=[yb.b(c), rstd.b()], writes=[yb.b(c)])
            K.op("dve", lambda e, c=c: e.scalar_tensor_tensor(out=self.xT[:, c, xs], in0=yb[:, c, :], scalar=self.lvec(li, jg, c),
                                                              in1=self.xT[:, c, xs], op0=ALU.mult, op1=ALU.add),
                 reads=[yb.b(c), self.vec.b(), self.xb(tb)], writes=[self.xb(tb)])

    def cast(self, out_ap, in_ap, rbufs, wbufs):
        K = self.K
        n = getattr(self, "_ncast", 0)
        self._ncast = n + 1
        if n % 2 == 0:
            K.op("dve", lambda e: e.tensor_copy(out=out_ap, in_=in_ap), reads=rbufs, writes=wbufs)
        else:
            K.op("act", lambda e: e.activation(out=out_ap, in_=in_ap, func=AF.Identity), reads=rbufs, writes=wbufs)

    def evac(self, eng, out_ap, pp, wbufs):
        K = self.K
        if eng == "act":
            K.op("act", lambda e: e.activation(out=out_ap, in_=pp[:], func=AF.Identity), reads=[pp.b()], writes=wbufs)
        else:
            K.op("dve", lambda e: e.tensor_copy(out=out_ap, in_=pp[:]), reads=[pp.b()], writes=wbufs)

    def ffn(self, pss, li):
        K, I = self.K, self.I
        w1v = I["ffn_w1"][li].rearrange("(kc p) n -> p kc n", p=128)
        w2v = I["ffn_w2"][li].rearrange("(hc p) n -> p hc n", p=128)
        ssem = [K.dsem("fw%d" % i) for i in range(2)]
        stsem = [K.dsem("fws%d" % i) for i in range(2)]
        for sbk in range(2):
            first = not self.ffn_cached[li]
            self.ffn_cached[li] = True
            with self.sbuf_scope() as S:
                hid = S.sb("hid", [128, 32, 2 * TB], BF16)
                with self.sbuf_scope() as S1:
                    hT = S1.sb("hT", [128, DC, 2 * TB], BF16)
                    nwb = 2 if first else 4
                    stg = [S1.sb("fstg%d" % i, [128, DC, 256], F32) for i in range(2)] if first else None
                    ws = self.norm_ws(S1, nset=1 if first else 2)
                    w1b = [S1.sb("w1b%d" % i, [128, DC, 256], BF16) for i in range(nwb)]
                    lsem4 = [K.dsem("fwl%d" % i) for i in range(4)]
                    rl = [S1.sb("rl%d" % i, [128, TB], BF16) for i in range(3)]
                    for t2 in range(2):
                        self.prenorm(ws, li, 1, sbk * 2 + t2, hT, t2 * TB, t2)
                    nrl = 0
                    for pc in range(16):
                        wb = w1b[pc % nwb]
                        st = stg[pc % 2] if first else None
                        cb = self._cb("w1s", li, pc, 0)
                        if first:
                            K.dma(ssem[pc % 2], st[:], w1v[:, :, pc * 256:(pc + 1) * 256], writes=[st.b()])
                            self.cast(wb[:], st[:], [st.b()], [wb.b()])
                            K.dma(stsem[pc % 2], self.w1s[li, pc].rearrange("p (k n) -> p k n", k=DC), wb[:], reads=[wb.b()], writes=[cb])
                        else:
                            K.dma(lsem4[pc % 4], wb[:], self.w1s[li, pc].rearrange("p (k n) -> p k n", k=DC), reads=[cb], writes=[wb.b()])
                        for hc2 in range(2):
                            hc = pc * 2 + hc2
                            for t2 in range(2):
                                pp = self.psum((0, 4))
                                for kc in range(DC):
                                    K.op("pe", lambda e, wb=wb, kc=kc, hc2=hc2, t2=t2, pp=pp: e.matmul(
                                        pp[:], lhsT=wb[:, kc, hc2 * 128:(hc2 + 1) * 128], rhs=hT[:, kc, t2 * TB:(t2 + 1) * TB],
                                        start=(kc == 0), stop=(kc == DC - 1)),
                                        reads=[wb.b(), hT.b((t2, kc))], writes=[pp.b()], inc=(kc == DC - 1))
                                r = rl[nrl % 3]
                                nrl += 1
                                K.op("act", lambda e, pp=pp, r=r: e.activation(out=r[:], in_=pp[:], func=AF.Relu),
                                     reads=[pp.b()], writes=[r.b()])
                                K.op("dve", lambda e, r=r, hc=hc, t2=t2: e.tensor_tensor(out=hid[:, hc, t2 * TB:(t2 + 1) * TB], in0=r[:], in1=r[:], op=ALU.mult),
                                     reads=[r.b()], writes=[hid.b((hc, t2))])
                with self.sbuf_scope() as S1:
                    stg = [S1.sb("gstg%d" % i, [128, 8, 128], F32) for i in range(2)] if first else None
                    w2b = [S1.sb("w2b%d" % i, [128, 32, 128], BF16) for i in range(2)]
                    yb = [S1.sb("yb%d" % i, [128, DC, TB], F32) for i in range(2)]
                    ws = self.norm_ws(S1, nset=1, pn=False)
                    npc = 0
                    for oc in range(DC):
                        wb = w2b[oc % 2]
                        cb = self._cb("w2s", li, oc, 0)
                        if first:
                            for hh in range(4):
                                st = stg[npc % 2]
                                K.dma(ssem[npc % 2], st[:], w2v[:, hh * 8:(hh + 1) * 8, oc * 128:(oc + 1) * 128], writes=[st.b()])
                                self.cast(wb[:, hh * 8:(hh + 1) * 8, :], st[:], [st.b()], [wb.b(hh)])
                                npc += 1
                            K.dma(stsem[oc % 2], self.w2s[li, oc].rearrange("p (k n) -> p k n", k=32), wb[:], reads=wb.bs(range(4)), writes=[cb])
                        else:
                            K.dma(ssem[oc % 2], wb[:], self.w2s[li, oc].rearrange("p (k n) -> p k n", k=32), reads=[cb], writes=wb.bs(range(4)))
                        for t2 in range(2):
                            pp = self.psum((0, 4))
                            for hc in range(32):
                                K.op("pe", lambda e, wb=wb, hc=hc, t2=t2, pp=pp: e.matmul(
                                    pp[:], lhsT=wb[:, hc, :], rhs=hid[:, hc, t2 * TB:(t2 + 1) * TB],
                                    start=(hc == 0), stop=(hc == 31)),
                                    reads=[wb.b(hc // 8), hid.b((hc, t2))], writes=[pp.b()], inc=(hc == 31))
                            self.evac("act" if (oc + t2) % 2 == 0 else "dve", yb[t2][:, oc, :], pp, [yb[t2].b(oc)])
                    for t2 in range(2):
                        self.postnorm_residual(ws, li, 1, sbk * 2 + t2, yb[t2])

    def layer(self, pss, li):
        kind = li % 3
        if self.do_mixers:
            if kind == 0:
                self.mla(pss, li)
            elif kind == 1:
                self.conv(pss, li)
            else:
                self.pool(pss, li)
        if self.do_ffn:
            self.ffn(pss, li)

    def load_w_bf16(self, dst, src3, stg, sems, key=0, cache=None):
        K = self.K
        A, N = src3.shape[1], src3.shape[2]
        if cache is not None:
            if not hasattr(self, "_wc"):
                self._wc = {}
            if cache in self._wc:
                scr, cb = self._wc[cache]
                self._wcn = getattr(self, "_wcn", 0) + 1
                K.dma(K.dsem("wcl%d" % (self._wcn % 4)), dst[:, 0:A, :], scr.rearrange("p (a n) -> p a n", n=N), reads=[cb], writes=[dst.b(key)])
                return
        cap = stg[0].t.shape[-1]
        per = max(1, cap // N)
        assert N <= cap
        a0 = 0
        n = getattr(self, "_lw", 0)
        while a0 < A:
            a1 = min(A, a0 + per)
            st = stg[n % len(stg)]
            view = st[:, 0:(a1 - a0) * N].rearrange("p (a n) -> p a n", n=N)
            K.dma(sems[n % len(stg)], view, src3[:, a0:a1, :], writes=[st.b()])
            self.cast(dst[:, a0:a1, :], view, [st.b()], [dst.b(key)])
            n += 1
            a0 = a1
        self._lw = n
        if cache is not None:
            scr = self.dram_tmp("wc_" + cache, [128, A * N], BF16)
            cb = Buf("wc_" + cache)
            self._wc[cache] = (scr, cb)
            K.dma(K.dsem("wcs%d" % (len(self._wc) % 2)), scr.rearrange("p (a n) -> p a n", n=N), dst[:, 0:A, :], reads=[dst.b(key)], writes=[cb])

    def rope_tables(self, S, pss, C32, S32):
        K = self.K
        pos = self.I["pos_prev" if pss == 0 else "pos_own"]
        TWO_PI = 2.0 * np.pi
        C1 = 6.28125
        C2 = TWO_PI - C1
        pi_ = S.sb("posi", [32, T], I32)
        ang = S.sb("ang", [32, T], F32)
        ki = S.sb("ki", [32, T], I32)
        kf = S.sb("kf", [32, T], F32)
        r = S.sb("rr", [32, T], F32)
        sem = K.dsem("misc")
        K.dma(sem, pi_[:], pos, writes=[pi_.b()])
        K.op("dve", lambda e: e.tensor_copy(out=ang[:], in_=pi_[:]), reads=[pi_.b()], writes=[ang.b()])
        K.op("dve", lambda e: e.tensor_scalar(out=ang[:], in0=ang[:], scalar1=self.ident[0:32, 128:129], scalar2=None, op0=ALU.mult),
             reads=[ang.b(), self.ident.b()], writes=[ang.b()])
        for which in range(2):
            off = 0.0 if which == 0 else 0.5 * np.pi
            K.op("dve", lambda e: e.tensor_scalar(out=ki[:], in0=ang[:], scalar1=float(off), scalar2=float(1.0 / TWO_PI), op0=ALU.add, op1=ALU.mult),
                 reads=[ang.b()], writes=[ki.b()])
            K.op("dve", lambda e: e.tensor_copy(out=kf[:], in_=ki[:]), reads=[ki.b()], writes=[kf.b()])
            K.op("dve", lambda e: e.scalar_tensor_tensor(out=r[:], in0=kf[:], scalar=float(-C1), in1=ang[:], op0=ALU.mult, op1=ALU.add),
                 reads=[kf.b(), ang.b()], writes=[r.b()])
            K.op("dve", lambda e: e.scalar_tensor_tensor(out=r[:], in0=kf[:], scalar=float(-C2), in1=r[:], op0=ALU.mult, op1=ALU.add),
                 reads=[kf.b(), r.b()], writes=[r.b()])
            K.op("dve", lambda e: e.tensor_scalar(out=r[:], in0=r[:], scalar1=float(off), scalar2=3.1415925, op0=ALU.add, op1=ALU.min),
                 reads=[r.b()], writes=[r.b()])
            K.op("dve", lambda e: e.tensor_scalar(out=r[:], in0=r[:], scalar1=-3.1415925, scalar2=None, op0=ALU.max),
                 reads=[r.b()], writes=[r.b()])
            if which == 0:
                for lo in (0, 64):
                    K.op("act", lambda e, lo=lo: e.activation(out=S32[lo:lo + 32, :], in_=r[:], func=AF.Sin, scale=self.ident[0:32, 129:130]),
                         reads=[r.b(), self.ident.b()], writes=[S32.b()])
            else:
                for lo in (0, 64):
                    K.op("act", lambda e, lo=lo: e.activation(out=C32[lo:lo + 32, :], in_=r[:], func=AF.Sin),
                         reads=[r.b()], writes=[C32.b()])

    def mla(self, pss, li, kv_only=False):
        K, I = self.K, self.I
        j = li // 3
        NKB_OWN = T // 128
        NK = T * (pss + 1)
        NKB = NK // 128
        prevb = self.flags[:, 0:1]
        wsem = [K.dsem("mw%d" % i) for i in range(2)]
        csems = [K.dsem("cache_st%d" % i) for i in range(4)]
        lsem = [K.dsem("cache_ld%d" % i) for i in range(2)]
        with self.sbuf_scope() as SM:
            cqn = SM.sb("cqn", [128, 3, T], BF16) if not kv_only else None
            C32 = SM.sb("C32", [96, T], BF16)
            S32 = SM.sb("S32", [96, T], BF16)
            ao = SM.sb("ao", [128, DC, T], BF16) if not kv_only else None
            with self.sbuf_scope() as SK:
                ckvn = SK.sb("ckvn", [128, 2, T], BF16)
                krope = SK.sb("krope", [96, T], BF16)
                with self.sbuf_scope() as S1:
                    with self.sbuf_scope() as SR:
                        self.rope_tables(SR, pss, C32, S32)
                    stg = [S1.sb("mstg%d" % i, [128, 1024], F32) for i in range(2)]
                    wdq = S1.sb("wdq", [128, DC, QL], BF16)
                    wdkv = S1.sb("wdkv", [128, DC, 320], BF16)
                    self.load_w_bf16(wdq, I["w_dq"][j].rearrange("(kc p) n -> p kc n", p=128), stg, wsem, cache="wdq%d" % j)
                    dkA = I["w_dkvA"][j].rearrange("(kc p) n -> p kc n", p=128)
                    dkB = I["w_dkvB"][j].rearrange("(kc p) n -> p kc n", p=128)
                    self.load_w_bf16(_Vn(wdkv, 0, 288), dkA, stg, wsem, cache="wdkvA%d" % j)
                    self.load_w_bf16(_Vn(wdkv, 288, 320), dkB, stg, wsem, cache="wdkvB%d" % j)
                    gq = self.sbv("qn_gT", [128, 6])
                    gkv = self.sbv("kvn_gT", [128, 4])
                    hTs = [S1.sb("mhT%d" % i, [128, DC, TB], BF16) for i in range(2)]
                    ws = self.norm_ws(S1, nset=1)
                    sqqs = [S1.sb("sqq%d" % i, [128, 3, TB], BF16) for i in range(1)] * 2
                    sqks = [S1.sb("sqk%d" % i, [128, 2, TB], BF16) for i in range(1)] * 2
                    kr1 = [S1.sb("kr1_%d" % i, [32, TB], F32) for i in range(1)] * 2
                    kr2 = [S1.sb("kr2_%d" % i, [32, TB], F32) for i in range(1)] * 2
                    for tb in range(NTB):
                        ts = slice(tb * TB, (tb + 1) * TB)
                        hT = hTs[tb % 2]
                        self.prenorm(ws, li, 0, tb, hT, 0, 0)
                        if True:
                            sqq = sqqs[tb % 2]
                            pq = []
                            for oc in range(0 if kv_only else 3):
                                pp = self.psum((0, 4))
                                pq.append(pp)
                                for kc in range(DC):
                                    K.op("pe", lambda e, pp=pp, oc=oc, kc=kc: e.matmul(pp[:], lhsT=wdq[:, kc, oc * 128:(oc + 1) * 128], rhs=hT[:, kc, :],
                                                                                         start=(kc == 0), stop=(kc == DC - 1)),
                                         reads=[wdq.b(), hT.b((0, kc))], writes=[pp.b()], inc=(kc == DC - 1))
                                self.sq_ops(sqq, oc, pp[:], [pp.b()], "act")
                            rstd = self.rstd_from_sq(self.ws_set(ws), sqq, 3, QL, TB) if not kv_only else None
                            for oc in range(0 if kv_only else 3):
                                K.op("dve", lambda e, oc=oc: e.scalar_tensor_tensor(out=cqn[:, oc, ts], in0=pq[oc][:], scalar=gq[:, j * 3 + oc:j * 3 + oc + 1],
                                                                                      in1=rstd[:], op0=ALU.mult, op1=ALU.mult),
                                     reads=[pq[oc].b(), rstd.b(), gq.b()], writes=[cqn.b(tb)])
                            sqk = sqks[tb % 2]
                            pk = []
                            for oc in range(2):
                                pp = self.psum((0, 4))
                                pk.append(pp)
                                for kc in range(DC):
                                    K.op("pe", lambda e, pp=pp, oc=oc, kc=kc: e.matmul(pp[:], lhsT=wdkv[:, kc, oc * 128:(oc + 1) * 128], rhs=hT[:, kc, :],
                                                                                         start=(kc == 0), stop=(kc == DC - 1)),
                                         reads=[wdkv.b(), hT.b((0, kc))], writes=[pp.b()], inc=(kc == DC - 1))
                                self.sq_ops(sqk, oc, pp[:], [pp.b()], "act")
                            rstd2 = self.rstd_from_sq(self.ws_set(ws), sqk, 2, KVL, TB)
                            for oc in range(2):
                                K.op("dve", lambda e, oc=oc: e.scalar_tensor_tensor(out=ckvn[:, oc, ts], in0=pk[oc][:], scalar=gkv[:, j * 2 + oc:j * 2 + oc + 1],
                                                                                      in1=rstd2[:], op0=ALU.mult, op1=ALU.mult),
                                     reads=[pk[oc].b(), rstd2.b(), gkv.b()], writes=[ckvn.b(tb)])
                            pr = []
                            for w in range(2):
                                pp = self.psum((4, 2))
                                pr.append(pp)
                                for kc in range(DC):
                                    K.op("pe", lambda e, pp=pp, w=w, kc=kc: e.matmul(pp[0:32, :], lhsT=wdkv[:, kc, 256 + 32 * w:288 + 32 * w], rhs=hT[:, kc, :],
                                                                                       start=(kc == 0), stop=(kc == DC - 1)),
                                         reads=[wdkv.b(), hT.b((0, kc))], writes=[pp.b()], inc=(kc == DC - 1))
                            t1 = kr1[tb % 2]
                            t2 = kr2[tb % 2]
                            K.op("dve", lambda e: e.tensor_tensor(out=t1[:], in0=pr[0][0:32, :], in1=C32[0:32, ts], op=ALU.mult),
                                 reads=[pr[0].b(), C32.b()], writes=[t1.b()])
                            K.op("dve", lambda e: e.tensor_tensor(out=t2[:], in0=pr[1][0:32, :], in1=S32[0:32, ts], op=ALU.mult),
                                 reads=[pr[1].b(), S32.b()], writes=[t2.b()])
                            K.op("dve", lambda e: e.tensor_tensor(out=krope[64:96, ts], in0=t1[:], in1=t2[:], op=ALU.add),
                                 reads=[t1.b(), t2.b()], writes=[krope.b(tb)])
                with self.sbuf_scope() as S1:
                    stg = [S1.sb("mstg%d" % i, [128, 1024], F32) for i in range(2)]
                    wk = S1.sb("wukvK", [128, 2, H * 64], BF16)
                    wv = S1.sb("wukvV", [128, 2, H * 64], BF16)
                    for hh in range(0, H, 8):
                        self.load_w_bf16(_Vn(wk, hh * 64, (hh + 8) * 64), I["w_ukvK"][j].rearrange("(kc p) n -> p kc n", p=128)[:, :, hh * 64:(hh + 8) * 64], stg, wsem, key=("k", hh), cache="wk%d_%d" % (j, hh))
                    for hh in range(0, H, 8):
                        self.load_w_bf16(_Vn(wv, hh * 64, (hh + 8) * 64), I["w_ukvV"][j].rearrange("(kc p) n -> p kc n", p=128)[:, :, hh * 64:(hh + 8) * 64], stg, wsem, key=("v", hh), cache="wv%d_%d" % (j, hh))
                    khs = [S1.sb("kh%d" % i, [96, T], BF16) for i in range(2)]
                    vhs = [S1.sb("vh%d" % i, [128, NKB_OWN, 128], BF16) for i in range(2)]
                    sqkhs = [S1.sb("sqkh%d" % i, [96, T], BF16) for i in range(2)]
                    mx4s = [S1.sb("mx4_%d" % i, [128, 4], F32) for i in range(2)]
                    for v in vhs:
                        K.op("pool", lambda e, v=v: e.memset(v[:, :, 64:128], 1.0), writes=[v.b("ones")])
                    for h in range(H):
                        kh, vh = khs[h % 2], vhs[h % 2]
                        sqkh, mx4 = sqkhs[h % 2], mx4s[h % 2]
                        K.op("act", lambda e, kh=kh: e.activation(out=kh[64:96, :], in_=krope[64:96, :], func=AF.Identity),
                             reads=krope.bs(range(NTB)), writes=[kh.b("r")])
                        for tb in range(NTB):
                            ts = slice(tb * TB, (tb + 1) * TB)
                            pp = self.psum((0, 4))
                            for c in range(2):
                                K.op("pe", lambda e, pp=pp, c=c: e.matmul(pp[0:64, :], lhsT=wk[:, c, h * 64:(h + 1) * 64], rhs=ckvn[:, c, ts],
                                                                            start=(c == 0), stop=(c == 1)),
                                     reads=[wk.b(("k", 0)), wk.b(("k", 8)), ckvn.b(tb)], writes=[pp.b()], inc=(c == 1))
                            if tb % 2 == 0:
                                K.op("dve", lambda e, pp=pp: e.tensor_copy(out=kh[0:64, ts], in_=pp[0:64, :]), reads=[pp.b()], writes=[kh.b(tb)])
                            else:
                                K.op("act", lambda e, pp=pp: e.activation(out=kh[0:64, ts], in_=pp[0:64, :], func=AF.Identity), reads=[pp.b()], writes=[kh.b(tb)])
                        K.dma(csems[(2 * h) % 4], self.kcache[j, h, :, pss * T:(pss + 1) * T], kh[:], reads=kh.bs(["r", 0, 1, 2, 3]), writes=[self.kcb(j, h, pss)])
                        for g in range(2):
                            pp = self.psum((0, 4))
                            for kb in range(8):
                                blk = g * 8 + kb
                                for c in range(2):
                                    K.op("pe", lambda e, pp=pp, kb=kb, blk=blk, c=c: e.matmul(pp[:, kb * 64:(kb + 1) * 64], lhsT=ckvn[:, c, blk * 128:(blk + 1) * 128],
                                                                                                rhs=wv[:, c, h * 64:(h + 1) * 64], start=(c == 0), stop=(c == 1)),
                                         reads=[wv.b(("v", 0)), wv.b(("v", 8)), ckvn.b(blk // 4)], writes=[pp.b()], inc=(kb == 7 and c == 1))
                            src = pp[:, :].rearrange("p (k d) -> p k d", d=64)
                            if g == 0:
                                K.op("dve", lambda e, src=src: e.tensor_copy(out=vh[:, 0:8, 0:64], in_=src), reads=[pp.b()], writes=[vh.b(0)])
                            else:
                                K.op("act", lambda e, src=src: e.activation(out=vh[:, 8:16, 0:64], in_=src, func=AF.Identity), reads=[pp.b()], writes=[vh.b(1)])
                        K.dma(csems[(2 * h + 1) % 4], self.vcache[j, h, :, pss * NKB_OWN:(pss + 1) * NKB_OWN, :], vh[:], reads=vh.bs(["ones", 0, 1]), writes=[self.vcb(j, h, pss)])
                        K.op("dve", lambda e: e.tensor_tensor(out=sqkh[:], in0=kh[:], in1=kh[:], op=ALU.mult), reads=kh.bs(["r", 0, 1, 2, 3]), writes=[sqkh.b()])
                        for tb in range(NTB):
                            ts = slice(tb * TB, (tb + 1) * TB)
                            pp = self.psum((6, 2))
                            K.op("pe", lambda e, pp=pp: e.matmul(pp[:], lhsT=self.onesb[0:96, :], rhs=sqkh[0:96, ts], start=True, stop=True),
                                 reads=[sqkh.b(), self.onesb.b()], writes=[pp.b()])
                            K.op("dve", lambda e, pp=pp, tb=tb: e.tensor_reduce(out=mx4[:, tb:tb + 1], in_=pp[:], axis=AX.X, op=ALU.max),
                                 reads=[pp.b()], writes=[mx4.b()])
                        kcol = (j * 2 + pss) * H + h
                        K.op("dve", lambda e: e.tensor_reduce(out=self.kmax2[:, kcol:kcol + 1], in_=mx4[:], axis=AX.X, op=ALU.max),
                             reads=[mx4.b()], writes=[self.kmax2.b()])
            if kv_only:
                return
            with self.sbuf_scope() as S1:
                wqa = S1.sb("wuqA", [128, 3, H * 96], BF16)
                wqb = S1.sb("wuqB", [128, 3, H * 32], BF16)
                with self.sbuf_scope() as SG:
                    stg = [SG.sb("mstg%d" % i, [128, 1024], F32) for i in range(2)]
                    for hh in range(0, H, 8):
                        self.load_w_bf16(_Vn(wqa, hh * 96, (hh + 8) * 96), I["w_uqA"][j].rearrange("(kc p) n -> p kc n", p=128)[:, :, hh * 96:(hh + 8) * 96], stg, wsem, cache="wqa%d_%d" % (j, hh))
                    self.load_w_bf16(wqb, I["w_uqB"][j].rearrange("(kc p) n -> p kc n", p=128), stg, wsem, cache="wqb%d" % j)
                khf = [S1.sb("khf%d" % i, [96, NK], BF16) for i in range(2)]
                vhf = [S1.sb("vhf%d" % i, [128, NKB, 128], BF16) for i in range(2)]
                qhs = [S1.sb("qh%d" % i, [96, T], BF16) for i in range(2)]
                sqqh = S1.sb("sqqh", [96, T], BF16)
                mxq = S1.sb("mxq", [128, 4], F32)
                sc = [S1.sb("attsc%d" % i, [128, 8], F32) for i in range(2)]
                pts = [S1.sb("pt%d" % i, [128, TB], BF16) for i in range(4)]
                ptd = [S1.sb("ptd%d" % i, [128, TB], BF16) for i in range(4)]
                r1 = [S1.sb("qr1_%d" % i, [96, TB], F32) for i in range(2)]
                r2 = [S1.sb("qr2_%d" % i, [96, TB], F32) for i in range(2)]
                rec = [S1.sb("rec%d" % i, [64, TB], F32) for i in range(2)]
                for dj in range(4):
                    K.op("pool", lambda e, dj=dj: e.memset(ptd[dj][:], 0.0), writes=[ptd[dj].b()])
                cnt = {"pt": 0, "rec": 0}

                def prep(h):
                    kf_, vf_, qh, s_ = khf[h % 2], vhf[h % 2], qhs[h % 2], sc[h % 2]
                    K.dma(lsem[h % 2], kf_[:], self.kcache[j, h, :, 0:NK], reads=[self.kcb(j, h, p_) for p_ in range(pss + 1)], writes=[kf_.b()], hold=True)
                    K.dma(lsem[h % 2], vf_[:], self.vcache[j, h, :, 0:NKB, :], reads=[self.vcb(j, h, p_) for p_ in range(pss + 1)], writes=[vf_.b()])
                    yield
                    for tb in range(NTB):
                        ts = slice(tb * TB, (tb + 1) * TB)
                        pa = self.psum((6, 1))
                        for c in range(3):
                            K.op("pe", lambda e, c=c: e.matmul(pa[0:96, :], lhsT=wqa[:, c, h * 96:(h + 1) * 96], rhs=cqn[:, c, ts], start=(c == 0), stop=(c == 2)),
                                 reads=[wqa.b(), cqn.b(tb)], writes=[pa.b()], inc=(c == 2))
                        pb = self.psum((7, 1))
                        for c in range(3):
                            K.op("pe", lambda e, c=c: e.matmul(pb[0:32, :], lhsT=wqb[:, c, h * 32:(h + 1) * 32], rhs=cqn[:, c, ts], start=(c == 0), stop=(c == 2)),
                                 reads=[wqb.b(), cqn.b(tb)], writes=[pb.b()], inc=(c == 2))
                        a1, a2 = r1[tb % 2], r2[tb % 2]
                        K.op("dve", lambda e: e.tensor_tensor(out=a1[64:96, :], in0=pa[64:96, :], in1=C32[64:96, ts], op=ALU.mult), reads=[pa.b(), C32.b()], writes=[a1.b()])
                        K.op("dve", lambda e: e.tensor_tensor(out=a2[64:96, :], in0=pb[0:32, :], in1=S32[0:32, ts], op=ALU.mult), reads=[pb.b(), S32.b()], writes=[a2.b()])
                        K.op("dve", lambda e: e.tensor_tensor(out=qh[64:96, ts], in0=a1[64:96, :], in1=a2[64:96, :], op=ALU.add), reads=[a1.b(), a2.b()], writes=[qh.b((tb, "r"))])
                        K.op("dve", lambda e: e.tensor_copy(out=qh[0:64, ts], in_=pa[0:64, :]), reads=[pa.b()], writes=[qh.b((tb, "n"))])
                        yield
                    K.op("dve", lambda e: e.tensor_tensor(out=sqqh[:], in0=qh[:], in1=qh[:], op=ALU.mult), reads=qh.bs([(t_, x_) for t_ in range(NTB) for x_ in "rn"]), writes=[sqqh.b()])
                    for tb in range(NTB):
                        ts = slice(tb * TB, (tb + 1) * TB)
                        pp = self.psum((6, 1))
                        K.op("pe", lambda e: e.matmul(pp[:], lhsT=self.onesb[0:96, :], rhs=sqqh[0:96, ts], start=True, stop=True),
                             reads=[sqqh.b(), self.onesb.b()], writes=[pp.b()])
                        K.op("dve", lambda e: e.tensor_reduce(out=mxq[:, tb:tb + 1], in_=pp[:], axis=AX.X, op=ALU.max), reads=[pp.b()], writes=[mxq.b()])
                    yield
                    K.op("dve", lambda e: e.tensor_reduce(out=s_[:, 0:1], in_=mxq[:], axis=AX.X, op=ALU.max), reads=[mxq.b()], writes=[s_.b()])
                    k0 = (j * 2 + 0) * H + h
                    k1 = (j * 2 + pss) * H + h
                    K.op("dve", lambda e: e.tensor_tensor(out=s_[:, 1:2], in0=self.kmax2[:, k0:k0 + 1], in1=self.kmax2[:, k1:k1 + 1], op=ALU.max),
                         reads=[self.kmax2.b()], writes=[s_.b()])
                    K.op("dve", lambda e: e.scalar_tensor_tensor(out=s_[:, 2:3], in0=s_[:, 0:1], scalar=1e-12, in1=s_[:, 1:2], op0=ALU.max, op1=ALU.mult),
                         reads=[s_.b()], writes=[s_.b()])
                    K.op("dve", lambda e: e.tensor_scalar(out=s_[:, 2:3], in0=s_[:, 2:3], scalar1=1e-12, scalar2=None, op0=ALU.max), reads=[s_.b()], writes=[s_.b()])
                    K.op("act", lambda e: e.activation(out=s_[:, 3:4], in_=s_[:, 2:3], func=AF.Ln), reads=[s_.b()], writes=[s_.b()])
                    K.op("act", lambda e: e.activation(out=s_[:, 4:5], in_=s_[:, 3:4], func=AF.Exp, scale=0.5), reads=[s_.b()], writes=[s_.b()])
                    K.op("dve", lambda e: e.tensor_scalar(out=s_[:, 5:6], in0=s_[:, 4:5], scalar1=float(-SCALE), scalar2=None, op0=ALU.mult), reads=[s_.b()], writes=[s_.b()])
                    K.op("dve", lambda e: e.tensor_tensor(out=s_[:, 6:7], in0=s_[:, 5:6], in1=prevb, op=ALU.add), reads=[s_.b(), self.flags.b()], writes=[s_.b()])

                LA = 3
                lazy = None
                lazy_layers = []
                if getattr(self, "ada_lazy_ok", False) and self.ada_rest and pss == 0:
                    lazy_layers = self.ada_rest
                    self.ada_rest = []
                    lazy = self.ada_gen(S1, lazy_layers, nbuf=3)

                def attn(h):
                    kf_, vf_, qh, s_ = khf[h % 2], vhf[h % 2], qhs[h % 2], sc[h % 2]
                    items = []
                    for qb in range(NTB):
                        blocks = []
                        if pss == 1:
                            for kb in range(NKB_OWN):
                                blocks.append((kb, 6, None))
                        for kbo in range(4 * (qb + 1)):
                            blocks.append((pss * NKB_OWN + kbo, 5, (kbo - 4 * qb) if kbo >= 4 * qb else None))
                        for i_, (blk, bcol, dj) in enumerate(blocks):
                            items.append((qb, blk, bcol, dj, i_ == 0, i_ == len(blocks) - 1))
                    n = len(items)
                    sts = [None] * n
                    accs = {}
                    gen = prep(h + 1) if h + 1 < H else iter(())
                    every = max(1, n // 8)
                    aevery = max(1, (n * H) // (48 * max(1, len(lazy_layers)) + 8)) if lazy is not None else 0

                    def qk(i):
                        qb, blk, bcol, dj, first, last = items[i]
                        c0 = 0 if dj is None else 128 * dj
                        st = self.psum((0, 4))
                        sts[i] = st
                        K.op("pe", lambda e: e.matmul(st[:, c0:TB], lhsT=kf_[0:96, blk * 128:(blk + 1) * 128], rhs=qh[0:96, qb * TB + c0:(qb + 1) * TB],
                                                      start=True, stop=True),
                             reads=[kf_.b(), qh.b((qb, "r")), qh.b((qb, "n"))], writes=[st.b()])
                    for i in range(min(LA, n)):
                        qk(i)
                    for i in range(n):
                        if i + LA < n:
                            qk(i + LA)
                        if i % every == every - 1:
                            next(gen, None)
                        if lazy is not None and i % aevery == aevery - 1:
                            next(lazy, None)
                        qb, blk, bcol, dj, first, last = items[i]
                        st = sts[i]
                        if first:
                            accs[qb] = self.psum((4, 2))
                        acc = accs[qb]
                        if dj is None:
                            pt = pts[cnt["pt"] % len(pts)]
                            cnt["pt"] += 1
                            c0 = 0
                            K.op("act", lambda e: e.activation(out=pt[:], in_=st[:], func=AF.Exp, bias=s_[:, bcol:bcol + 1], scale=float(SCALE)),
                                 reads=[st.b(), s_.b()], writes=[pt.b()])
                        else:
                            pt = ptd[dj]
                            c0 = 128 * dj
                            K.op("act", lambda e: e.activation(out=pt[0:64, c0:TB], in_=st[0:64, c0:TB], func=AF.Exp, bias=s_[0:64, bcol:bcol + 1], scale=float(SCALE)),
                                 reads=[st.b(), s_.b()], writes=[pt.b()])
                            K.op("act", lambda e: e.activation(out=pt[64:128, c0 + 64:TB], in_=st[64:128, c0 + 64:TB], func=AF.Exp, bias=s_[64:128, bcol:bcol + 1], scale=float(SCALE)),
                                 reads=[st.b(), s_.b()], writes=[pt.b()])
                        K.op("pe", lambda e: e.matmul(acc[:, c0:TB], lhsT=vf_[:, blk, :], rhs=pt[:, c0:TB], start=first, stop=last),
                             reads=[vf_.b(), pt.b()], writes=[acc.b()], inc=last)
                        if self.warm and not last:
                            K.op("pe", lambda e: e.matmul(self.ps[7][64:128, :], lhsT=self.onesb[:, 0:64], rhs=self.onesb_w[:, :], start=True, stop=True),
                                 reads=[self.onesb.b(), self.onesb_w.b()], inc=False)
                        if last:
                            rc = rec[cnt["rec"] % 2]
                            cnt["rec"] += 1
                            K.op("dve", lambda e: e.reciprocal(out=rc[0:64, :], in_=acc[64:128, :]), reads=[acc.b()], writes=[rc.b()])
                            po = (h % 2) * 64
                            K.op("dve", lambda e: e.tensor_tensor(out=ao[po:po + 64, h // 2, qb * TB:(qb + 1) * TB], in0=acc[0:64, :], in1=rc[0:64, :], op=ALU.mult),
                                 reads=[acc.b(), rc.b()], writes=[ao.b((h // 2, qb))])
                    for _ in gen:
                        pass

                for _ in prep(0):
                    pass
                for h in range(H):
                    attn(h)
                if lazy is not None:
                    for _ in lazy:
                        pass
            with self.sbuf_scope() as S1:
                stg = [S1.sb("mstg%d" % i, [128, 1024], F32) for i in range(2)]
                wo = S1.sb("wo", [128, DC, D], BF16)
                self.load_w_bf16(wo, I["w_o"][j].rearrange("(kc p) n -> p kc n", p=128), stg, wsem, cache="wo%d" % j)
                ybs = [S1.sb("myb%d" % i, [128, DC, TB], F32) for i in range(2)]
                ws = self.norm_ws(S1, nset=2, pn=False)
                for tb in range(NTB):
                    yb = ybs[tb % 2]
                    for oc in range(DC):
                        pp = self.psum((0, 4))
                        for kc in range(DC):
                            K.op("pe", lambda e, pp=pp, oc=oc, kc=kc: e.matmul(pp[:], lhsT=wo[:, kc, oc * 128:(oc + 1) * 128], rhs=ao[:, kc, tb * TB:(tb + 1) * TB],
                                                                                 start=(kc == 0), stop=(kc == DC - 1)),
                                 reads=[wo.b(), ao.b((kc, tb))], writes=[pp.b()], inc=(kc == DC - 1))
                        self.evac("act" if oc % 2 == 0 else "dve", yb[:, oc, :], pp, [yb.b(oc)])
                    self.postnorm_residual(ws, li, 0, tb, yb)

    def kcb(self, j, h, p_):
        return self._cb("k", j, h, p_)

    def vcb(self, j, h, p_):
        return self._cb("v", j, h, p_)

    def _cb(self, kind, j, h, p_):
        if not hasattr(self, "_cbufs"):
            self._cbufs = {}
        key = (kind, j, h, p_)
        if key not in self._cbufs:
            self._cbufs[key] = Buf("cache%s" % (key,))
        return self._cbufs[key]

    def sbv(self, name, shape):
        if not hasattr(self, "_sbv"):
            self._sbv = {}
        if name not in self._sbv:
            assert shape is not None
            t = self.sb(name, shape, F32)
            self.K.dma(self.K.dsem("misc"), t[:], self.I[name], writes=[t.b()], hold=True)
            self._sbv[name] = t
        return self._sbv[name]


    def conv(self, pss, li):
        K, I = self.K, self.I
        wsem = [K.dsem("mw%d" % i) for i in range(2)]
        b1 = self.sbv("b_pw1T", None)
        cv = self.sbv("cv_vecT", None)
        wdw = self.sbv("w_dwT", None)
        uh = self.uhalo
        hflag = self.flags[:, 1:2]
        if pss == 0:
            K.op("dve", lambda e: e.memset(uh[:], 0.0), writes=[uh.b()])
        else:
            K.op("dve", lambda e: e.tensor_scalar(out=uh[:], in0=uh[:], scalar1=hflag, scalar2=None, op0=ALU.mult),
                 reads=[uh.b(), self.flags.b()], writes=[uh.b()])
        w1v = I["w_pw1"].rearrange("(kc p) n -> p kc n", p=128)
        w2v = I["w_pw2"].rearrange("(kc p) n -> p kc n", p=128)
        W = 2 * TB
        for sbk in range(2):
            with self.sbuf_scope() as SV:
                vT = SV.sb("vT", [128, DC, W], F32)
                with self.sbuf_scope() as SU:
                    uT = SU.sb("uT", [128, DC, 32 + W], BF16)
                    K.op("pool", lambda e: e.tensor_copy(out=uT[:, :, 0:32], in_=uh[:]), reads=[uh.b()], writes=[uT.b("h")])
                    with self.sbuf_scope() as S1:
                        w1 = S1.sb("wpw1", [128, DC, 2 * D], BF16)
                        with self.sbuf_scope() as SG:
                            stg = [SG.sb("cstg%d" % i, [128, 1024], F32) for i in range(2)]
                            for q4 in range(4):
                                self.load_w_bf16(_Vn(w1, q4 * 512, (q4 + 1) * 512), w1v[:, :, q4 * 512:(q4 + 1) * 512], stg, wsem, key=q4, cache="pw1_%d" % q4)
                        hT = S1.sb("chT", [128, DC, TB], BF16)
                        sg = [S1.sb("sig%d" % i, [128, TB], F32) for i in range(2)]
                        ws = self.norm_ws(S1, nset=1)
                        for t2 in range(2):
                            tb = sbk * 2 + t2
                            self.prenorm(ws, li, 0, tb, hT, 0, 0)
                            for oc in range(DC):
                                pa = self.psum((0, 4))
                                pb = self.psum((0, 4))
                                for (pp, off) in ((pa, 0), (pb, D)):
                                    for kc in range(DC):
                                        K.op("pe", lambda e, pp=pp, off=off, kc=kc: e.matmul(pp[:], lhsT=w1[:, kc, off + oc * 128:off + (oc + 1) * 128], rhs=hT[:, kc, :],
                                                                                               start=(kc == 0), stop=(kc == DC - 1)),
                                             reads=[w1.b((off + oc * 128) // 512), hT.b((0, kc))], writes=[pp.b()], inc=(kc == DC - 1))
                                sgt = sg[oc % 2]
                                K.op("act", lambda e: e.activation(out=sgt[:], in_=pb[:], func=AF.Sigmoid, bias=b1[:, DC + oc:DC + oc + 1]),
                                     reads=[pb.b(), b1.b()], writes=[sgt.b()])
                                K.op("dve", lambda e: e.scalar_tensor_tensor(out=uT[:, oc, 32 + t2 * TB:32 + (t2 + 1) * TB], in0=pa[:], scalar=b1[:, oc:oc + 1],
                                                                              in1=sgt[:], op0=ALU.add, op1=ALU.mult),
                                     reads=[pa.b(), sgt.b(), b1.b()], writes=[uT.b((oc, t2))])
                    K.op("pool", lambda e: e.tensor_copy(out=uh[:], in_=uT[:, :, W:W + 32]), reads=uT.bs([(c, 1) for c in range(DC)]), writes=[uh.b()])
                    with self.sbuf_scope() as S1:
                        dgs = [S1.sb("dg%d" % i, [128, CW, 128], BF16) for i in range(2)]
                        for c in range(DC):
                            dg = dgs[c % 2]
                            for jt in range(CW):
                                if jt % 2 == 0:
                                    K.op("act", lambda e, jt=jt: e.activation(out=dg[:, jt, :], in_=self.identb[:], func=AF.Copy, scale=wdw[:, c, jt:jt + 1]),
                                         reads=[self.identb.b(), wdw.b()], writes=[dg.b(jt % 2)])
                                else:
                                    K.op("dve", lambda e, jt=jt: e.tensor_scalar(out=dg[:, jt, :], in0=self.identb[:], scalar1=wdw[:, c, jt:jt + 1], scalar2=None, op0=ALU.mult),
                                         reads=[self.identb.b(), wdw.b()], writes=[dg.b(jt % 2)])
                            for t2 in range(2):
                                pp = self.psum((0, 4))
                                for jt in range(CW):
                                    c0 = t2 * TB + 2 + jt
                                    K.op("pe", lambda e, jt=jt, c0=c0: e.matmul(pp[:], lhsT=dg[:, jt, :], rhs=uT[:, c, c0:c0 + TB], start=(jt == 0), stop=(jt == CW - 1)),
                                         reads=[dg.b(jt % 2), uT.b("h"), uT.b((c, 0)), uT.b((c, 1))], writes=[pp.b()], inc=(jt == CW - 1))
                                if (c + t2) % 2 == 0:
                                    K.op("act", lambda e: e.activation(out=vT[:, c, t2 * TB:(t2 + 1) * TB], in_=pp[:], func=AF.Identity, bias=cv[:, c:c + 1]),
                                         reads=[pp.b(), cv.b()], writes=[vT.b((c, t2))])
                                else:
                                    K.op("dve", lambda e: e.tensor_scalar(out=vT[:, c, t2 * TB:(t2 + 1) * TB], in0=pp[:], scalar1=cv[:, c:c + 1], scalar2=None, op0=ALU.add),
                                         reads=[pp.b(), cv.b()], writes=[vT.b((c, t2))])
                with self.sbuf_scope() as S1:
                    w2 = S1.sb("wpw2", [128, DC, D], BF16)
                    with self.sbuf_scope() as SG:
                        stg = [SG.sb("cstg%d" % i, [128, 1024], F32) for i in range(2)]
                        self.load_w_bf16(w2, w2v, stg, wsem, cache="pw2")
                    for t2 in range(2):
                        tb = sbk * 2 + t2
                        vs = slice(t2 * TB, (t2 + 1) * TB)
                        with self.sbuf_scope() as S2:
                            yb = S2.sb("cyb", [128, DC, TB], F32)
                            with self.sbuf_scope() as S3:
                                vb = S3.sb("vb", [128, DC, TB], BF16)
                                sqv = S3.sb("sqv", [128, DC, TB], BF16)
                                for c in range(DC):
                                    K.op("act", lambda e, c=c: e.activation(out=vb[:, c, :], in_=vT[:, c, vs], func=AF.Identity), reads=[vT.b((c, t2))], writes=[vb.b(c)])
                                    self.sq_ops(sqv, c, vT[:, c, vs], [vT.b((c, t2))], "dve")
                                p1 = self.psum((6, 2))
                                p2 = self.psum((6, 2))
                                for (pp, src) in ((p1, vb), (p2, sqv)):
                                    for c in range(DC):
                                        K.op("pe", lambda e, pp=pp, src=src, c=c: e.matmul(pp[:], lhsT=self.onesb[:], rhs=src[:, c, :], start=(c == 0), stop=(c == DC - 1)),
                                             reads=[src.b(c), self.onesb.b()], writes=[pp.b()], inc=(c == DC - 1))
                                m = S3.sb("lnm", [128, TB], F32)
                                msq = S3.sb("lnmsq", [128, TB], F32)
                                var = S3.sb("lnvar", [128, TB], F32)
                                lt = S3.sb("lnlt", [128, TB], F32)
                                rstd = S3.sb("lnrstd", [128, TB], F32)
                                nmr = S3.sb("lnnmr", [128, TB], F32)
                                K.op("act", lambda e: e.activation(out=m[:], in_=p1[:], func=AF.Identity, scale=1.0 / D), reads=[p1.b()], writes=[m.b()])
                                K.op("dve", lambda e: e.tensor_tensor(out=msq[:], in0=m[:], in1=m[:], op=ALU.mult), reads=[m.b()], writes=[msq.b()])
                                K.op("dve", lambda e: e.scalar_tensor_tensor(out=var[:], in0=p2[:], scalar=1.0 / D, in1=msq[:], op0=ALU.mult, op1=ALU.subtract),
                                     reads=[p2.b(), msq.b()], writes=[var.b()])
                                K.op("dve", lambda e: e.tensor_scalar(out=var[:], in0=var[:], scalar1=0.0, scalar2=None, op0=ALU.max), reads=[var.b()], writes=[var.b()])
                                K.op("act", lambda e: e.activation(out=lt[:], in_=var[:], func=AF.Ln, bias=self.epsc[:, 0:1]), reads=[var.b(), self.epsc.b()], writes=[lt.b()])
                                K.op("act", lambda e: e.activation(out=rstd[:], in_=lt[:], func=AF.Exp, scale=-0.5), reads=[lt.b()], writes=[rstd.b()])
                                K.op("dve", lambda e: e.scalar_tensor_tensor(out=nmr[:], in0=m[:], scalar=-1.0, in1=rstd[:], op0=ALU.mult, op1=ALU.mult),
                                     reads=[m.b(), rstd.b()], writes=[nmr.b()])
                                sT = S3.sb("sT", [128, DC, TB], BF16)
                                tt = [S3.sb("lntt%d" % i, [128, TB], F32) for i in range(4)]
                                for c in range(DC):
                                    ta, tb_ = tt[(2 * c) % 4], tt[(2 * c + 1) % 4]
                                    K.op("dve", lambda e, c=c, ta=ta: e.tensor_tensor(out=ta[:], in0=vT[:, c, vs], in1=rstd[:], op=ALU.mult),
                                         reads=[vT.b((c, t2)), rstd.b()], writes=[ta.b()])
                                    K.op("dve", lambda e, ta=ta, tb_=tb_: e.tensor_tensor(out=tb_[:], in0=ta[:], in1=nmr[:], op=ALU.add),
                                         reads=[ta.b(), nmr.b()], writes=[tb_.b()])
                                    K.op("act", lambda e, c=c, tb_=tb_: e.activation(out=sT[:, c, :], in_=tb_[:], func=AF.Silu, scale=cv[:, DC + c:DC + c + 1], bias=cv[:, 2 * DC + c:2 * DC + c + 1]),
                                         reads=[tb_.b(), cv.b()], writes=[sT.b(c)])
                                for oc in range(DC):
                                    pp = self.psum((0, 4))
                                    for kc in range(DC):
                                        K.op("pe", lambda e, pp=pp, oc=oc, kc=kc: e.matmul(pp[:], lhsT=w2[:, kc, oc * 128:(oc + 1) * 128], rhs=sT[:, kc, :], start=(kc == 0), stop=(kc == DC - 1)),
                                             reads=[w2.b(), sT.b(kc)], writes=[pp.b()], inc=(kc == DC - 1))
                                    if oc % 2 == 0:
                                        K.op("act", lambda e, pp=pp, oc=oc: e.activation(out=yb[:, oc, :], in_=pp[:], func=AF.Identity, bias=cv[:, 3 * DC + oc:3 * DC + oc + 1]),
                                             reads=[pp.b(), cv.b()], writes=[yb.b(oc)])
                                    else:
                                        K.op("dve", lambda e, pp=pp, oc=oc: e.tensor_scalar(out=yb[:, oc, :], in0=pp[:], scalar1=cv[:, 3 * DC + oc:3 * DC + oc + 1], scalar2=None, op0=ALU.add),
                                             reads=[pp.b(), cv.b()], writes=[yb.b(oc)])
                            with self.sbuf_scope() as S3:
                                self.postnorm_residual(self.norm_ws(S3, nset=1, pn=False), li, 0, tb, yb)

    def pool(self, pss, li):
        K, I = self.K, self.I
        wsem = [K.dsem("mw%d" % i) for i in range(2)]
        pv = self.sbv("pl_vecT", None)
        pf = self.sbv("poolfac", None)
        hh = self.hhalo
        hflag = self.flags[:, 1:2]
        if pss == 0:
            K.op("dve", lambda e: e.memset(hh[:], 0.0), writes=[hh.b()])
        else:
            K.op("dve", lambda e: e.tensor_scalar(out=hh[:], in0=hh[:], scalar1=hflag, scalar2=None, op0=ALU.mult),
                 reads=[hh.b(), self.flags.b()], writes=[hh.b()])
        with self.sbuf_scope() as S0:
            pw = S0.sb("poolw", [128, 8, 256], BF16)
            with self.sbuf_scope() as SG:
                stg = [SG.sb("pstg%d" % i, [128, 1024], F32) for i in range(2)]
                self.load_w_bf16(pw, I["pool_w"].rearrange("g (kc p) n -> p (g kc) n", p=128), stg, wsem, cache="poolw")
            WW = 16 + TB
            pws = self.norm_ws(S0, nset=2)
            for tb in range(NTB):
                with self.sbuf_scope() as S1:
                    hp = S1.sb("hp", [128, DC, WW], F32)
                    K.op("pool", lambda e: e.tensor_copy(out=hp[:, :, 0:16], in_=hh[:]), reads=[hh.b()], writes=[hp.b("h")])
                    self.prenorm(pws, li, 0, tb, _Vc(hp, 16), 0, 0)
                    K.op("pool", lambda e: e.tensor_copy(out=hh[:], in_=hp[:, :, TB:TB + 16]), reads=hp.bs([(0, c) for c in range(DC)]), writes=[hh.b()])
                    pT = S1.sb("ppT", [128, DC, TB], BF16)
                    yb = S1.sb("pyb", [128, DC, TB], F32)
                    sa = [S1.sb("psa%d" % i, [128, WW], F32) for i in range(4)]
                    for c in range(DC):
                        g = c // 2
                        eng = "dve"
                        bufs = sa[0:2] if c % 2 == 0 else sa[2:4]
                        cur_ap = lambda lo, hi, c=c: hp[:, c, lo:hi]
                        cur_b = [hp.b("h"), hp.b((0, c))]
                        lo = 0
                        for lvl in range(g + 1):
                            sh = 1 << lvl
                            dst = bufs[lvl % 2]
                            nlo = lo + sh
                            K.op(eng, lambda e, dst=dst, cur_ap=cur_ap, nlo=nlo, sh=sh: e.tensor_tensor(out=dst[:, nlo:WW], in0=cur_ap(nlo, WW), in1=cur_ap(nlo - sh, WW - sh), op=ALU.add),
                                 reads=cur_b, writes=[dst.b()])
                            cur_ap = (lambda lo_, hi_, dst=dst: dst[:, lo_:hi_])
                            cur_b = [dst.b()]
                            lo = nlo
                        if tb == 0:
                            K.op(eng, lambda e, cur_ap=cur_ap: e.tensor_tensor(out=cur_ap(16, 32), in0=cur_ap(16, 32), in1=pf[:, pss, g, :], op=ALU.mult),
                                 reads=cur_b + [pf.b()], writes=cur_b)
                        K.op("dve", lambda e, cur_ap=cur_ap, c=c, g=g: e.scalar_tensor_tensor(out=pT[:, c, :], in0=cur_ap(16, WW), scalar=1.0 / PWIN[g], in1=hp[:, c, 16:WW],
                                                                                               op0=ALU.mult, op1=ALU.subtract),
                             reads=cur_b + [hp.b((0, c))], writes=[pT.b(c)])
                    for g in range(4):
                        for o2 in range(2):
                            oc = 2 * g + o2
                            pp = self.psum((0, 4))
                            for k2 in range(2):
                                K.op("pe", lambda e, pp=pp, g=g, o2=o2, k2=k2: e.matmul(pp[:], lhsT=pw[:, 2 * g + k2, o2 * 128:(o2 + 1) * 128], rhs=pT[:, 2 * g + k2, :],
                                                                                          start=(k2 == 0), stop=(k2 == 1)),
                                     reads=[pw.b(), pT.b(2 * g + k2)], writes=[pp.b()], inc=(k2 == 1))
                            K.op("dve", lambda e, pp=pp, oc=oc: e.tensor_scalar(out=yb[:, oc, :], in0=pp[:], scalar1=pv[:, oc:oc + 1], scalar2=pv[:, DC + oc:DC + oc + 1],
                                                                                 op0=ALU.add, op1=ALU.mult),
                                 reads=[pp.b(), pv.b()], writes=[yb.b(oc)])
                    self.postnorm_residual(pws, li, 0, tb, yb)


class _Vc:
    def __init__(self, tl, off):
        self.tl, self.off = tl, off

    def __getitem__(self, idx):
        p, c, n = idx
        return self.tl.t[p, c, n.start + self.off:n.stop + self.off]

    def b(self, key=0):
        return self.tl.b(key)


class _Vn:
    def __init__(self, tl, lo, hi):
        self.tl, self.lo, self.hi = tl, lo, hi

    def __getitem__(self, idx):
        p, a, n = idx
        return self.tl.t[p, a, self.lo:self.hi]

    def b(self, key=0):
        return self.tl.b(key)


class _Scope:
    def __init__(self, prog):
        self.p = prog
        self.cms = []

    def __enter__(self):
        return self

    def sb(self, name, shape, dt=F32):
        self.p.uid += 1
        nm = "%s_%d" % (name, self.p.uid)
        cm = self.p.nc.sbuf_tensor(nm, list(shape), dt)
        t = cm.__enter__()
        self.cms.append(cm)
        return Tl(t, nm)

    def __exit__(self, *a):
        self.p.K.barrier()
        for cm in reversed(self.cms):
            cm.__exit__(None, None, None)
        return False


def _t128(v, n):
    return np.ascontiguousarray(np.asarray(v, np.float32).reshape(n, 128).T)


def make_in_maps(inp):
    x = np.asarray(inp["x"], np.float32)
    B = x.shape[0]
    pos = np.asarray(inp["positions"]).astype(np.int32)
    shared = {}
    shared["ada_wT"] = np.ascontiguousarray(np.asarray(inp["ada_w"], np.float32).transpose(0, 2, 1))
    shared["ada_bT"] = np.concatenate([_t128(inp["ada_b"][i], 48) for i in range(DEPTH)], axis=1)
    shared["norm_gT"] = np.concatenate([_t128(inp["norm_g"][i, j], DC) for i in range(DEPTH) for j in range(4)], axis=1)
    shared["w_dq"] = np.ascontiguousarray(inp["mla_w_dq"], np.float32)
    shared["qn_gT"] = np.concatenate([_t128(inp["mla_q_norm_g"][j], 3) for j in range(2)], axis=1)
    wuq = np.asarray(inp["mla_w_uq"], np.float32).reshape(2, QL, H, 96)
    shared["w_uqA"] = np.ascontiguousarray(wuq.reshape(2, QL, H * 96))
    shared["w_uqB"] = np.ascontiguousarray(np.concatenate([wuq[..., 80:96], wuq[..., 64:80]], axis=-1).reshape(2, QL, H * 32))
    wdkv = np.asarray(inp["mla_w_dkv"], np.float32)
    shared["w_dkvA"] = np.ascontiguousarray(wdkv)
    shared["w_dkvB"] = np.ascontiguousarray(np.concatenate([wdkv[..., 272:288], wdkv[..., 256:272]], axis=-1))
    shared["kvn_gT"] = np.concatenate([_t128(inp["mla_kv_norm_g"][j], 2) for j in range(2)], axis=1)
    wukv = np.asarray(inp["mla_w_ukv"], np.float32).reshape(2, KVL, H, 128)
    shared["w_ukvK"] = np.ascontiguousarray(wukv[..., 0:64].reshape(2, KVL, H * 64))
    shared["w_ukvV"] = np.ascontiguousarray(wukv[..., 64:128].reshape(2, KVL, H * 64))
    shared["w_o"] = np.ascontiguousarray(inp["mla_w_o"], np.float32)
    shared["w_pw1"] = np.ascontiguousarray(inp["conv_w_pw1"][0], np.float32)
    shared["b_pw1T"] = _t128(inp["conv_b_pw1"][0], 16)
    shared["w_dwT"] = np.ascontiguousarray(np.asarray(inp["conv_w_dw"][0], np.float32).T.reshape(DC, 128, CW).transpose(1, 0, 2))
    shared["cv_vecT"] = np.concatenate([_t128(inp[k][0], DC) for k in ("conv_b_dw", "conv_ln_g", "conv_ln_b", "conv_b_pw2")], axis=1)
    shared["w_pw2"] = np.ascontiguousarray(inp["conv_w_pw2"][0], np.float32)
    shared["pool_w"] = np.ascontiguousarray(inp["pool_w"][0], np.float32)
    shared["pl_vecT"] = np.concatenate([_t128(np.asarray(inp["pool_b"][0]).reshape(-1), DC), _t128(inp["pool_scale"][0], DC)], axis=1)
    shared["ffn_w1"] = np.ascontiguousarray(inp["ffn_w1"], np.float32)
    shared["ffn_w2"] = np.ascontiguousarray(inp["ffn_w2"], np.float32)
    consts = np.zeros((128, 136), np.float32)
    consts[:, :128] = np.eye(128, dtype=np.float32)
    invf = (10000.0 ** (-np.arange(0, 32, 2, dtype=np.float32) / np.float32(32.0))).astype(np.float32)
    consts[0:16, 128] = invf
    consts[16:32, 128] = invf
    consts[0:16, 129] = -1.0
    consts[16:32, 129] = 1.0
    shared["consts"] = consts
    fac_start = np.ones((4, 16), np.float32)
    for g, w in enumerate(PWIN):
        for t in range(16):
            fac_start[g, t] = float(w) / float(min(t + 1, w))
    maps = []
    for core in range(2 * B):
        b, half = core // 2, core % 2
        m = dict(shared)
        own = x[b, half * T:(half + 1) * T]
        m["xT_own"] = np.ascontiguousarray(own.T)
        m["pos_own"] = np.ascontiguousarray(np.broadcast_to(pos[b, half * T:(half + 1) * T].reshape(1, T), (32, T)))
        if half == 1:
            m["xT_prev"] = np.ascontiguousarray(x[b, 0:T].T)
            m["pos_prev"] = np.ascontiguousarray(np.broadcast_to(pos[b, 0:T].reshape(1, T), (32, T)))
        else:
            m["xT_prev"] = np.zeros((D, T), np.float32)
            m["pos_prev"] = np.zeros((32, T), np.int32)
        m["c"] = _t128(inp["c"][b], DC)
        m["c_rep"] = np.ascontiguousarray(np.broadcast_to(np.asarray(inp["c"][b], np.float32).reshape(1, D), (128, D)))
        fl = np.zeros((128, 4), np.float32)
        fl[:, 0] = 0.0 if half == 1 else NEG
        fl[:, 1] = 1.0 if half == 1 else 0.0
        m["flags"] = fl
        pf = np.ones((2, 4, 16), np.float32)
        pf[0] = fac_start
        if half == 0:
            pf[1] = fac_start
        m["poolfac"] = np.ascontiguousarray(np.broadcast_to(pf[None], (128, 2, 4, 16)))
        maps.append(m)
    return maps


_PROG = {}


def get_prog(**kw):
    key = tuple(sorted((k, str(v)) for k, v in kw.items()))
    if key not in _PROG:
        _PROG[key] = Prog(**kw)
    return _PROG[key]


def kernel(**inputs):
    x = np.asarray(inputs["x"])
    B, S, _ = x.shape
    prog = get_prog()
    maps = make_in_maps(inputs)
    res = run_bass_kernel_spmd(prog.nc, maps, core_ids=list(range(8)))
    out = np.empty((B, S, D), np.float32)
    for core in range(8):
        b, half = core // 2, core % 2
        out[b, half * T:(half + 1) * T] = np.asarray(res.results[core]["outT"]).T
    return out
```

```python
import numpy as np
import ml_dtypes
import concourse.bass as bass
import concourse.mybir as mybir
from concourse.bass_utils import run_bass_kernel_spmd

F32 = mybir.dt.float32
BF16 = mybir.dt.bfloat16
I32 = mybir.dt.int32
AF = mybir.ActivationFunctionType
ALU = mybir.AluOpType
AX = mybir.AxisListType

D = 1024
DC = 8
T = 2048
TB = 512
NTB = 4
DEPTH = 4
H = 16
DFF = 4096
EPS = 1e-6
QL = 384
KVL = 256
NEG = -30000.0
SCALE = 1.0 / float(np.sqrt(96.0))
CW = 31
PWIN = (2, 4, 8, 16)


class Buf:
    __slots__ = ("name", "w", "r")

    def __init__(self, name):
        self.name = name
        self.w = None
        self.r = {}


class KB:
    def __init__(self, nc):
        self.nc = nc
        self.eng = {"pe": nc.tensor, "act": nc.scalar, "dve": nc.vector,
                    "pool": nc.gpsimd, "sp": nc.sync}
        self.sems = {}
        self.cnt = {}
        for e in self.eng:
            self.sems[e] = nc.alloc_semaphore("c_" + e)
            self.cnt[e] = 0
        self.waited = {e: {} for e in self.eng}
        self.pending = {e: [] for e in self.eng}
        self.ndma = 0
        self.nsem = 0
        self.groups = {}

    def dsem(self, name):
        key = "d_" + name
        if key not in self.sems:
            self.sems[key] = self.nc.alloc_semaphore(key)
            self.cnt[key] = 0
        return key

    def _need(self, eng, evs):
        for k, v in evs.items():
            if self.waited[eng].get(k, 0) >= v:
                continue
            self.eng[eng].wait_ge(self.sems[k], v)
            self.waited[eng][k] = v

    def _deps(self, eng, reads, writes):
        evs = {}

        def add(ev, raw):
            if ev is None:
                return
            k, v = ev
            if k == "PENDING":
                assert v == eng and (eng == "pe" or not raw), "dependency on un-flushed op"
                return
            if k == eng:
                if eng in ("pe", "sp"):
                    return
                if not raw:
                    return
            if evs.get(k, 0) < v:
                evs[k] = v
        for b in reads:
            add(b.w, True)
        for b in writes:
            add(b.w, False)
            for k, v in b.r.items():
                add((k, v), False)
        return evs

    def _commit(self, ev, reads, writes):
        for b in reads:
            if b.r.get(ev[0], 0) < ev[1]:
                b.r[ev[0]] = ev[1]
        for b in writes:
            b.w = ev
            b.r = {}

    def op(self, eng, fn, reads=(), writes=(), inc=True):
        reads = list(reads)
        writes = list(writes)
        self._need(eng, self._deps(eng, reads, writes))
        ins = fn(self.eng[eng])
        if inc:
            self.cnt[eng] += 1
            ins.then_inc(self.sems[eng], 1)
            ev = (eng, self.cnt[eng])
            for (r, w) in self.pending[eng]:
                self._commit(ev, r, w)
            self.pending[eng] = []
            self._commit(ev, reads, writes)
        else:
            for b in writes:
                b.w = ("PENDING", eng)
            self.pending[eng].append((reads, writes))
        return ins

    def dma(self, sem, out, in_, reads=(), writes=(), q="sp", hold=False, **kw):
        reads = list(reads)
        writes = list(writes)
        grp = self.groups.setdefault(sem, [])
        if not grp and self.cnt[sem] > 0:
            self._need(q, {sem: self.cnt[sem]})
        self._need(q, self._deps(q, reads, writes))
        self.eng[q].dma_start(out=out, in_=in_, **kw).then_inc(self.sems[sem], 16)
        self.cnt[sem] += 16
        self.ndma += 1
        for b in writes:
            b.w = ("PENDING", "dma")
        grp.append((reads, writes))
        if not hold:
            self.dma_flush(sem)

    def dma_flush(self, sem):
        ev = (sem, self.cnt[sem])
        for (r, w) in self.groups.get(sem, []):
            self._commit(ev, r, w)
        self.groups[sem] = []

    def barrier(self):
        for e in self.eng:
            assert not self.pending[e]
        assert not any(self.groups.values()), "open dma group at barrier"
        evs = {k: v for k, v in self.cnt.items() if v > 0}
        for e in self.eng:
            self._need(e, {k: v for k, v in evs.items() if not (k == e and e in ("pe", "sp"))})

    def final_wait(self):
        evs = {k: v for k, v in self.cnt.items() if v > 0 and k != "sp"}
        self._need("sp", evs)


class Tl:
    def __init__(self, t, name):
        self.t = t
        self.name = name
        self.bufs = {}

    def b(self, key=0):
        if key not in self.bufs:
            self.bufs[key] = Buf("%s.%s" % (self.name, key))
        return self.bufs[key]

    def bs(self, keys):
        return [self.b(k) for k in keys]

    def __getitem__(self, idx):
        return self.t[idx]


def _bf16(a):
    return np.asarray(a, dtype=np.float32).astype(ml_dtypes.bfloat16)


class Prog:
    def __init__(self, layers_p1=(0, 1, 2), layers_p2=(0, 1, 2, 3), debug=None, mixers=True, ffn=True, warm=False):
        self.warm = warm
        self.do_mixers = mixers
        self.do_ffn = ffn
        self.layers_p1 = tuple(layers_p1)
        self.layers_p2 = tuple(layers_p2)
        self.debug = debug
        self.nc = bass.Bass("TRN2", target_bir_lowering=False)
        self.K = KB(self.nc)
        self.uid = 0
        self.build()

    def dram_in(self, name, shape, dt=F32):
        return self.nc.dram_tensor(name, list(shape), dt, kind="ExternalInput").ap()

    def dram_out(self, name, shape, dt=F32):
        return self.nc.dram_tensor(name, list(shape), dt, kind="ExternalOutput").ap()

    def dram_tmp(self, name, shape, dt):
        return self.nc.dram_tensor(name, list(shape), dt).ap()

    def sb(self, name, shape, dt=F32):
        self.uid += 1
        nm = "%s_%d" % (name, self.uid)
        return Tl(self.nc.alloc_sbuf_tensor(nm, list(shape), dt), nm)

    def sbuf_scope(self):
        return _Scope(self)

    def build(self):
        nc, K = self.nc, self.K
        I = {}
        I["xT_prev"] = self.dram_in("xT_prev", [D, T])
        I["xT_own"] = self.dram_in("xT_own", [D, T])
        I["pos_prev"] = self.dram_in("pos_prev", [32, T], I32)
        I["pos_own"] = self.dram_in("pos_own", [32, T], I32)
        I["c"] = self.dram_in("c", [128, DC])
        I["flags"] = self.dram_in("flags", [128, 4])
        I["poolfac"] = self.dram_in("poolfac", [128, 2, 4, 16])
        I["consts"] = self.dram_in("consts", [128, 128 + 8])
        I["ada_wT"] = self.dram_in("ada_wT", [DEPTH, 6 * D, D])
        I["c_rep"] = self.dram_in("c_rep", [128, D])
        I["ada_bT"] = self.dram_in("ada_bT", [128, DEPTH * 48])
        I["norm_gT"] = self.dram_in("norm_gT", [128, DEPTH * 4 * DC])
        I["w_dq"] = self.dram_in("w_dq", [2, D, QL])
        I["qn_gT"] = self.dram_in("qn_gT", [128, 2 * 3])
        I["w_uqA"] = self.dram_in("w_uqA", [2, QL, H * 96])
        I["w_uqB"] = self.dram_in("w_uqB", [2, QL, H * 32])
        I["w_dkvA"] = self.dram_in("w_dkvA", [2, D, 288])
        I["w_dkvB"] = self.dram_in("w_dkvB", [2, D, 32])
        I["kvn_gT"] = self.dram_in("kvn_gT", [128, 2 * 2])
        I["w_ukvK"] = self.dram_in("w_ukvK", [2, KVL, H * 64])
        I["w_ukvV"] = self.dram_in("w_ukvV", [2, KVL, H * 64])
        I["w_o"] = self.dram_in("w_o", [2, D, D])
        I["w_pw1"] = self.dram_in("w_pw1", [D, 2 * D])
        I["b_pw1T"] = self.dram_in("b_pw1T", [128, 16])
        I["w_dwT"] = self.dram_in("w_dwT", [128, DC, CW])
        I["cv_vecT"] = self.dram_in("cv_vecT", [128, 4 * DC])
        I["w_pw2"] = self.dram_in("w_pw2", [D, D])
        I["pool_w"] = self.dram_in("pool_w", [4, 256, 256])
        I["pl_vecT"] = self.dram_in("pl_vecT", [128, 2 * DC])
        I["ffn_w1"] = self.dram_in("ffn_w1", [DEPTH, D, DFF])
        I["ffn_w2"] = self.dram_in("ffn_w2", [DEPTH, DFF, D])
        self.I = I
        self.outT = self.dram_out("outT", [D, T])
        if self.debug:
            self.dbg = self.dram_out("dbg", [D, T])
        self.kcache = self.dram_tmp("kcache", [2, H, 96, 2 * T], BF16)
        self.vcache = self.dram_tmp("vcache", [2, H, 128, 32, 128], BF16)
        self.halo_u = self.dram_tmp("halo_u", [128, DC, 32], BF16)
        self.halo_h = self.dram_tmp("halo_h", [128, DC, 16], F32)
        self.w1s = self.dram_tmp("w1s", [DEPTH, 16, 128, DC * 256], BF16)
        self.w2s = self.dram_tmp("w2s", [DEPTH, DC, 128, 32 * 128], BF16)
        self.ffn_cached = [False] * DEPTH

        self.xT = self.sb("xT", [128, DC, T], F32)
        self.modT = self.sb("modT", [128, DEPTH * 48], F32)
        self.vec = self.sb("vec", [128, DEPTH * 6 * DC], F32)
        self.ngT = self.sb("ngT", [128, DEPTH * 4 * DC], F32)
        self.abT = self.sb("abT", [128, DEPTH * 48], F32)
        self.ident = self.sb("ident", [128, 128 + 8], F32)
        self.identb = self.sb("identb", [128, 128], BF16)
        self.onesb = self.sb("onesb", [128, 128], BF16)
        self.onesb_w = self.sb("onesb_w", [128, TB], BF16)
        self.flags = self.sb("flags", [128, 4], F32)
        self.kmax2 = self.sb("kmax2", [128, 2 * 2 * H], F32)
        self.epsc = self.sb("epsc", [128, 4], F32)
        self.uhalo = self.sb("uhalo", [128, DC, 32], BF16)
        self.hhalo = self.sb("hhalo", [128, DC, 16], F32)
        self.ps = [Tl(nc.alloc_psum_tensor("ps%d" % i, [128, 512], F32), "ps%d" % i) for i in range(8)]
        self.ps_rr = 0

        s_misc = K.dsem("misc")
        K.dma(s_misc, self.ident[:], I["consts"], writes=[self.ident.b()], hold=True)
        K.dma(s_misc, self.flags[:], I["flags"], writes=[self.flags.b()], hold=True)
        K.dma(s_misc, self.ngT[:], I["norm_gT"], writes=[self.ngT.b()], hold=True)
        K.dma(s_misc, self.abT[:], I["ada_bT"], writes=[self.abT.b()], hold=True)
        for nm_, shp_ in (("qn_gT", [128, 6]), ("kvn_gT", [128, 4]), ("b_pw1T", [128, 16]), ("cv_vecT", [128, 4 * DC]),
                          ("pl_vecT", [128, 2 * DC]), ("w_dwT", [128, DC, CW]), ("poolfac", [128, 2, 4, 16])):
            self.sbv(nm_, shp_)
        K.dma_flush(s_misc)
        K.op("dve", lambda e: e.tensor_copy(out=self.identb[:], in_=self.ident[:, 0:128]),
             reads=[self.ident.b()], writes=[self.identb.b()])
        K.op("dve", lambda e: e.memset(self.onesb[:], 1.0), writes=[self.onesb.b()])
        K.op("dve", lambda e: e.memset(self.onesb_w[:], 1.0), writes=[self.onesb_w.b()])
        K.op("dve", lambda e: e.memset(self.epsc[:, 0:1], EPS), writes=[self.epsc.b()])
        K.op("dve", lambda e: e.memset(self.kmax2[:], 0.0), writes=[self.kmax2.b()])

        self.prologue_mod()

        for pss, layers in ((0, self.layers_p1), (1, self.layers_p2)):
            if not layers:
                continue
            self.load_x(pss)
            for li in layers:
                self.layer(pss, li)
            if pss == 0 and 3 in self.layers_p2 and self.do_mixers:
                self.mla(0, 3, kv_only=True)
        self.store_out()
        K.final_wait()

    def psum(self, pool):
        lo, n = pool
        if not isinstance(self.ps_rr, dict):
            self.ps_rr = {}
        r = self.ps_rr.get(pool, 0)
        self.ps_rr[pool] = r + 1
        return self.ps[lo + (r % n)]

    def ada_gen(self, S, layers, nbuf=3):
        K, I = self.K, self.I
        crep = S.sb("crep", [128, D], F32)
        junk = S.sb("adajunk", [128, D], F32)
        stg = [S.sb("adastg%d" % i, [128, D], F32) for i in range(nbuf)]
        ss = [K.dsem("ada%d" % i) for i in range(nbuf)]
        K.dma(K.dsem("pm"), crep[:], I["c_rep"], writes=[crep.b()])
        K.op("act", lambda e: e.activation(out=crep[:], in_=crep[:], func=AF.Silu), reads=[crep.b()], writes=[crep.b()])
        n = 0
        for li in layers:
            wv = I["ada_wT"][li].rearrange("(j p) k -> j p k", p=128)
            for jc in range(48):
                st = stg[n % nbuf]
                col = li * 48 + jc
                K.dma(ss[n % nbuf], st[:], wv[jc], writes=[st.b()])
                K.op("dve", lambda e, st=st, col=col: e.scalar_tensor_tensor(out=junk[:], in0=st[:], scalar=1.0, in1=crep[:], op0=ALU.mult, op1=ALU.mult,
                                                                             accum_out=self.modT[:, col:col + 1]),
                     reads=[st.b(), crep.b()], writes=[junk.b(), self.modT.b()])
                n += 1
                yield
            c0 = li * 48
            K.op("dve", lambda e: e.tensor_tensor(out=self.modT[:, c0:c0 + 48], in0=self.modT[:, c0:c0 + 48], in1=self.abT[:, c0:c0 + 48], op=ALU.add),
                 reads=[self.modT.b(), self.abT.b()], writes=[self.modT.b()])
            m = lambda j: self.modT[:, li * 48 + j * DC: li * 48 + (j + 1) * DC]
            g = lambda j: self.ngT[:, (li * 4 + j) * DC:(li * 4 + j + 1) * DC]
            v = lambda j: self.vec[:, (li * 6 + j) * DC:(li * 6 + j + 1) * DC]
            rd = [self.modT.b(), self.ngT.b()]
            wr = [self.vec.b()]
            K.op("dve", lambda e: e.scalar_tensor_tensor(out=v(0), in0=m(1), scalar=1.0, in1=g(0), op0=ALU.add, op1=ALU.mult), reads=rd, writes=wr)
            K.op("dve", lambda e: e.tensor_copy(out=v(1), in_=m(0)), reads=rd, writes=wr)
            K.op("dve", lambda e: e.tensor_tensor(out=v(2), in0=m(2), in1=g(1), op=ALU.mult), reads=rd, writes=wr)
            K.op("dve", lambda e: e.scalar_tensor_tensor(out=v(3), in0=m(4), scalar=1.0, in1=g(2), op0=ALU.add, op1=ALU.mult), reads=rd, writes=wr)
            K.op("dve", lambda e: e.tensor_copy(out=v(4), in_=m(3)), reads=rd, writes=wr)
            K.op("dve", lambda e: e.tensor_tensor(out=v(5), in0=m(5), in1=g(3), op=ALU.mult), reads=rd, writes=wr)
            yield

    def prologue_mod(self):
        first = (self.layers_p1 + self.layers_p2)[0]
        rest = [l for l in sorted(set(self.layers_p1 + self.layers_p2)) if l != first]
        with self.sbuf_scope() as S:
            for _ in self.ada_gen(S, [first], nbuf=4):
                pass
        self.ada_rest = rest
        self.ada_lazy_ok = self.do_mixers and first % 3 == 0 and bool(rest)
        if rest and not self.ada_lazy_ok:
            with self.sbuf_scope() as S:
                for _ in self.ada_gen(S, rest, nbuf=4):
                    pass
            self.ada_rest = []

    def lvec(self, li, j, c):
        o = (li * 6 + j) * DC + c
        return self.vec[:, o:o + 1]

    def xb(self, tb):
        return self.xT.b(tb)

    def load_x(self, pss):
        K = self.K
        src = self.I["xT_prev" if pss == 0 else "xT_own"].rearrange("(c p) t -> p c t", p=128)
        sem = K.dsem("ldx")
        for tb in range(NTB):
            K.dma(sem, self.xT[:, :, tb * TB:(tb + 1) * TB], src[:, :, tb * TB:(tb + 1) * TB], writes=[self.xb(tb)], hold=(tb < NTB - 1))

    def store_out(self):
        K = self.K
        dst = self.outT.rearrange("(c p) t -> p c t", p=128)
        sem = K.dsem("stx")
        for tb in range(NTB):
            K.dma(sem, dst[:, :, tb * TB:(tb + 1) * TB], self.xT[:, :, tb * TB:(tb + 1) * TB], reads=[self.xb(tb)], hold=(tb < NTB - 1))

    def sq_ops(self, sq, c, src_ap, src_bufs, eng):
        K = self.K
        if eng == "act":
            K.op("act", lambda e: e.activation(out=sq[:, c, :], in_=src_ap, func=AF.Square),
                 reads=src_bufs, writes=[sq.b(c)])
        else:
            K.op(eng, lambda e: e.tensor_tensor(out=sq[:, c, :], in0=src_ap, in1=src_ap, op=ALU.mult),
                 reads=src_bufs, writes=[sq.b(c)])

    def norm_ws(self, S, nset=1, pn=True, width=TB):
        ws = {"sets": [], "i": 0, "pn": [], "j": 0}
        for k in range(nset):
            ws["sets"].append((S.sb("ws_sq%d" % k, [128, DC, width], BF16), S.sb("ws_ln%d" % k, [128, width], F32),
                               S.sb("ws_rs%d" % k, [128, width], F32)))
        if pn:
            ws["pn"] = [S.sb("ws_pn%d" % k, [128, width], F32) for k in range(3)]
        return ws

    def ws_set(self, ws):
        st = ws["sets"][ws["i"] % len(ws["sets"])]
        ws["i"] += 1
        return st

    def rstd_from_sq(self, ws_set, sq, nchunk, n, width, kparts=128):
        K = self.K
        _, tmp, rstd = ws_set
        pp = self.psum((6, 2))
        for c in range(nchunk):
            K.op("pe", lambda e, c=c: e.matmul(pp[:, 0:width], lhsT=self.onesb[0:kparts, :], rhs=sq[0:kparts, c, 0:width],
                                                 start=(c == 0), stop=(c == nchunk - 1)),
                 reads=[sq.b(c), self.onesb.b()], writes=[pp.b()], inc=(c == nchunk - 1))
        K.op("act", lambda e: e.activation(out=tmp[:, 0:width], in_=pp[:, 0:width], func=AF.Ln, bias=self.epsc[:, 0:1], scale=1.0 / n),
             reads=[pp.b(), self.epsc.b()], writes=[tmp.b()])
        K.op("act", lambda e: e.activation(out=rstd[:, 0:width], in_=tmp[:, 0:width], func=AF.Exp, scale=-0.5),
             reads=[tmp.b()], writes=[rstd.b()])
        return rstd

    def prenorm(self, ws, li, sub, tb, hT, hcol0, hkey):
        K = self.K
        st = self.ws_set(ws)
        sq = st[0]
        xs = slice(tb * TB, (tb + 1) * TB)
        for c in range(DC):
            self.sq_ops(sq, c, self.xT[:, c, xs], [self.xb(tb)], "dve" if c % 4 == 3 else "act")
        rstd = self.rstd_from_sq(st, sq, DC, D, TB)
        ja, jb = (0, 1) if sub == 0 else (3, 4)
        for c in range(DC):
            tmp = ws["pn"][ws["j"] % 3]
            ws["j"] += 1
            K.op("dve", lambda e, c=c, tmp=tmp: e.tensor_tensor(out=tmp[:], in0=self.xT[:, c, xs], in1=rstd[:], op=ALU.mult),
                 reads=[self.xb(tb), rstd.b()], writes=[tmp.b()])
            if c % 4 != 0:
                K.op("act", lambda e, c=c, tmp=tmp: e.activation(out=hT[:, c, hcol0:hcol0 + TB], in_=tmp[:], func=AF.Identity,
                                                                  bias=self.lvec(li, jb, c), scale=self.lvec(li, ja, c)),
                     reads=[tmp.b(), self.vec.b()], writes=[hT.b((hkey, c))])
            else:
                K.op("dve", lambda e, c=c, tmp=tmp: e.tensor_scalar(out=hT[:, c, hcol0:hcol0 + TB], in0=tmp[:],
                                                                     scalar1=self.lvec(li, ja, c), scalar2=self.lvec(li, jb, c),
                                                                     op0=ALU.mult, op1=ALU.add),
                     reads=[tmp.b(), self.vec.b()], writes=[hT.b((hkey, c))])

    def postnorm_residual(self, ws, li, sub, tb, yb):
        K = self.K
        st = self.ws_set(ws)
        sq = st[0]
        for c in range(DC):
            self.sq_ops(sq, c, yb[:, c, :], [yb.b(c)], "act")
        rstd = self.rstd_from_sq(st, sq, DC, D, TB)
        xs = slice(tb * TB, (tb + 1) * TB)
        jg = 2 if sub == 0 else 5
        for c in range(DC):
            K.op("dve", lambda e, c=c: e.tensor_tensor(out=yb[:, c, :], in0=yb[:, c, :], in1=rstd[:], op=ALU.mult),
                 reads=[yb.b(c), rstd.b()], writes=[yb.b(c)])
            K.op("dve", lambda e, c=c: e.scalar_tensor_tensor(out=self.xT[:, c, xs], in0=yb[:, c, :], scalar=self.lvec(li, jg, c),
                                                              in1=self.xT[:, c, xs], op0=ALU.mult, op1=ALU.add),
                 reads=[yb.b(c), self.vec.b(), self.xb(tb)], writes=[self.xb(tb)])

    def cast(self, out_ap, in_ap, rbufs, wbufs):
        K = self.K
        n = getattr(self, "_ncast", 0)
        self._ncast = n + 1
        if n % 2 == 0:
            K.op("dve", lambda e: e.tensor_copy(out=out_ap, in_=in_ap), reads=rbufs, writes=wbufs)
        else:
            K.op("act", lambda e: e.activation(out=out_ap, in_=in_ap, func=AF.Identity), reads=rbufs, writes=wbufs)

    def evac(self, eng, out_ap, pp, wbufs):
        K = self.K
        if eng == "act":
            K.op("act", lambda e: e.activation(out=out_ap, in_=pp[:], func=AF.Identity), reads=[pp.b()], writes=wbufs)
        else:
            K.op("dve", lambda e: e.tensor_copy(out=out_ap, in_=pp[:]), reads=[pp.b()], writes=wbufs)

    def ffn(self, pss, li):
        K, I = self.K, self.I
        w1v = I["ffn_w1"][li].rearrange("(kc p) n -> p kc n", p=128)
        w2v = I["ffn_w2"][li].rearrange("(hc p) n -> p hc n", p=128)
        ssem = [K.dsem("fw%d" % i) for i in range(2)]
        stsem = [K.dsem("fws%d" % i) for i in range(2)]
        for sbk in range(2):
            first = not self.ffn_cached[li]
            self.ffn_cached[li] = True
            with self.sbuf_scope() as S:
                hid = S.sb("hid", [128, 32, 2 * TB], BF16)
                with self.sbuf_scope() as S1:
                    hT = S1.sb("hT", [128, DC, 2 * TB], BF16)
                    nwb = 2 if first else 4
                    stg = [S1.sb("fstg%d" % i, [128, DC, 256], F32) for i in range(2)] if first else None
                    ws = self.norm_ws(S1, nset=1 if first else 2)
                    w1b = [S1.sb("w1b%d" % i, [128, DC, 256], BF16) for i in range(nwb)]
                    lsem4 = [K.dsem("fwl%d" % i) for i in range(4)]
                    rl = [S1.sb("rl%d" % i, [128, TB], BF16) for i in range(3)]
                    for t2 in range(2):
                        self.prenorm(ws, li, 1, sbk * 2 + t2, hT, t2 * TB, t2)
                    nrl = 0
                    for pc in range(16):
                        wb = w1b[pc % nwb]
                        st = stg[pc % 2] if first else None
                        cb = self._cb("w1s", li, pc, 0)
                        if first:
                            K.dma(ssem[pc % 2], st[:], w1v[:, :, pc * 256:(pc + 1) * 256], writes=[st.b()])
                            self.cast(wb[:], st[:], [st.b()], [wb.b()])
                            K.dma(stsem[pc % 2], self.w1s[li, pc].rearrange("p (k n) -> p k n", k=DC), wb[:], reads=[wb.b()], writes=[cb])
                        else:
                            K.dma(lsem4[pc % 4], wb[:], self.w1s[li, pc].rearrange("p (k n) -> p k n", k=DC), reads=[cb], writes=[wb.b()])
                        for hc2 in range(2):
                            hc = pc * 2 + hc2
                            for t2 in range(2):
                                pp = self.psum((0, 4))
                                for kc in range(DC):
                                    K.op("pe", lambda e, wb=wb, kc=kc, hc2=hc2, t2=t2, pp=pp: e.matmul(
                                        pp[:], lhsT=wb[:, kc, hc2 * 128:(hc2 + 1) * 128], rhs=hT[:, kc, t2 * TB:(t2 + 1) * TB],
                                        start=(kc == 0), stop=(kc == DC - 1)),
                                        reads=[wb.b(), hT.b((t2, kc))], writes=[pp.b()], inc=(kc == DC - 1))
                                r = rl[nrl % 3]
                                nrl += 1
                                K.op("act", lambda e, pp=pp, r=r: e.activation(out=r[:], in_=pp[:], func=AF.Relu),
                                     reads=[pp.b()], writes=[r.b()])
                                K.op("dve", lambda e, r=r, hc=hc, t2=t2: e.tensor_tensor(out=hid[:, hc, t2 * TB:(t2 + 1) * TB], in0=r[:], in1=r[:], op=ALU.mult),
                                     reads=[r.b()], writes=[hid.b((hc, t2))])
                with self.sbuf_scope() as S1:
                    stg = [S1.sb("gstg%d" % i, [128, 8, 128], F32) for i in range(2)] if first else None
                    w2b = [S1.sb("w2b%d" % i, [128, 32, 128], BF16) for i in range(2)]
                    yb = [S1.sb("yb%d" % i, [128, DC, TB], F32) for i in range(2)]
                    ws = self.norm_ws(S1, nset=1, pn=False)
                    npc = 0
                    for oc in range(DC):
                        wb = w2b[oc % 2]
                        cb = self._cb("w2s", li, oc, 0)
                        if first:
                            for hh in range(4):
                                st = stg[npc % 2]
                                K.dma(ssem[npc % 2], st[:], w2v[:, hh * 8:(hh + 1) * 8, oc * 128:(oc + 1) * 128], writes=[st.b()])
                                self.cast(wb[:, hh * 8:(hh + 1) * 8, :], st[:], [st.b()], [wb.b(hh)])
                                npc += 1
                            K.dma(stsem[oc % 2], self.w2s[li, oc].rearrange("p (k n) -> p k n", k=32), wb[:], reads=wb.bs(range(4)), writes=[cb])
                        else:
                            K.dma(ssem[oc % 2], wb[:], self.w2s[li, oc].rearrange("p (k n) -> p k n", k=32), reads=[cb], writes=wb.bs(range(4)))
                        for t2 in range(2):
                            pp = self.psum((0, 4))
                            for hc in range(32):
                                K.op("pe", lambda e, wb=wb, hc=hc, t2=t2, pp=pp: e.matmul(
                                    pp[:], lhsT=wb[:, hc, :], rhs=hid[:, hc, t2 * TB:(t2 + 1) * TB],
                                    start=(hc == 0), stop=(hc == 31)),
                                    reads=[wb.b(hc // 8), hid.b((hc, t2))], writes=[pp.b()], inc=(hc == 31))
                            self.evac("act" if (oc + t2) % 2 == 0 else "dve", yb[t2][:, oc, :], pp, [yb[t2].b(oc)])
                    for t2 in range(2):
                        self.postnorm_residual(ws, li, 1, sbk * 2 + t2, yb[t2])

    def layer(self, pss, li):
        kind = li % 3
        if self.do_mixers:
            if kind == 0:
                self.mla(pss, li)
            elif kind == 1:
                self.conv(pss, li)
            else:
                self.pool(pss, li)
        if self.do_ffn:
            self.ffn(pss, li)

    def load_w_bf16(self, dst, src3, stg, sems, key=0, cache=None):
        K = self.K
        A, N = src3.shape[1], src3.shape[2]
        if cache is not None:
            if not hasattr(self, "_wc"):
                self._wc = {}
            if cache in self._wc:
                scr, cb = self._wc[cache]
                self._wcn = getattr(self, "_wcn", 0) + 1
                K.dma(K.dsem("wcl%d" % (self._wcn % 4)), dst[:, 0:A, :], scr.rearrange("p (a n) -> p a n", n=N), reads=[cb], writes=[dst.b(key)])
                return
        cap = stg[0].t.shape[-1]
        per = max(1, cap // N)
        assert N <= cap
        a0 = 0
        n = getattr(self, "_lw", 0)
        while a0 < A:
            a1 = min(A, a0 + per)
            st = stg[n % len(stg)]
            view = st[:, 0:(a1 - a0) * N].rearrange("p (a n) -> p a n", n=N)
            K.dma(sems[n % len(stg)], view, src3[:, a0:a1, :], writes=[st.b()])
            self.cast(dst[:, a0:a1, :], view, [st.b()], [dst.b(key)])
            n += 1
            a0 = a1
        self._lw = n
        if cache is not None:
            scr = self.dram_tmp("wc_" + cache, [128, A * N], BF16)
            cb = Buf("wc_" + cache)
            self._wc[cache] = (scr, cb)
            K.dma(K.dsem("wcs%d" % (len(self._wc) % 2)), scr.rearrange("p (a n) -> p a n", n=N), dst[:, 0:A, :], reads=[dst.b(key)], writes=[cb])

    def rope_tables(self, S, pss, C32, S32):
        K = self.K
        pos = self.I["pos_prev" if pss == 0 else "pos_own"]
        TWO_PI = 2.0 * np.pi
        C1 = 6.28125
        C2 = TWO_PI - C1
        pi_ = S.sb("posi", [32, T], I32)
        ang = S.sb("ang", [32, T], F32)
        ki = S.sb("ki", [32, T], I32)
        kf = S.sb("kf", [32, T], F32)
        r = S.sb("rr", [32, T], F32)
        sem = K.dsem("misc")
        K.dma(sem, pi_[:], pos, writes=[pi_.b()])
        K.op("dve", lambda e: e.tensor_copy(out=ang[:], in_=pi_[:]), reads=[pi_.b()], writes=[ang.b()])
        K.op("dve", lambda e: e.tensor_scalar(out=ang[:], in0=ang[:], scalar1=self.ident[0:32, 128:129], scalar2=None, op0=ALU.mult),
             reads=[ang.b(), self.ident.b()], writes=[ang.b()])
        for which in range(2):
            off = 0.0 if which == 0 else 0.5 * np.pi
            K.op("dve", lambda e: e.tensor_scalar(out=ki[:], in0=ang[:], scalar1=float(off), scalar2=float(1.0 / TWO_PI), op0=ALU.add, op1=ALU.mult),
                 reads=[ang.b()], writes=[ki.b()])
            K.op("dve", lambda e: e.tensor_copy(out=kf[:], in_=ki[:]), reads=[ki.b()], writes=[kf.b()])
            K.op("dve", lambda e: e.scalar_tensor_tensor(out=r[:], in0=kf[:], scalar=float(-C1), in1=ang[:], op0=ALU.mult, op1=ALU.add),
                 reads=[kf.b(), ang.b()], writes=[r.b()])
            K.op("dve", lambda e: e.scalar_tensor_tensor(out=r[:], in0=kf[:], scalar=float(-C2), in1=r[:], op0=ALU.mult, op1=ALU.add),
                 reads=[kf.b(), r.b()], writes=[r.b()])
            K.op("dve", lambda e: e.tensor_scalar(out=r[:], in0=r[:], scalar1=float(off), scalar2=3.1415925, op0=ALU.add, op1=ALU.min),
                 reads=[r.b()], writes=[r.b()])
            K.op("dve", lambda e: e.tensor_scalar(out=r[:], in0=r[:], scalar1=-3.1415925, scalar2=None, op0=ALU.max),
                 reads=[r.b()], writes=[r.b()])
            if which == 0:
                for lo in (0, 64):
                    K.op("act", lambda e, lo=lo: e.activation(out=S32[lo:lo + 32, :], in_=r[:], func=AF.Sin, scale=self.ident[0:32, 129:130]),
                         reads=[r.b(), self.ident.b()], writes=[S32.b()])
            else:
                for lo in (0, 64):
                    K.op("act", lambda e, lo=lo: e.activation(out=C32[lo:lo + 32, :], in_=r[:], func=AF.Sin),
                         reads=[r.b()], writes=[C32.b()])

    def mla(self, pss, li, kv_only=False):
        K, I = self.K, self.I
        j = li // 3
        NKB_OWN = T // 128
        NK = T * (pss + 1)
        NKB = NK // 128
        prevb = self.flags[:, 0:1]
        wsem = [K.dsem("mw%d" % i) for i in range(2)]
        csems = [K.dsem("cache_st%d" % i) for i in range(4)]
        lsem = [K.dsem("cache_ld%d" % i) for i in range(2)]
        with self.sbuf_scope() as SM:
            cqn = SM.sb("cqn", [128, 3, T], BF16) if not kv_only else None
            C32 = SM.sb("C32", [96, T], BF16)
            S32 = SM.sb("S32", [96, T], BF16)
            ao = SM.sb("ao", [128, DC, T], BF16) if not kv_only else None
            with self.sbuf_scope() as SK:
                ckvn = SK.sb("ckvn", [128, 2, T], BF16)
                krope = SK.sb("krope", [96, T], BF16)
                with self.sbuf_scope() as S1:
                    with self.sbuf_scope() as SR:
                        self.rope_tables(SR, pss, C32, S32)
                    stg = [S1.sb("mstg%d" % i, [128, 1024], F32) for i in range(2)]
                    wdq = S1.sb("wdq", [128, DC, QL], BF16)
                    wdkv = S1.sb("wdkv", [128, DC, 320], BF16)
                    self.load_w_bf16(wdq, I["w_dq"][j].rearrange("(kc p) n -> p kc n", p=128), stg, wsem, cache="wdq%d" % j)
                    dkA = I["w_dkvA"][j].rearrange("(kc p) n -> p kc n", p=128)
                    dkB = I["w_dkvB"][j].rearrange("(kc p) n -> p kc n", p=128)
                    self.load_w_bf16(_Vn(wdkv, 0, 288), dkA, stg, wsem, cache="wdkvA%d" % j)
                    self.load_w_bf16(_Vn(wdkv, 288, 320), dkB, stg, wsem, cache="wdkvB%d" % j)
                    gq = self.sbv("qn_gT", [128, 6])
                    gkv = self.sbv("kvn_gT", [128, 4])
                    hTs = [S1.sb("mhT%d" % i, [128, DC, TB], BF16) for i in range(2)]
                    ws = self.norm_ws(S1, nset=1)
                    sqqs = [S1.sb("sqq%d" % i, [128, 3, TB], BF16) for i in range(1)] * 2
                    sqks = [S1.sb("sqk%d" % i, [128, 2, TB], BF16) for i in range(1)] * 2
                    kr1 = [S1.sb("kr1_%d" % i, [32, TB], F32) for i in range(1)] * 2
                    kr2 = [S1.sb("kr2_%d" % i, [32, TB], F32) for i in range(1)] * 2
                    for tb in range(NTB):
                        ts = slice(tb * TB, (tb + 1) * TB)
                        hT = hTs[tb % 2]
                        self.prenorm(ws, li, 0, tb, hT, 0, 0)
                        if True:
                            sqq = sqqs[tb % 2]
                            pq = []
                            for oc in range(0 if kv_only else 3):
                                pp = self.psum((0, 4))
                                pq.append(pp)
                                for kc in range(DC):
                                    K.op("pe", lambda e, pp=pp, oc=oc, kc=kc: e.matmul(pp[:], lhsT=wdq[:, kc, oc * 128:(oc + 1) * 128], rhs=hT[:, kc, :],
                                                                                         start=(kc == 0), stop=(kc == DC - 1)),
                                         reads=[wdq.b(), hT.b((0, kc))], writes=[pp.b()], inc=(kc == DC - 1))
                                self.sq_ops(sqq, oc, pp[:], [pp.b()], "act")
                            rstd = self.rstd_from_sq(self.ws_set(ws), sqq, 3, QL, TB) if not kv_only else None
                            for oc in range(0 if kv_only else 3):
                                K.op("dve", lambda e, oc=oc: e.scalar_tensor_tensor(out=cqn[:, oc, ts], in0=pq[oc][:], scalar=gq[:, j * 3 + oc:j * 3 + oc + 1],
                                                                                      in1=rstd[:], op0=ALU.mult, op1=ALU.mult),
                                     reads=[pq[oc].b(), rstd.b(), gq.b()], writes=[cqn.b(tb)])
                            sqk = sqks[tb % 2]
                            pk = []
                            for oc in range(2):
                                pp = self.psum((0, 4))
                                pk.append(pp)
                                for kc in range(DC):
                                    K.op("pe", lambda e, pp=pp, oc=oc, kc=kc: e.matmul(pp[:], lhsT=wdkv[:, kc, oc * 128:(oc + 1) * 128], rhs=hT[:, kc, :],
                                                                                         start=(kc == 0), stop=(kc == DC - 1)),
                                         reads=[wdkv.b(), hT.b((0, kc))], writes=[pp.b()], inc=(kc == DC - 1))
                                self.sq_ops(sqk, oc, pp[:], [pp.b()], "act")
                            rstd2 = self.rstd_from_sq(self.ws_set(ws), sqk, 2, KVL, TB)
                            for oc in range(2):
                                K.op("dve", lambda e, oc=oc: e.scalar_tensor_tensor(out=ckvn[:, oc, ts], in0=pk[oc][:], scalar=gkv[:, j * 2 + oc:j * 2 + oc + 1],
                                                                                      in1=rstd2[:], op0=ALU.mult, op1=ALU.mult),
                                     reads=[pk[oc].b(), rstd2.b(), gkv.b()], writes=[ckvn.b(tb)])
                            pr = []
                            for w in range(2):
                                pp = self.psum((4, 2))
                                pr.append(pp)
                                for kc in range(DC):
                                    K.op("pe", lambda e, pp=pp, w=w, kc=kc: e.matmul(pp[0:32, :], lhsT=wdkv[:, kc, 256 + 32 * w:288 + 32 * w], rhs=hT[:, kc, :],
                                                                                       start=(kc == 0), stop=(kc == DC - 1)),
                                         reads=[wdkv.b(), hT.b((0, kc))], writes=[pp.b()], inc=(kc == DC - 1))
                            t1 = kr1[tb % 2]
                            t2 = kr2[tb % 2]
                            K.op("dve", lambda e: e.tensor_tensor(out=t1[:], in0=pr[0][0:32, :], in1=C32[0:32, ts], op=ALU.mult),
                                 reads=[pr[0].b(), C32.b()], writes=[t1.b()])
                            K.op("dve", lambda e: e.tensor_tensor(out=t2[:], in0=pr[1][0:32, :], in1=S32[0:32, ts], op=ALU.mult),
                                 reads=[pr[1].b(), S32.b()], writes=[t2.b()])
                            K.op("dve", lambda e: e.tensor_tensor(out=krope[64:96, ts], in0=t1[:], in1=t2[:], op=ALU.add),
                                 reads=[t1.b(), t2.b()], writes=[krope.b(tb)])
                with self.sbuf_scope() as S1:
                    stg = [S1.sb("mstg%d" % i, [128, 1024], F32) for i in range(2)]
                    wk = S1.sb("wukvK", [128, 2, H * 64], BF16)
                    wv = S1.sb("wukvV", [128, 2, H * 64], BF16)
                    for hh in range(0, H, 8):
                        self.load_w_bf16(_Vn(wk, hh * 64, (hh + 8) * 64), I["w_ukvK"][j].rearrange("(kc p) n -> p kc n", p=128)[:, :, hh * 64:(hh + 8) * 64], stg, wsem, key=("k", hh), cache="wk%d_%d" % (j, hh))
                    for hh in range(0, H, 8):
                        self.load_w_bf16(_Vn(wv, hh * 64, (hh + 8) * 64), I["w_ukvV"][j].rearrange("(kc p) n -> p kc n", p=128)[:, :, hh * 64:(hh + 8) * 64], stg, wsem, key=("v", hh), cache="wv%d_%d" % (j, hh))
                    khs = [S1.sb("kh%d" % i, [96, T], BF16) for i in range(2)]
                    vhs = [S1.sb("vh%d" % i, [128, NKB_OWN, 128], BF16) for i in range(2)]
                    sqkhs = [S1.sb("sqkh%d" % i, [96, T], BF16) for i in range(2)]
                    mx4s = [S1.sb("mx4_%d" % i, [128, 4], F32) for i in range(2)]
                    for v in vhs:
                        K.op("pool", lambda e, v=v: e.memset(v[:, :, 64:128], 1.0), writes=[v.b("ones")])
                    for h in range(H):
                        kh, vh = khs[h % 2], vhs[h % 2]
                        sqkh, mx4 = sqkhs[h % 2], mx4s[h % 2]
                        K.op("act", lambda e, kh=kh: e.activation(out=kh[64:96, :], in_=krope[64:96, :], func=AF.Identity),
                             reads=krope.bs(range(NTB)), writes=[kh.b("r")])
                        for tb in range(NTB):
                            ts = slice(tb * TB, (tb + 1) * TB)
                            pp = self.psum((0, 4))
                            for c in range(2):
                                K.op("pe", lambda e, pp=pp, c=c: e.matmul(pp[0:64, :], lhsT=wk[:, c, h * 64:(h + 1) * 64], rhs=ckvn[:, c, ts],
                                                                            start=(c == 0), stop=(c == 1)),
                                     reads=[wk.b(("k", 0)), wk.b(("k", 8)), ckvn.b(tb)], writes=[pp.b()], inc=(c == 1))
                            if tb % 2 == 0:
                                K.op("dve", lambda e, pp=pp: e.tensor_copy(out=kh[0:64, ts], in_=pp[0:64, :]), reads=[pp.b()], writes=[kh.b(tb)])
                            else:
                                K.op("act", lambda e, pp=pp: e.activation(out=kh[0:64, ts], in_=pp[0:64, :], func=AF.Identity), reads=[pp.b()], writes=[kh.b(tb)])
                        K.dma(csems[(2 * h) % 4], self.kcache[j, h, :, pss * T:(pss + 1) * T], kh[:], reads=kh.bs(["r", 0, 1, 2, 3]), writes=[self.kcb(j, h, pss)])
                        for g in range(2):
                            pp = self.psum((0, 4))
                            for kb in range(8):
                                blk = g * 8 + kb
                                for c in range(2):
                                    K.op("pe", lambda e, pp=pp, kb=kb, blk=blk, c=c: e.matmul(pp[:, kb * 64:(kb + 1) * 64], lhsT=ckvn[:, c, blk * 128:(blk + 1) * 128],
                                                                                                rhs=wv[:, c, h * 64:(h + 1) * 64], start=(c == 0), stop=(c == 1)),
                                         reads=[wv.b(("v", 0)), wv.b(("v", 8)), ckvn.b(blk // 4)], writes=[pp.b()], inc=(kb == 7 and c == 1))
                            src = pp[:, :].rearrange("p (k d) -> p k d", d=64)
                            if g == 0:
                                K.op("dve", lambda e, src=src: e.tensor_copy(out=vh[:, 0:8, 0:64], in_=src), reads=[pp.b()], writes=[vh.b(0)])
                            else:
                                K.op("act", lambda e, src=src: e.activation(out=vh[:, 8:16, 0:64], in_=src, func=AF.Identity), reads=[pp.b()], writes=[vh.b(1)])
                        K.dma(csems[(2 * h + 1) % 4], self.vcache[j, h, :, pss * NKB_OWN:(pss + 1) * NKB_OWN, :], vh[:], reads=vh.bs(["ones", 0, 1]), writes=[self.vcb(j, h, pss)])
                        K.op("dve", lambda e: e.tensor_tensor(out=sqkh[:], in0=kh[:], in1=kh[:], op=ALU.mult), reads=kh.bs(["r", 0, 1, 2, 3]), writes=[sqkh.b()])
                        for tb in range(NTB):
                            ts = slice(tb * TB, (tb + 1) * TB)
                            pp = self.psum((6, 2))
                            K.op("pe", lambda e, pp=pp: e.matmul(pp[:], lhsT=self.onesb[0:96, :], rhs=sqkh[0:96, ts], start=True, stop=True),
                                 reads=[sqkh.b(), self.onesb.b()], writes=[pp.b()])
                            K.op("dve", lambda e, pp=pp, tb=tb: e.tensor_reduce(out=mx4[:, tb:tb + 1], in_=pp[:], axis=AX.X, op=ALU.max),
                                 reads=[pp.b()], writes=[mx4.b()])
                        kcol = (j * 2 + pss) * H + h
                        K.op("dve", lambda e: e.tensor_reduce(out=self.kmax2[:, kcol:kcol + 1], in_=mx4[:], axis=AX.X, op=ALU.max),
                             reads=[mx4.b()], writes=[self.kmax2.b()])
            if kv_only:
                return
            with self.sbuf_scope() as S1:
                wqa = S1.sb("wuqA", [128, 3, H * 96], BF16)
                wqb = S1.sb("wuqB", [128, 3, H * 32], BF16)
                with self.sbuf_scope() as SG:
                    stg = [SG.sb("mstg%d" % i, [128, 1024], F32) for i in range(2)]
                    for hh in range(0, H, 8):
                        self.load_w_bf16(_Vn(wqa, hh * 96, (hh + 8) * 96), I["w_uqA"][j].rearrange("(kc p) n -> p kc n", p=128)[:, :, hh * 96:(hh + 8) * 96], stg, wsem, cache="wqa%d_%d" % (j, hh))
                    self.load_w_bf16(wqb, I["w_uqB"][j].rearrange("(kc p) n -> p kc n", p=128), stg, wsem, cache="wqb%d" % j)
                khf = [S1.sb("khf%d" % i, [96, NK], BF16) for i in range(2)]
                vhf = [S1.sb("vhf%d" % i, [128, NKB, 128], BF16) for i in range(2)]
                qhs = [S1.sb("qh%d" % i, [96, T], BF16) for i in range(2)]
                sqqh = S1.sb("sqqh", [96, T], BF16)
                mxq = S1.sb("mxq", [128, 4], F32)
                sc = [S1.sb("attsc%d" % i, [128, 8], F32) for i in range(2)]
                pts = [S1.sb("pt%d" % i, [128, TB], BF16) for i in range(4)]
                ptd = [S1.sb("ptd%d" % i, [128, TB], BF16) for i in range(4)]
                r1 = [S1.sb("qr1_%d" % i, [96, TB], F32) for i in range(2)]
                r2 = [S1.sb("qr2_%d" % i, [96, TB], F32) for i in range(2)]
                rec = [S1.sb("rec%d" % i, [64, TB], F32) for i in range(2)]
                for dj in range(4):
                    K.op("pool", lambda e, dj=dj: e.memset(ptd[dj][:], 0.0), writes=[ptd[dj].b()])
                cnt = {"pt": 0, "rec": 0}

                def prep(h):
                    kf_, vf_, qh, s_ = khf[h % 2], vhf[h % 2], qhs[h % 2], sc[h % 2]
                    K.dma(lsem[h % 2], kf_[:], self.kcache[j, h, :, 0:NK], reads=[self.kcb(j, h, p_) for p_ in range(pss + 1)], writes=[kf_.b()], hold=True)
                    K.dma(lsem[h % 2], vf_[:], self.vcache[j, h, :, 0:NKB, :], reads=[self.vcb(j, h, p_) for p_ in range(pss + 1)], writes=[vf_.b()])
                    yield
                    for tb in range(NTB):
                        ts = slice(tb * TB, (tb + 1) * TB)
                        pa = self.psum((6, 1))
                        for c in range(3):
                            K.op("pe", lambda e, c=c: e.matmul(pa[0:96, :], lhsT=wqa[:, c, h * 96:(h + 1) * 96], rhs=cqn[:, c, ts], start=(c == 0), stop=(c == 2)),
                                 reads=[wqa.b(), cqn.b(tb)], writes=[pa.b()], inc=(c == 2))
                        pb = self.psum((7, 1))
                        for c in range(3):
                            K.op("pe", lambda e, c=c: e.matmul(pb[0:32, :], lhsT=wqb[:, c, h * 32:(h + 1) * 32], rhs=cqn[:, c, ts], start=(c == 0), stop=(c == 2)),
                                 reads=[wqb.b(), cqn.b(tb)], writes=[pb.b()], inc=(c == 2))
                        a1, a2 = r1[tb % 2], r2[tb % 2]
                        K.op("dve", lambda e: e.tensor_tensor(out=a1[64:96, :], in0=pa[64:96, :], in1=C32[64:96, ts], op=ALU.mult), reads=[pa.b(), C32.b()], writes=[a1.b()])
                        K.op("dve", lambda e: e.tensor_tensor(out=a2[64:96, :], in0=pb[0:32, :], in1=S32[0:32, ts], op=ALU.mult), reads=[pb.b(), S32.b()], writes=[a2.b()])
                        K.op("dve", lambda e: e.tensor_tensor(out=qh[64:96, ts], in0=a1[64:96, :], in1=a2[64:96, :], op=ALU.add), reads=[a1.b(), a2.b()], writes=[qh.b((tb, "r"))])
                        K.op("dve", lambda e: e.tensor_copy(out=qh[0:64, ts], in_=pa[0:64, :]), reads=[pa.b()], writes=[qh.b((tb, "n"))])
                        yield
                    K.op("dve", lambda e: e.tensor_tensor(out=sqqh[:], in0=qh[:], in1=qh[:], op=ALU.mult), reads=qh.bs([(t_, x_) for t_ in range(NTB) for x_ in "rn"]), writes=[sqqh.b()])
                    for tb in range(NTB):
                        ts = slice(tb * TB, (tb + 1) * TB)
                        pp = self.psum((6, 1))
                        K.op("pe", lambda e: e.matmul(pp[:], lhsT=self.onesb[0:96, :], rhs=sqqh[0:96, ts], start=True, stop=True),
                             reads=[sqqh.b(), self.onesb.b()], writes=[pp.b()])
                        K.op("dve", lambda e: e.tensor_reduce(out=mxq[:, tb:tb + 1], in_=pp[:], axis=AX.X, op=ALU.max), reads=[pp.b()], writes=[mxq.b()])
                    yield
                    K.op("dve", lambda e: e.tensor_reduce(out=s_[:, 0:1], in_=mxq[:], axis=AX.X, op=ALU.max), reads=[mxq.b()], writes=[s_.b()])
                    k0 = (j * 2 + 0) * H + h
                    k1 = (j * 2 + pss) * H + h
                    K.op("dve", lambda e: e.tensor_tensor(out=s_[:, 1:2], in0=self.kmax2[:, k0:k0 + 1], in1=self.kmax2[:, k1:k1 + 1], op=ALU.max),
                         reads=[self.kmax2.b()], writes=[s_.b()])
                    K.op("dve", lambda e: e.scalar_tensor_tensor(out=s_[:, 2:3], in0=s_[:, 0:1], scalar=1e-12, in1=s_[:, 1:2], op0=ALU.max, op1=ALU.mult),
                         reads=[s_.b()], writes=[s_.b()])
                    K.op("dve", lambda e: e.tensor_scalar(out=s_[:, 2:3], in0=s_[:, 2:3], scalar1=1e-12, scalar2=None, op0=ALU.max), reads=[s_.b()], writes=[s_.b()])
                    K.op("act", lambda e: e.activation(out=s_[:, 3:4], in_=s_[:, 2:3], func=AF.Ln), reads=[s_.b()], writes=[s_.b()])
                    K.op("act", lambda e: e.activation(out=s_[:, 4:5], in_=s_[:, 3:4], func=AF.Exp, scale=0.5), reads=[s_.b()], writes=[s_.b()])
                    K.op("dve", lambda e: e.tensor_scalar(out=s_[:, 5:6], in0=s_[:, 4:5], scalar1=float(-SCALE), scalar2=None, op0=ALU.mult), reads=[s_.b()], writes=[s_.b()])
                    K.op("dve", lambda e: e.tensor_tensor(out=s_[:, 6:7], in0=s_[:, 5:6], in1=prevb, op=ALU.add), reads=[s_.b(), self.flags.b()], writes=[s_.b()])

                LA = 3
                lazy = None
                lazy_layers = []
                if getattr(self, "ada_lazy_ok", False) and self.ada_rest and pss == 0:
                    lazy_layers = self.ada_rest
                    self.ada_rest = []
                    lazy = self.ada_gen(S1, lazy_layers, nbuf=3)

                def attn(h):
                    kf_, vf_, qh, s_ = khf[h % 2], vhf[h % 2], qhs[h % 2], sc[h % 2]
                    items = []
                    for qb in range(NTB):
                        blocks = []
                        if pss == 1:
                            for kb in range(NKB_OWN):
                                blocks.append((kb, 6, None))
                        for kbo in range(4 * (qb + 1)):
                            blocks.append((pss * NKB_OWN + kbo, 5, (kbo - 4 * qb) if kbo >= 4 * qb else None))
                        for i_, (blk, bcol, dj) in enumerate(blocks):
                            items.append((qb, blk, bcol, dj, i_ == 0, i_ == len(blocks) - 1))
                    n = len(items)
                    sts = [None] * n
                    accs = {}
                    gen = prep(h + 1) if h + 1 < H else iter(())
                    every = max(1, n // 8)
                    aevery = max(1, (n * H) // (48 * max(1, len(lazy_layers)) + 8)) if lazy is not None else 0

                    def qk(i):
                        qb, blk, bcol, dj, first, last = items[i]
                        c0 = 0 if dj is None else 128 * dj
                        st = self.psum((0, 4))
                        sts[i] = st
                        K.op("pe", lambda e: e.matmul(st[:, c0:TB], lhsT=kf_[0:96, blk * 128:(blk + 1) * 128], rhs=qh[0:96, qb * TB + c0:(qb + 1) * TB],
                                                      start=True, stop=True),
                             reads=[kf_.b(), qh.b((qb, "r")), qh.b((qb, "n"))], writes=[st.b()])
                    for i in range(min(LA, n)):
                        qk(i)
                    for i in range(n):
                        if i + LA < n:
                            qk(i + LA)
                        if i % every == every - 1:
                            next(gen, None)
                        if lazy is not None and i % aevery == aevery - 1:
                            next(lazy, None)
                        qb, blk, bcol, dj, first, last = items[i]
                        st = sts[i]
                        if first:
                            accs[qb] = self.psum((4, 2))
                        acc = accs[qb]
                        if dj is None:
                            pt = pts[cnt["pt"] % len(pts)]
                            cnt["pt"] += 1
                            c0 = 0
                            K.op("act", lambda e: e.activation(out=pt[:], in_=st[:], func=AF.Exp, bias=s_[:, bcol:bcol + 1], scale=float(SCALE)),
                                 reads=[st.b(), s_.b()], writes=[pt.b()])
                        else:
                            pt = ptd[dj]
                            c0 = 128 * dj
                            K.op("act", lambda e: e.activation(out=pt[:, c0:TB], in_=st[:, c0:TB], func=AF.Exp, bias=s_[:, bcol:bcol + 1], scale=float(SCALE)),
                                 reads=[st.b(), s_.b()], writes=[pt.b()])
                            K.op("pool", lambda e: e.memset(pt[64:128, c0:c0 + 64], 0.0), reads=[pt.b()], writes=[pt.b()])
                        K.op("pe", lambda e: e.matmul(acc[:, c0:TB], lhsT=vf_[:, blk, :], rhs=pt[:, c0:TB], start=first, stop=last),
                             reads=[vf_.b(), pt.b()], writes=[acc.b()], inc=last)
                        if self.warm and not last:
                            K.op("pe", lambda e: e.matmul(self.ps[7][64:128, :], lhsT=self.onesb[:, 0:64], rhs=self.onesb_w[:, :], start=True, stop=True),
                                 reads=[self.onesb.b(), self.onesb_w.b()], inc=False)
                        if last:
                            rc = rec[cnt["rec"] % 2]
                            cnt["rec"] += 1
                            K.op("dve", lambda e: e.reciprocal(out=rc[0:64, :], in_=acc[64:128, :]), reads=[acc.b()], writes=[rc.b()])
                            po = (h % 2) * 64
                            K.op("dve", lambda e: e.tensor_tensor(out=ao[po:po + 64, h // 2, qb * TB:(qb + 1) * TB], in0=acc[0:64, :], in1=rc[0:64, :], op=ALU.mult),
                                 reads=[acc.b(), rc.b()], writes=[ao.b((h // 2, qb))])
                    for _ in gen:
                        pass

                for _ in prep(0):
                    pass
                for h in range(H):
                    attn(h)
                if lazy is not None:
                    for _ in lazy:
                        pass
            with self.sbuf_scope() as S1:
                stg = [S1.sb("mstg%d" % i, [128, 1024], F32) for i in range(2)]
                wo = S1.sb("wo", [128, DC, D], BF16)
                self.load_w_bf16(wo, I["w_o"][j].rearrange("(kc p) n -> p kc n", p=128), stg, wsem, cache="wo%d" % j)
                ybs = [S1.sb("myb%d" % i, [128, DC, TB], F32) for i in range(2)]
                ws = self.norm_ws(S1, nset=2, pn=False)
                for tb in range(NTB):
                    yb = ybs[tb % 2]
                    for oc in range(DC):
                        pp = self.psum((0, 4))
                        for kc in range(DC):
                            K.op("pe", lambda e, pp=pp, oc=oc, kc=kc: e.matmul(pp[:], lhsT=wo[:, kc, oc * 128:(oc + 1) * 128], rhs=ao[:, kc, tb * TB:(tb + 1) * TB],
                                                                                 start=(kc == 0), stop=(kc == DC - 1)),
                                 reads=[wo.b(), ao.b((kc, tb))], writes=[pp.b()], inc=(kc == DC - 1))
                        self.evac("act" if oc % 2 == 0 else "dve", yb[:, oc, :], pp, [yb.b(oc)])
                    self.postnorm_residual(ws, li, 0, tb, yb)

    def kcb(self, j, h, p_):
        return self._cb("k", j, h, p_)

    def vcb(self, j, h, p_):
        return self._cb("v", j, h, p_)

    def _cb(self, kind, j, h, p_):
        if not hasattr(self, "_cbufs"):
            self._cbufs = {}
        key = (kind, j, h, p_)
        if key not in self._cbufs:
            self._cbufs[key] = Buf("cache%s" % (key,))
        return self._cbufs[key]

    def sbv(self, name, shape):
        if not hasattr(self, "_sbv"):
            self._sbv = {}
        if name not in self._sbv:
            assert shape is not None
            t = self.sb(name, shape, F32)
            self.K.dma(self.K.dsem("misc"), t[:], self.I[name], writes=[t.b()], hold=True)
            self._sbv[name] = t
        return self._sbv[name]


    def conv(self, pss, li):
        K, I = self.K, self.I
        wsem = [K.dsem("mw%d" % i) for i in range(2)]
        b1 = self.sbv("b_pw1T", None)
        cv = self.sbv("cv_vecT", None)
        wdw = self.sbv("w_dwT", None)
        uh = self.uhalo
        hflag = self.flags[:, 1:2]
        if pss == 0:
            K.op("dve", lambda e: e.memset(uh[:], 0.0), writes=[uh.b()])
        else:
            K.op("dve", lambda e: e.tensor_scalar(out=uh[:], in0=uh[:], scalar1=hflag, scalar2=None, op0=ALU.mult),
                 reads=[uh.b(), self.flags.b()], writes=[uh.b()])
        w1v = I["w_pw1"].rearrange("(kc p) n -> p kc n", p=128)
        w2v = I["w_pw2"].rearrange("(kc p) n -> p kc n", p=128)
        W = 2 * TB
        for sbk in range(2):
            with self.sbuf_scope() as SV:
                vT = SV.sb("vT", [128, DC, W], F32)
                with self.sbuf_scope() as SU:
                    uT = SU.sb("uT", [128, DC, 32 + W], BF16)
                    K.op("pool", lambda e: e.tensor_copy(out=uT[:, :, 0:32], in_=uh[:]), reads=[uh.b()], writes=[uT.b("h")])
                    with self.sbuf_scope() as S1:
                        w1 = S1.sb("wpw1", [128, DC, 2 * D], BF16)
                        with self.sbuf_scope() as SG:
                            stg = [SG.sb("cstg%d" % i, [128, 1024], F32) for i in range(2)]
                            for q4 in range(4):
                                self.load_w_bf16(_Vn(w1, q4 * 512, (q4 + 1) * 512), w1v[:, :, q4 * 512:(q4 + 1) * 512], stg, wsem, key=q4, cache="pw1_%d" % q4)
                        hT = S1.sb("chT", [128, DC, TB], BF16)
                        sg = [S1.sb("sig%d" % i, [128, TB], F32) for i in range(2)]
                        ws = self.norm_ws(S1, nset=1)
                        for t2 in range(2):
                            tb = sbk * 2 + t2
                            self.prenorm(ws, li, 0, tb, hT, 0, 0)
                            for oc in range(DC):
                                pa = self.psum((0, 4))
                                pb = self.psum((0, 4))
                                for (pp, off) in ((pa, 0), (pb, D)):
                                    for kc in range(DC):
                                        K.op("pe", lambda e, pp=pp, off=off, kc=kc: e.matmul(pp[:], lhsT=w1[:, kc, off + oc * 128:off + (oc + 1) * 128], rhs=hT[:, kc, :],
                                                                                               start=(kc == 0), stop=(kc == DC - 1)),
                                             reads=[w1.b((off + oc * 128) // 512), hT.b((0, kc))], writes=[pp.b()], inc=(kc == DC - 1))
                                sgt = sg[oc % 2]
                                K.op("act", lambda e: e.activation(out=sgt[:], in_=pb[:], func=AF.Sigmoid, bias=b1[:, DC + oc:DC + oc + 1]),
                                     reads=[pb.b(), b1.b()], writes=[sgt.b()])
                                K.op("dve", lambda e: e.scalar_tensor_tensor(out=uT[:, oc, 32 + t2 * TB:32 + (t2 + 1) * TB], in0=pa[:], scalar=b1[:, oc:oc + 1],
                                                                              in1=sgt[:], op0=ALU.add, op1=ALU.mult),
                                     reads=[pa.b(), sgt.b(), b1.b()], writes=[uT.b((oc, t2))])
                    K.op("pool", lambda e: e.tensor_copy(out=uh[:], in_=uT[:, :, W:W + 32]), reads=uT.bs([(c, 1) for c in range(DC)]), writes=[uh.b()])
                    with self.sbuf_scope() as S1:
                        dgs = [S1.sb("dg%d" % i, [128, CW, 128], BF16) for i in range(2)]
                        for c in range(DC):
                            dg = dgs[c % 2]
                            for jt in range(CW):
                                if jt % 2 == 0:
                                    K.op("act", lambda e, jt=jt: e.activation(out=dg[:, jt, :], in_=self.identb[:], func=AF.Copy, scale=wdw[:, c, jt:jt + 1]),
                                         reads=[self.identb.b(), wdw.b()], writes=[dg.b(jt % 2)])
                                else:
                                    K.op("dve", lambda e, jt=jt: e.tensor_scalar(out=dg[:, jt, :], in0=self.identb[:], scalar1=wdw[:, c, jt:jt + 1], scalar2=None, op0=ALU.mult),
                                         reads=[self.identb.b(), wdw.b()], writes=[dg.b(jt % 2)])
                            for t2 in range(2):
                                pp = self.psum((0, 4))
                                for jt in range(CW):
                                    c0 = t2 * TB + 2 + jt
                                    K.op("pe", lambda e, jt=jt, c0=c0: e.matmul(pp[:], lhsT=dg[:, jt, :], rhs=uT[:, c, c0:c0 + TB], start=(jt == 0), stop=(jt == CW - 1)),
                                         reads=[dg.b(jt % 2), uT.b("h"), uT.b((c, 0)), uT.b((c, 1))], writes=[pp.b()], inc=(jt == CW - 1))
                                if (c + t2) % 2 == 0:
                                    K.op("act", lambda e: e.activation(out=vT[:, c, t2 * TB:(t2 + 1) * TB], in_=pp[:], func=AF.Identity, bias=cv[:, c:c + 1]),
                                         reads=[pp.b(), cv.b()], writes=[vT.b((c, t2))])
                                else:
                                    K.op("dve", lambda e: e.tensor_scalar(out=vT[:, c, t2 * TB:(t2 + 1) * TB], in0=pp[:], scalar1=cv[:, c:c + 1], scalar2=None, op0=ALU.add),
                                         reads=[pp.b(), cv.b()], writes=[vT.b((c, t2))])
                with self.sbuf_scope() as S1:
                    w2 = S1.sb("wpw2", [128, DC, D], BF16)
                    with self.sbuf_scope() as SG:
                        stg = [SG.sb("cstg%d" % i, [128, 1024], F32) for i in range(2)]
                        self.load_w_bf16(w2, w2v, stg, wsem, cache="pw2")
                    for t2 in range(2):
                        tb = sbk * 2 + t2
                        vs = slice(t2 * TB, (t2 + 1) * TB)
                        with self.sbuf_scope() as S2:
                            yb = S2.sb("cyb", [128, DC, TB], F32)
                            with self.sbuf_scope() as S3:
                                vb = S3.sb("vb", [128, DC, TB], BF16)
                                sqv = S3.sb("sqv", [128, DC, TB], BF16)
                                for c in range(DC):
                                    K.op("act", lambda e, c=c: e.activation(out=vb[:, c, :], in_=vT[:, c, vs], func=AF.Identity), reads=[vT.b((c, t2))], writes=[vb.b(c)])
                                    self.sq_ops(sqv, c, vT[:, c, vs], [vT.b((c, t2))], "dve")
                                p1 = self.psum((6, 2))
                                p2 = self.psum((6, 2))
                                for (pp, src) in ((p1, vb), (p2, sqv)):
                                    for c in range(DC):
                                        K.op("pe", lambda e, pp=pp, src=src, c=c: e.matmul(pp[:], lhsT=self.onesb[:], rhs=src[:, c, :], start=(c == 0), stop=(c == DC - 1)),
                                             reads=[src.b(c), self.onesb.b()], writes=[pp.b()], inc=(c == DC - 1))
                                m = S3.sb("lnm", [128, TB], F32)
                                msq = S3.sb("lnmsq", [128, TB], F32)
                                var = S3.sb("lnvar", [128, TB], F32)
                                lt = S3.sb("lnlt", [128, TB], F32)
                                rstd = S3.sb("lnrstd", [128, TB], F32)
                                nmr = S3.sb("lnnmr", [128, TB], F32)
                                K.op("act", lambda e: e.activation(out=m[:], in_=p1[:], func=AF.Identity, scale=1.0 / D), reads=[p1.b()], writes=[m.b()])
                                K.op("dve", lambda e: e.tensor_tensor(out=msq[:], in0=m[:], in1=m[:], op=ALU.mult), reads=[m.b()], writes=[msq.b()])
                                K.op("dve", lambda e: e.scalar_tensor_tensor(out=var[:], in0=p2[:], scalar=1.0 / D, in1=msq[:], op0=ALU.mult, op1=ALU.subtract),
                                     reads=[p2.b(), msq.b()], writes=[var.b()])
                                K.op("dve", lambda e: e.tensor_scalar(out=var[:], in0=var[:], scalar1=0.0, scalar2=None, op0=ALU.max), reads=[var.b()], writes=[var.b()])
                                K.op("act", lambda e: e.activation(out=lt[:], in_=var[:], func=AF.Ln, bias=self.epsc[:, 0:1]), reads=[var.b(), self.epsc.b()], writes=[lt.b()])
                                K.op("act", lambda e: e.activation(out=rstd[:], in_=lt[:], func=AF.Exp, scale=-0.5), reads=[lt.b()], writes=[rstd.b()])
                                K.op("dve", lambda e: e.scalar_tensor_tensor(out=nmr[:], in0=m[:], scalar=-1.0, in1=rstd[:], op0=ALU.mult, op1=ALU.mult),
                                     reads=[m.b(), rstd.b()], writes=[nmr.b()])
                                sT = S3.sb("sT", [128, DC, TB], BF16)
                                tt = [S3.sb("lntt%d" % i, [128, TB], F32) for i in range(4)]
                                for c in range(DC):
                                    ta, tb_ = tt[(2 * c) % 4], tt[(2 * c + 1) % 4]
                                    K.op("dve", lambda e, c=c, ta=ta: e.tensor_tensor(out=ta[:], in0=vT[:, c, vs], in1=rstd[:], op=ALU.mult),
                                         reads=[vT.b((c, t2)), rstd.b()], writes=[ta.b()])
                                    K.op("dve", lambda e, ta=ta, tb_=tb_: e.tensor_tensor(out=tb_[:], in0=ta[:], in1=nmr[:], op=ALU.add),
                                         reads=[ta.b(), nmr.b()], writes=[tb_.b()])
                                    K.op("act", lambda e, c=c, tb_=tb_: e.activation(out=sT[:, c, :], in_=tb_[:], func=AF.Silu, scale=cv[:, DC + c:DC + c + 1], bias=cv[:, 2 * DC + c:2 * DC + c + 1]),
                                         reads=[tb_.b(), cv.b()], writes=[sT.b(c)])
                                for oc in range(DC):
                                    pp = self.psum((0, 4))
                                    for kc in range(DC):
                                        K.op("pe", lambda e, pp=pp, oc=oc, kc=kc: e.matmul(pp[:], lhsT=w2[:, kc, oc * 128:(oc + 1) * 128], rhs=sT[:, kc, :], start=(kc == 0), stop=(kc == DC - 1)),
                                             reads=[w2.b(), sT.b(kc)], writes=[pp.b()], inc=(kc == DC - 1))
                                    if oc % 2 == 0:
                                        K.op("act", lambda e, pp=pp, oc=oc: e.activation(out=yb[:, oc, :], in_=pp[:], func=AF.Identity, bias=cv[:, 3 * DC + oc:3 * DC + oc + 1]),
                                             reads=[pp.b(), cv.b()], writes=[yb.b(oc)])
                                    else:
                                        K.op("dve", lambda e, pp=pp, oc=oc: e.tensor_scalar(out=yb[:, oc, :], in0=pp[:], scalar1=cv[:, 3 * DC + oc:3 * DC + oc + 1], scalar2=None, op0=ALU.add),
                                             reads=[pp.b(), cv.b()], writes=[yb.b(oc)])
                            with self.sbuf_scope() as S3:
                                self.postnorm_residual(self.norm_ws(S3, nset=1, pn=False), li, 0, tb, yb)

    def pool(self, pss, li):
        K, I = self.K, self.I
        wsem = [K.dsem("mw%d" % i) for i in range(2)]
        pv = self.sbv("pl_vecT", None)
        pf = self.sbv("poolfac", None)
        hh = self.hhalo
        hflag = self.flags[:, 1:2]
        if pss == 0:
            K.op("dve", lambda e: e.memset(hh[:], 0.0), writes=[hh.b()])
        else:
            K.op("dve", lambda e: e.tensor_scalar(out=hh[:], in0=hh[:], scalar1=hflag, scalar2=None, op0=ALU.mult),
                 reads=[hh.b(), self.flags.b()], writes=[hh.b()])
        with self.sbuf_scope() as S0:
            pw = S0.sb("poolw", [128, 8, 256], BF16)
            with self.sbuf_scope() as SG:
                stg = [SG.sb("pstg%d" % i, [128, 1024], F32) for i in range(2)]
                self.load_w_bf16(pw, I["pool_w"].rearrange("g (kc p) n -> p (g kc) n", p=128), stg, wsem, cache="poolw")
            WW = 16 + TB
            pws = self.norm_ws(S0, nset=2)
            for tb in range(NTB):
                with self.sbuf_scope() as S1:
                    hp = S1.sb("hp", [128, DC, WW], F32)
                    K.op("pool", lambda e: e.tensor_copy(out=hp[:, :, 0:16], in_=hh[:]), reads=[hh.b()], writes=[hp.b("h")])
                    self.prenorm(pws, li, 0, tb, _Vc(hp, 16), 0, 0)
                    K.op("pool", lambda e: e.tensor_copy(out=hh[:], in_=hp[:, :, TB:TB + 16]), reads=hp.bs([(0, c) for c in range(DC)]), writes=[hh.b()])
                    pT = S1.sb("ppT", [128, DC, TB], BF16)
                    yb = S1.sb("pyb", [128, DC, TB], F32)
                    sa = [S1.sb("psa%d" % i, [128, WW], F32) for i in range(4)]
                    for c in range(DC):
                        g = c // 2
                        eng = "dve"
                        bufs = sa[0:2] if c % 2 == 0 else sa[2:4]
                        cur_ap = lambda lo, hi, c=c: hp[:, c, lo:hi]
                        cur_b = [hp.b("h"), hp.b((0, c))]
                        lo = 0
                        for lvl in range(g + 1):
                            sh = 1 << lvl
                            dst = bufs[lvl % 2]
                            nlo = lo + sh
                            K.op(eng, lambda e, dst=dst, cur_ap=cur_ap, nlo=nlo, sh=sh: e.tensor_tensor(out=dst[:, nlo:WW], in0=cur_ap(nlo, WW), in1=cur_ap(nlo - sh, WW - sh), op=ALU.add),
                                 reads=cur_b, writes=[dst.b()])
                            cur_ap = (lambda lo_, hi_, dst=dst: dst[:, lo_:hi_])
                            cur_b = [dst.b()]
                            lo = nlo
                        if tb == 0:
                            K.op(eng, lambda e, cur_ap=cur_ap: e.tensor_tensor(out=cur_ap(16, 32), in0=cur_ap(16, 32), in1=pf[:, pss, g, :], op=ALU.mult),
                                 reads=cur_b + [pf.b()], writes=cur_b)
                        K.op("dve", lambda e, cur_ap=cur_ap, c=c, g=g: e.scalar_tensor_tensor(out=pT[:, c, :], in0=cur_ap(16, WW), scalar=1.0 / PWIN[g], in1=hp[:, c, 16:WW],
                                                                                               op0=ALU.mult, op1=ALU.subtract),
                             reads=cur_b + [hp.b((0, c))], writes=[pT.b(c)])
                    for g in range(4):
                        for o2 in range(2):
                            oc = 2 * g + o2
                            pp = self.psum((0, 4))
                            for k2 in range(2):
                                K.op("pe", lambda e, pp=pp, g=g, o2=o2, k2=k2: e.matmul(pp[:], lhsT=pw[:, 2 * g + k2, o2 * 128:(o2 + 1) * 128], rhs=pT[:, 2 * g + k2, :],
                                                                                          start=(k2 == 0), stop=(k2 == 1)),
                                     reads=[pw.b(), pT.b(2 * g + k2)], writes=[pp.b()], inc=(k2 == 1))
                            K.op("dve", lambda e, pp=pp, oc=oc: e.tensor_scalar(out=yb[:, oc, :], in0=pp[:], scalar1=pv[:, oc:oc + 1], scalar2=pv[:, DC + oc:DC + oc + 1],
                                                                                 op0=ALU.add, op1=ALU.mult),
                                 reads=[pp.b(), pv.b()], writes=[yb.b(oc)])
                    self.postnorm_residual(pws, li, 0, tb, yb)


class _Vc:
    def __init__(self, tl, off):
        self.tl, self.off = tl, off

    def __getitem__(self, idx):
        p, c, n = idx
        return self.tl.t[p, c, n.start + self.off:n.stop + self.off]

    def b(self, key=0):
        return self.tl.b(key)


class _Vn:
    def __init__(self, tl, lo, hi):
        self.tl, self.lo, self.hi = tl, lo, hi

    def __getitem__(self, idx):
        p, a, n = idx
        return self.tl.t[p, a, self.lo:self.hi]

    def b(self, key=0):
        return self.tl.b(key)


class _Scope:
    def __init__(self, prog):
        self.p = prog
        self.cms = []

    def __enter__(self):
        return self

    def sb(self, name, shape, dt=F32):
        self.p.uid += 1
        nm = "%s_%d" % (name, self.p.uid)
        cm = self.p.nc.sbuf_tensor(nm, list(shape), dt)
        t = cm.__enter__()
        self.cms.append(cm)
        return Tl(t, nm)

    def __exit__(self, *a):
        self.p.K.barrier()
        for cm in reversed(self.cms):
            cm.__exit__(None, None, None)
        return False


def _t128(v, n):
    return np.ascontiguousarray(np.asarray(v, np.float32).reshape(n, 128).T)


def make_in_maps(inp):
    x = np.asarray(inp["x"], np.float32)
    B = x.shape[0]
    pos = np.asarray(inp["positions"]).astype(np.int32)
    shared = {}
    shared["ada_wT"] = np.ascontiguousarray(np.asarray(inp["ada_w"], np.float32).transpose(0, 2, 1))
    shared["ada_bT"] = np.concatenate([_t128(inp["ada_b"][i], 48) for i in range(DEPTH)], axis=1)
    shared["norm_gT"] = np.concatenate([_t128(inp["norm_g"][i, j], DC) for i in range(DEPTH) for j in range(4)], axis=1)
    shared["w_dq"] = np.ascontiguousarray(inp["mla_w_dq"], np.float32)
    shared["qn_gT"] = np.concatenate([_t128(inp["mla_q_norm_g"][j], 3) for j in range(2)], axis=1)
    wuq = np.asarray(inp["mla_w_uq"], np.float32).reshape(2, QL, H, 96)
    shared["w_uqA"] = np.ascontiguousarray(wuq.reshape(2, QL, H * 96))
    shared["w_uqB"] = np.ascontiguousarray(np.concatenate([wuq[..., 80:96], wuq[..., 64:80]], axis=-1).reshape(2, QL, H * 32))
    wdkv = np.asarray(inp["mla_w_dkv"], np.float32)
    shared["w_dkvA"] = np.ascontiguousarray(wdkv)
    shared["w_dkvB"] = np.ascontiguousarray(np.concatenate([wdkv[..., 272:288], wdkv[..., 256:272]], axis=-1))
    shared["kvn_gT"] = np.concatenate([_t128(inp["mla_kv_norm_g"][j], 2) for j in range(2)], axis=1)
    wukv = np.asarray(inp["mla_w_ukv"], np.float32).reshape(2, KVL, H, 128)
    shared["w_ukvK"] = np.ascontiguousarray(wukv[..., 0:64].reshape(2, KVL, H * 64))
    shared["w_ukvV"] = np.ascontiguousarray(wukv[..., 64:128].reshape(2, KVL, H * 64))
    shared["w_o"] = np.ascontiguousarray(inp["mla_w_o"], np.float32)
    shared["w_pw1"] = np.ascontiguousarray(inp["conv_w_pw1"][0], np.float32)
    shared["b_pw1T"] = _t128(inp["conv_b_pw1"][0], 16)
    shared["w_dwT"] = np.ascontiguousarray(np.asarray(inp["conv_w_dw"][0], np.float32).T.reshape(DC, 128, CW).transpose(1, 0, 2))
    shared["cv_vecT"] = np.concatenate([_t128(inp[k][0], DC) for k in ("conv_b_dw", "conv_ln_g", "conv_ln_b", "conv_b_pw2")], axis=1)
    shared["w_pw2"] = np.ascontiguousarray(inp["conv_w_pw2"][0], np.float32)
    shared["pool_w"] = np.ascontiguousarray(inp["pool_w"][0], np.float32)
    shared["pl_vecT"] = np.concatenate([_t128(np.asarray(inp["pool_b"][0]).reshape(-1), DC), _t128(inp["pool_scale"][0], DC)], axis=1)
    shared["ffn_w1"] = np.ascontiguousarray(inp["ffn_w1"], np.float32)
    shared["ffn_w2"] = np.ascontiguousarray(inp["ffn_w2"], np.float32)
    consts = np.zeros((128, 136), np.float32)
    consts[:, :128] = np.eye(128, dtype=np.float32)
    invf = (10000.0 ** (-np.arange(0, 32, 2, dtype=np.float32) / np.float32(32.0))).astype(np.float32)
    consts[0:16, 128] = invf
    consts[16:32, 128] = invf
    consts[0:16, 129] = -1.0
    consts[16:32, 129] = 1.0
    shared["consts"] = consts
    fac_start = np.ones((4, 16), np.float32)
    for g, w in enumerate(PWIN):
        for t in range(16):
            fac_start[g, t] = float(w) / float(min(t + 1, w))
    maps = []
    for core in range(2 * B):
        b, half = core // 2, core % 2
        m = dict(shared)
        own = x[b, half * T:(half + 1) * T]
        m["xT_own"] = np.ascontiguousarray(own.T)
        m["pos_own"] = np.ascontiguousarray(np.broadcast_to(pos[b, half * T:(half + 1) * T].reshape(1, T), (32, T)))
        if half == 1:
            m["xT_prev"] = np.ascontiguousarray(x[b, 0:T].T)
            m["pos_prev"] = np.ascontiguousarray(np.broadcast_to(pos[b, 0:T].reshape(1, T), (32, T)))
        else:
            m["xT_prev"] = np.zeros((D, T), np.float32)
            m["pos_prev"] = np.zeros((32, T), np.int32)
        m["c"] = _t128(inp["c"][b], DC)
        m["c_rep"] = np.ascontiguousarray(np.broadcast_to(np.asarray(inp["c"][b], np.float32).reshape(1, D), (128, D)))
        fl = np.zeros((128, 4), np.float32)
        fl[:, 0] = 0.0 if half == 1 else NEG
        fl[:, 1] = 1.0 if half == 1 else 0.0
        m["flags"] = fl
        pf = np.ones((2, 4, 16), np.float32)
        pf[0] = fac_start
        if half == 0:
            pf[1] = fac_start
        m["poolfac"] = np.ascontiguousarray(np.broadcast_to(pf[None], (128, 2, 4, 16)))
        maps.append(m)
    return maps


_PROG = {}


def get_prog(**kw):
    key = tuple(sorted((k, str(v)) for k, v in kw.items()))
    if key not in _PROG:
        _PROG[key] = Prog(**kw)
    return _PROG[key]


def kernel(**inputs):
    x = np.asarray(inputs["x"])
    B, S, _ = x.shape
    prog = get_prog()
    maps = make_in_maps(inputs)
    res = run_bass_kernel_spmd(prog.nc, maps, core_ids=list(range(8)))
    out = np.empty((B, S, D), np.float32)
    for core in range(8):
        b, half = core // 2, core % 2
        out[b, half * T:(half + 1) * T] = np.asarray(res.results[core]["outT"]).T
    return out
```

```python
import numpy as np
import ml_dtypes
import concourse.bass as bass
import concourse.mybir as mybir
from concourse.bass_utils import run_bass_kernel_spmd

F32 = mybir.dt.float32
BF16 = mybir.dt.bfloat16
I32 = mybir.dt.int32
AF = mybir.ActivationFunctionType
ALU = mybir.AluOpType
AX = mybir.AxisListType

D = 1024
DC = 8
T = 2048
TB = 512
NTB = 4
DEPTH = 4
H = 16
DFF = 4096
EPS = 1e-6
QL = 384
KVL = 256
NEG = -30000.0
SCALE = 1.0 / float(np.sqrt(96.0))
CW = 31
PWIN = (2, 4, 8, 16)


class Buf:
    __slots__ = ("name", "w", "r")

    def __init__(self, name):
        self.name = name
        self.w = None
        self.r = {}


class KB:
    def __init__(self, nc):
        self.nc = nc
        self.eng = {"pe": nc.tensor, "act": nc.scalar, "dve": nc.vector,
                    "pool": nc.gpsimd, "sp": nc.sync}
        self.sems = {}
        self.cnt = {}
        for e in self.eng:
            self.sems[e] = nc.alloc_semaphore("c_" + e)
            self.cnt[e] = 0
        self.waited = {e: {} for e in self.eng}
        self.pending = {e: [] for e in self.eng}
        self.ndma = 0
        self.nsem = 0
        self.groups = {}

    def dsem(self, name):
        key = "d_" + name
        if key not in self.sems:
            self.sems[key] = self.nc.alloc_semaphore(key)
            self.cnt[key] = 0
        return key

    def _need(self, eng, evs):
        for k, v in evs.items():
            if self.waited[eng].get(k, 0) >= v:
                continue
            self.eng[eng].wait_ge(self.sems[k], v)
            self.waited[eng][k] = v

    def _deps(self, eng, reads, writes):
        evs = {}

        def add(ev, raw):
            if ev is None:
                return
            k, v = ev
            if k == "PENDING":
                assert v == eng and (eng == "pe" or not raw), "dependency on un-flushed op"
                return
            if k == eng:
                if eng in ("pe", "sp"):
                    return
                if not raw:
                    return
            if evs.get(k, 0) < v:
                evs[k] = v
        for b in reads:
            add(b.w, True)
        for b in writes:
            add(b.w, False)
            for k, v in b.r.items():
                add((k, v), False)
        return evs

    def _commit(self, ev, reads, writes):
        for b in reads:
            if b.r.get(ev[0], 0) < ev[1]:
                b.r[ev[0]] = ev[1]
        for b in writes:
            b.w = ev
            b.r = {}

    def op(self, eng, fn, reads=(), writes=(), inc=True):
        reads = list(reads)
        writes = list(writes)
        self._need(eng, self._deps(eng, reads, writes))
        ins = fn(self.eng[eng])
        if inc:
            self.cnt[eng] += 1
            ins.then_inc(self.sems[eng], 1)
            ev = (eng, self.cnt[eng])
            for (r, w) in self.pending[eng]:
                self._commit(ev, r, w)
            self.pending[eng] = []
            self._commit(ev, reads, writes)
        else:
            for b in writes:
                b.w = ("PENDING", eng)
            self.pending[eng].append((reads, writes))
        return ins

    def dma(self, sem, out, in_, reads=(), writes=(), q="sp", hold=False, **kw):
        reads = list(reads)
        writes = list(writes)
        grp = self.groups.setdefault(sem, [])
        if not grp and self.cnt[sem] > 0:
            self._need(q, {sem: self.cnt[sem]})
        self._need(q, self._deps(q, reads, writes))
        self.eng[q].dma_start(out=out, in_=in_, **kw).then_inc(self.sems[sem], 16)
        self.cnt[sem] += 16
        self.ndma += 1
        for b in writes:
            b.w = ("PENDING", "dma")
        grp.append((reads, writes))
        if not hold:
            self.dma_flush(sem)

    def dma_flush(self, sem):
        ev = (sem, self.cnt[sem])
        for (r, w) in self.groups.get(sem, []):
            self._commit(ev, r, w)
        self.groups[sem] = []

    def barrier(self):
        for e in self.eng:
            assert not self.pending[e]
        assert not any(self.groups.values()), "open dma group at barrier"
        evs = {k: v for k, v in self.cnt.items() if v > 0}
        for e in self.eng:
            self._need(e, {k: v for k, v in evs.items() if not (k == e and e in ("pe", "sp"))})

    def final_wait(self):
        evs = {k: v for k, v in self.cnt.items() if v > 0 and k != "sp"}
        self._need("sp", evs)


class Tl:
    def __init__(self, t, name):
        self.t = t
        self.name = name
        self.bufs = {}

    def b(self, key=0):
        if key not in self.bufs:
            self.bufs[key] = Buf("%s.%s" % (self.name, key))
        return self.bufs[key]

    def bs(self, keys):
        return [self.b(k) for k in keys]

    def __getitem__(self, idx):
        return self.t[idx]


def _bf16(a):
    return np.asarray(a, dtype=np.float32).astype(ml_dtypes.bfloat16)


class Prog:
    def __init__(self, layers_p1=(0, 1, 2), layers_p2=(0, 1, 2, 3), debug=None, mixers=True, ffn=True, warm=False):
        self.warm = warm
        self.do_mixers = mixers
        self.do_ffn = ffn
        self.layers_p1 = tuple(layers_p1)
        self.layers_p2 = tuple(layers_p2)
        self.debug = debug
        self.nc = bass.Bass("TRN2", target_bir_lowering=False)
        self.K = KB(self.nc)
        self.uid = 0
        self.build()

    def dram_in(self, name, shape, dt=F32):
        return self.nc.dram_tensor(name, list(shape), dt, kind="ExternalInput").ap()

    def dram_out(self, name, shape, dt=F32):
        return self.nc.dram_tensor(name, list(shape), dt, kind="ExternalOutput").ap()

    def dram_tmp(self, name, shape, dt):
        return self.nc.dram_tensor(name, list(shape), dt).ap()

    def sb(self, name, shape, dt=F32):
        self.uid += 1
        nm = "%s_%d" % (name, self.uid)
        return Tl(self.nc.alloc_sbuf_tensor(nm, list(shape), dt), nm)

    def sbuf_scope(self):
        return _Scope(self)

    def build(self):
        nc, K = self.nc, self.K
        I = {}
        I["xT_prev"] = self.dram_in("xT_prev", [D, T])
        I["xT_own"] = self.dram_in("xT_own", [D, T])
        I["pos_prev"] = self.dram_in("pos_prev", [32, T], I32)
        I["pos_own"] = self.dram_in("pos_own", [32, T], I32)
        I["c"] = self.dram_in("c", [128, DC])
        I["flags"] = self.dram_in("flags", [128, 4])
        I["poolfac"] = self.dram_in("poolfac", [128, 2, 4, 16])
        I["consts"] = self.dram_in("consts", [128, 128 + 8])
        I["ada_wT"] = self.dram_in("ada_wT", [DEPTH, 6 * D, D])
        I["c_rep"] = self.dram_in("c_rep", [128, D])
        I["ada_bT"] = self.dram_in("ada_bT", [128, DEPTH * 48])
        I["norm_gT"] = self.dram_in("norm_gT", [128, DEPTH * 4 * DC])
        I["w_dq"] = self.dram_in("w_dq", [2, D, QL])
        I["qn_gT"] = self.dram_in("qn_gT", [128, 2 * 3])
        I["w_uqA"] = self.dram_in("w_uqA", [2, QL, H * 96])
        I["w_uqB"] = self.dram_in("w_uqB", [2, QL, H * 32])
        I["w_dkvA"] = self.dram_in("w_dkvA", [2, D, 288])
        I["w_dkvB"] = self.dram_in("w_dkvB", [2, D, 32])
        I["kvn_gT"] = self.dram_in("kvn_gT", [128, 2 * 2])
        I["w_ukvK"] = self.dram_in("w_ukvK", [2, KVL, H * 64])
        I["w_ukvV"] = self.dram_in("w_ukvV", [2, KVL, H * 64])
        I["w_o"] = self.dram_in("w_o", [2, D, D])
        I["w_pw1"] = self.dram_in("w_pw1", [D, 2 * D])
        I["b_pw1T"] = self.dram_in("b_pw1T", [128, 16])
        I["w_dwT"] = self.dram_in("w_dwT", [128, DC, CW])
        I["cv_vecT"] = self.dram_in("cv_vecT", [128, 4 * DC])
        I["w_pw2"] = self.dram_in("w_pw2", [D, D])
        I["pool_w"] = self.dram_in("pool_w", [4, 256, 256])
        I["pl_vecT"] = self.dram_in("pl_vecT", [128, 2 * DC])
        I["ffn_w1"] = self.dram_in("ffn_w1", [DEPTH, D, DFF])
        I["ffn_w2"] = self.dram_in("ffn_w2", [DEPTH, DFF, D])
        self.I = I
        self.outT = self.dram_out("outT", [D, T])
        if self.debug:
            self.dbg = self.dram_out("dbg", [D, T])
        self.kcache = self.dram_tmp("kcache", [2, H, 96, 2 * T], BF16)
        self.vcache = self.dram_tmp("vcache", [2, H, 128, 32, 128], BF16)
        self.halo_u = self.dram_tmp("halo_u", [128, DC, 32], BF16)
        self.halo_h = self.dram_tmp("halo_h", [128, DC, 16], F32)
        self.w1s = self.dram_tmp("w1s", [DEPTH, 16, 128, DC * 256], BF16)
        self.w2s = self.dram_tmp("w2s", [DEPTH, DC, 128, 32 * 128], BF16)
        self.ffn_cached = [False] * DEPTH

        self.xT = self.sb("xT", [128, DC, T], F32)
        self.modT = self.sb("modT", [128, DEPTH * 48], F32)
        self.vec = self.sb("vec", [128, DEPTH * 6 * DC], F32)
        self.ngT = self.sb("ngT", [128, DEPTH * 4 * DC], F32)
        self.abT = self.sb("abT", [128, DEPTH * 48], F32)
        self.ident = self.sb("ident", [128, 128 + 8], F32)
        self.identb = self.sb("identb", [128, 128], BF16)
        self.onesb = self.sb("onesb", [128, 128], BF16)
        self.onesb_w = self.sb("onesb_w", [128, TB], BF16)
        self.flags = self.sb("flags", [128, 4], F32)
        self.kmax2 = self.sb("kmax2", [128, 2 * 2 * H], F32)
        self.epsc = self.sb("epsc", [128, 4], F32)
        self.uhalo = self.sb("uhalo", [128, DC, 32], BF16)
        self.hhalo = self.sb("hhalo", [128, DC, 16], F32)
        self.ps = [Tl(nc.alloc_psum_tensor("ps%d" % i, [128, 512], F32), "ps%d" % i) for i in range(8)]
        self.ps_rr = 0

        s_misc = K.dsem("misc")
        K.dma(s_misc, self.ident[:], I["consts"], writes=[self.ident.b()], hold=True)
        K.dma(s_misc, self.flags[:], I["flags"], writes=[self.flags.b()], hold=True)
        K.dma(s_misc, self.ngT[:], I["norm_gT"], writes=[self.ngT.b()], hold=True)
        K.dma(s_misc, self.abT[:], I["ada_bT"], writes=[self.abT.b()], hold=True)
        for nm_, shp_ in (("qn_gT", [128, 6]), ("kvn_gT", [128, 4]), ("b_pw1T", [128, 16]), ("cv_vecT", [128, 4 * DC]),
                          ("pl_vecT", [128, 2 * DC]), ("w_dwT", [128, DC, CW]), ("poolfac", [128, 2, 4, 16])):
            self.sbv(nm_, shp_)
        K.dma_flush(s_misc)
        K.op("dve", lambda e: e.tensor_copy(out=self.identb[:], in_=self.ident[:, 0:128]),
             reads=[self.ident.b()], writes=[self.identb.b()])
        K.op("dve", lambda e: e.memset(self.onesb[:], 1.0), writes=[self.onesb.b()])
        K.op("dve", lambda e: e.memset(self.onesb_w[:], 1.0), writes=[self.onesb_w.b()])
        K.op("dve", lambda e: e.memset(self.epsc[:, 0:1], EPS), writes=[self.epsc.b()])
        K.op("dve", lambda e: e.memset(self.kmax2[:], 0.0), writes=[self.kmax2.b()])

        self.prologue_mod()

        for pss, layers in ((0, self.layers_p1), (1, self.layers_p2)):
            if not layers:
                continue
            self.load_x(pss)
            for li in layers:
                self.layer(pss, li)
            if pss == 0 and 3 in self.layers_p2 and self.do_mixers:
                self.mla(0, 3, kv_only=True)
        self.store_out()
        K.final_wait()

    def psum(self, pool):
        lo, n = pool
        if not isinstance(self.ps_rr, dict):
            self.ps_rr = {}
        r = self.ps_rr.get(pool, 0)
        self.ps_rr[pool] = r + 1
        return self.ps[lo + (r % n)]

    def ada_gen(self, S, layers, nbuf=3):
        K, I = self.K, self.I
        crep = S.sb("crep", [128, D], F32)
        junk = S.sb("adajunk", [128, D], F32)
        stg = [S.sb("adastg%d" % i, [128, D], F32) for i in range(nbuf)]
        ss = [K.dsem("ada%d" % i) for i in range(nbuf)]
        K.dma(K.dsem("pm"), crep[:], I["c_rep"], writes=[crep.b()])
        K.op("act", lambda e: e.activation(out=crep[:], in_=crep[:], func=AF.Silu), reads=[crep.b()], writes=[crep.b()])
        n = 0
        for li in layers:
            wv = I["ada_wT"][li].rearrange("(j p) k -> j p k", p=128)
            for jc in range(48):
                st = stg[n % nbuf]
                col = li * 48 + jc
                K.dma(ss[n % nbuf], st[:], wv[jc], writes=[st.b()])
                K.op("dve", lambda e, st=st, col=col: e.scalar_tensor_tensor(out=junk[:], in0=st[:], scalar=1.0, in1=crep[:], op0=ALU.mult, op1=ALU.mult,
                                                                             accum_out=self.modT[:, col:col + 1]),
                     reads=[st.b(), crep.b()], writes=[junk.b(), self.modT.b()])
                n += 1
                yield
            c0 = li * 48
            K.op("dve", lambda e: e.tensor_tensor(out=self.modT[:, c0:c0 + 48], in0=self.modT[:, c0:c0 + 48], in1=self.abT[:, c0:c0 + 48], op=ALU.add),
                 reads=[self.modT.b(), self.abT.b()], writes=[self.modT.b()])
            m = lambda j: self.modT[:, li * 48 + j * DC: li * 48 + (j + 1) * DC]
            g = lambda j: self.ngT[:, (li * 4 + j) * DC:(li * 4 + j + 1) * DC]
            v = lambda j: self.vec[:, (li * 6 + j) * DC:(li * 6 + j + 1) * DC]
            rd = [self.modT.b(), self.ngT.b()]
            wr = [self.vec.b()]
            K.op("dve", lambda e: e.scalar_tensor_tensor(out=v(0), in0=m(1), scalar=1.0, in1=g(0), op0=ALU.add, op1=ALU.mult), reads=rd, writes=wr)
            K.op("dve", lambda e: e.tensor_copy(out=v(1), in_=m(0)), reads=rd, writes=wr)
            K.op("dve", lambda e: e.tensor_tensor(out=v(2), in0=m(2), in1=g(1), op=ALU.mult), reads=rd, writes=wr)
            K.op("dve", lambda e: e.scalar_tensor_tensor(out=v(3), in0=m(4), scalar=1.0, in1=g(2), op0=ALU.add, op1=ALU.mult), reads=rd, writes=wr)
            K.op("dve", lambda e: e.tensor_copy(out=v(4), in_=m(3)), reads=rd, writes=wr)
            K.op("dve", lambda e: e.tensor_tensor(out=v(5), in0=m(5), in1=g(3), op=ALU.mult), reads=rd, writes=wr)
            yield

    def prologue_mod(self):
        first = (self.layers_p1 + self.layers_p2)[0]
        rest = [l for l in sorted(set(self.layers_p1 + self.layers_p2)) if l != first]
        with self.sbuf_scope() as S:
            for _ in self.ada_gen(S, [first], nbuf=4):
                pass
        self.ada_pending = set(rest)

    def ada_take(self, layers):
        got = [l for l in layers if l in self.ada_pending]
        for l in got:
            self.ada_pending.discard(l)
        return got

    def ada_eager(self, li):
        if li in self.ada_pending:
            self.ada_pending.discard(li)
            with self.sbuf_scope() as S:
                for _ in self.ada_gen(S, [li], nbuf=4):
                    pass

    def lvec(self, li, j, c):
        o = (li * 6 + j) * DC + c
        return self.vec[:, o:o + 1]

    def xb(self, tb):
        return self.xT.b(tb)

    def load_x(self, pss):
        K = self.K
        src = self.I["xT_prev" if pss == 0 else "xT_own"].rearrange("(c p) t -> p c t", p=128)
        sem = K.dsem("ldx")
        for tb in range(NTB):
            K.dma(sem, self.xT[:, :, tb * TB:(tb + 1) * TB], src[:, :, tb * TB:(tb + 1) * TB], writes=[self.xb(tb)], hold=(tb < NTB - 1))

    def store_out(self):
        K = self.K
        dst = self.outT.rearrange("(c p) t -> p c t", p=128)
        sem = K.dsem("stx")
        for tb in range(NTB):
            K.dma(sem, dst[:, :, tb * TB:(tb + 1) * TB], self.xT[:, :, tb * TB:(tb + 1) * TB], reads=[self.xb(tb)], hold=(tb < NTB - 1))

    def sq_ops(self, sq, c, src_ap, src_bufs, eng):
        K = self.K
        if eng == "act":
            K.op("act", lambda e: e.activation(out=sq[:, c, :], in_=src_ap, func=AF.Square),
                 reads=src_bufs, writes=[sq.b(c)])
        else:
            K.op(eng, lambda e: e.tensor_tensor(out=sq[:, c, :], in0=src_ap, in1=src_ap, op=ALU.mult),
                 reads=src_bufs, writes=[sq.b(c)])

    def norm_ws(self, S, nset=1, pn=True, width=TB):
        ws = {"sets": [], "i": 0, "pn": [], "j": 0}
        for k in range(nset):
            ws["sets"].append((S.sb("ws_sq%d" % k, [128, DC, width], BF16), S.sb("ws_ln%d" % k, [128, width], F32),
                               S.sb("ws_rs%d" % k, [128, width], F32)))
        if pn:
            ws["pn"] = [S.sb("ws_pn%d" % k, [128, width], F32) for k in range(3)]
        return ws

    def ws_set(self, ws):
        st = ws["sets"][ws["i"] % len(ws["sets"])]
        ws["i"] += 1
        return st

    def rstd_from_sq(self, ws_set, sq, nchunk, n, width, kparts=128):
        K = self.K
        _, tmp, rstd = ws_set
        pp = self.psum((6, 2))
        for c in range(nchunk):
            K.op("pe", lambda e, c=c: e.matmul(pp[:, 0:width], lhsT=self.onesb[0:kparts, :], rhs=sq[0:kparts, c, 0:width],
                                                 start=(c == 0), stop=(c == nchunk - 1)),
                 reads=[sq.b(c), self.onesb.b()], writes=[pp.b()], inc=(c == nchunk - 1))
        K.op("act", lambda e: e.activation(out=tmp[:, 0:width], in_=pp[:, 0:width], func=AF.Ln, bias=self.epsc[:, 0:1], scale=1.0 / n),
             reads=[pp.b(), self.epsc.b()], writes=[tmp.b()])
        K.op("act", lambda e: e.activation(out=rstd[:, 0:width], in_=tmp[:, 0:width], func=AF.Exp, scale=-0.5),
             reads=[tmp.b()], writes=[rstd.b()])
        return rstd

    def prenorm(self, ws, li, sub, tb, hT, hcol0, hkey):
        K = self.K
        st = self.ws_set(ws)
        sq = st[0]
        xs = slice(tb * TB, (tb + 1) * TB)
        for c in range(DC):
            self.sq_ops(sq, c, self.xT[:, c, xs], [self.xb(tb)], "dve" if c % 4 == 3 else "act")
        rstd = self.rstd_from_sq(st, sq, DC, D, TB)
        ja, jb = (0, 1) if sub == 0 else (3, 4)
        for c in range(DC):
            tmp = ws["pn"][ws["j"] % 3]
            ws["j"] += 1
            K.op("dve", lambda e, c=c, tmp=tmp: e.tensor_tensor(out=tmp[:], in0=self.xT[:, c, xs], in1=rstd[:], op=ALU.mult),
                 reads=[self.xb(tb), rstd.b()], writes=[tmp.b()])
            if c % 4 != 0:
                K.op("act", lambda e, c=c, tmp=tmp: e.activation(out=hT[:, c, hcol0:hcol0 + TB], in_=tmp[:], func=AF.Identity,
                                                                  bias=self.lvec(li, jb, c), scale=self.lvec(li, ja, c)),
                     reads=[tmp.b(), self.vec.b()], writes=[hT.b((hkey, c))])
            else:
                K.op("dve", lambda e, c=c, tmp=tmp: e.tensor_scalar(out=hT[:, c, hcol0:hcol0 + TB], in0=tmp[:],
                                                                     scalar1=self.lvec(li, ja, c), scalar2=self.lvec(li, jb, c),
                                                                     op0=ALU.mult, op1=ALU.add),
                     reads=[tmp.b(), self.vec.b()], writes=[hT.b((hkey, c))])

    def postnorm_residual(self, ws, li, sub, tb, yb):
        K = self.K
        st = self.ws_set(ws)
        sq = st[0]
        for c in range(DC):
            self.sq_ops(sq, c, yb[:, c, :], [yb.b(c)], "act")
        rstd = self.rstd_from_sq(st, sq, DC, D, TB)
        xs = slice(tb * TB, (tb + 1) * TB)
        jg = 2 if sub == 0 else 5
        for c in range(DC):
            K.op("dve", lambda e, c=c: e.tensor_tensor(out=yb[:, c, :], in0=yb[:, c, :], in1=rstd[:], op=ALU.mult),
                 reads=[yb.b(c), rstd.b()], writes=[yb.b(c)])
            K.op("dve", lambda e, c=c: e.scalar_tensor_tensor(out=self.xT[:, c, xs], in0=yb[:, c, :], scalar=self.lvec(li, jg, c),
                                                              in1=self.xT[:, c, xs], op0=ALU.mult, op1=ALU.add),
                 reads=[yb.b(c), self.vec.b(), self.xb(tb)], writes=[self.xb(tb)])

    def cast(self, out_ap, in_ap, rbufs, wbufs):
        K = self.K
        n = getattr(self, "_ncast", 0)
        self._ncast = n + 1
        if n % 2 == 0:
            K.op("dve", lambda e: e.tensor_copy(out=out_ap, in_=in_ap), reads=rbufs, writes=wbufs)
        else:
            K.op("act", lambda e: e.activation(out=out_ap, in_=in_ap, func=AF.Identity), reads=rbufs, writes=wbufs)

    def evac(self, eng, out_ap, pp, wbufs):
        K = self.K
        if eng == "act":
            K.op("act", lambda e: e.activation(out=out_ap, in_=pp[:], func=AF.Identity), reads=[pp.b()], writes=wbufs)
        else:
            K.op("dve", lambda e: e.tensor_copy(out=out_ap, in_=pp[:]), reads=[pp.b()], writes=wbufs)

    def ffn(self, pss, li):
        K, I = self.K, self.I
        w1v = I["ffn_w1"][li].rearrange("(kc p) n -> p kc n", p=128)
        w2v = I["ffn_w2"][li].rearrange("(hc p) n -> p hc n", p=128)
        ssem = [K.dsem("fw%d" % i) for i in range(2)]
        stsem = [K.dsem("fws%d" % i) for i in range(2)]
        for sbk in range(2):
            first = not self.ffn_cached[li]
            self.ffn_cached[li] = True
            with self.sbuf_scope() as S:
                hid = S.sb("hid", [128, 32, 2 * TB], BF16)
                with self.sbuf_scope() as S1:
                    hT = S1.sb("hT", [128, DC, 2 * TB], BF16)
                    nwb = 2 if first else 4
                    stg = [S1.sb("fstg%d" % i, [128, DC, 256], F32) for i in range(2)] if first else None
                    ws = self.norm_ws(S1, nset=1 if first else 2)
                    w1b = [S1.sb("w1b%d" % i, [128, DC, 256], BF16) for i in range(nwb)]
                    lsem4 = [K.dsem("fwl%d" % i) for i in range(4)]
                    rl = [S1.sb("rl%d" % i, [128, TB], BF16) for i in range(3)]
                    for t2 in range(2):
                        self.prenorm(ws, li, 1, sbk * 2 + t2, hT, t2 * TB, t2)
                    nrl = 0
                    for pc in range(16):
                        wb = w1b[pc % nwb]
                        st = stg[pc % 2] if first else None
                        cb = self._cb("w1s", li, pc, 0)
                        if first:
                            K.dma(ssem[pc % 2], st[:], w1v[:, :, pc * 256:(pc + 1) * 256], writes=[st.b()])
                            self.cast(wb[:], st[:], [st.b()], [wb.b()])
                            K.dma(stsem[pc % 2], self.w1s[li, pc].rearrange("p (k n) -> p k n", k=DC), wb[:], reads=[wb.b()], writes=[cb])
                        else:
                            K.dma(lsem4[pc % 4], wb[:], self.w1s[li, pc].rearrange("p (k n) -> p k n", k=DC), reads=[cb], writes=[wb.b()])
                        for hc2 in range(2):
                            hc = pc * 2 + hc2
                            for t2 in range(2):
                                pp = self.psum((0, 4))
                                for kc in range(DC):
                                    K.op("pe", lambda e, wb=wb, kc=kc, hc2=hc2, t2=t2, pp=pp: e.matmul(
                                        pp[:], lhsT=wb[:, kc, hc2 * 128:(hc2 + 1) * 128], rhs=hT[:, kc, t2 * TB:(t2 + 1) * TB],
                                        start=(kc == 0), stop=(kc == DC - 1)),
                                        reads=[wb.b(), hT.b((t2, kc))], writes=[pp.b()], inc=(kc == DC - 1))
                                r = rl[nrl % 3]
                                nrl += 1
                                K.op("act", lambda e, pp=pp, r=r: e.activation(out=r[:], in_=pp[:], func=AF.Relu),
                                     reads=[pp.b()], writes=[r.b()])
                                K.op("dve", lambda e, r=r, hc=hc, t2=t2: e.tensor_tensor(out=hid[:, hc, t2 * TB:(t2 + 1) * TB], in0=r[:], in1=r[:], op=ALU.mult),
                                     reads=[r.b()], writes=[hid.b((hc, t2))])
                with self.sbuf_scope() as S1:
                    stg = [S1.sb("gstg%d" % i, [128, 8, 128], F32) for i in range(2)] if first else None
                    w2b = [S1.sb("w2b%d" % i, [128, 32, 128], BF16) for i in range(2)]
                    yb = [S1.sb("yb%d" % i, [128, DC, TB], F32) for i in range(2)]
                    ws = self.norm_ws(S1, nset=1, pn=False)
                    npc = 0
                    for oc in range(DC):
                        wb = w2b[oc % 2]
                        cb = self._cb("w2s", li, oc, 0)
                        if first:
                            for hh in range(4):
                                st = stg[npc % 2]
                                K.dma(ssem[npc % 2], st[:], w2v[:, hh * 8:(hh + 1) * 8, oc * 128:(oc + 1) * 128], writes=[st.b()])
                                self.cast(wb[:, hh * 8:(hh + 1) * 8, :], st[:], [st.b()], [wb.b(hh)])
                                npc += 1
                            K.dma(stsem[oc % 2], self.w2s[li, oc].rearrange("p (k n) -> p k n", k=32), wb[:], reads=wb.bs(range(4)), writes=[cb])
                        else:
                            K.dma(ssem[oc % 2], wb[:], self.w2s[li, oc].rearrange("p (k n) -> p k n", k=32), reads=[cb], writes=wb.bs(range(4)))
                        for t2 in range(2):
                            pp = self.psum((0, 4))
                            for hc in range(32):
                                K.op("pe", lambda e, wb=wb, hc=hc, t2=t2, pp=pp: e.matmul(
                                    pp[:], lhsT=wb[:, hc, :], rhs=hid[:, hc, t2 * TB:(t2 + 1) * TB],
                                    start=(hc == 0), stop=(hc == 31)),
                                    reads=[wb.b(hc // 8), hid.b((hc, t2))], writes=[pp.b()], inc=(hc == 31))
                            self.evac("act" if (oc + t2) % 2 == 0 else "dve", yb[t2][:, oc, :], pp, [yb[t2].b(oc)])
                    for t2 in range(2):
                        self.postnorm_residual(ws, li, 1, sbk * 2 + t2, yb[t2])

    def layer(self, pss, li):
        kind = li % 3
        self.ada_eager(li)
        if self.do_mixers:
            if kind == 0:
                self.mla(pss, li)
            elif kind == 1:
                self.conv(pss, li)
            else:
                self.pool(pss, li)
        if self.do_ffn:
            self.ffn(pss, li)

    def load_w_bf16(self, dst, src3, stg, sems, key=0, cache=None):
        K = self.K
        A, N = src3.shape[1], src3.shape[2]
        if cache is not None:
            if not hasattr(self, "_wc"):
                self._wc = {}
            if cache in self._wc:
                scr, cb = self._wc[cache]
                self._wcn = getattr(self, "_wcn", 0) + 1
                K.dma(K.dsem("wcl%d" % (self._wcn % 4)), dst[:, 0:A, :], scr.rearrange("p (a n) -> p a n", n=N), reads=[cb], writes=[dst.b(key)])
                return
        cap = stg[0].t.shape[-1]
        per = max(1, cap // N)
        assert N <= cap
        a0 = 0
        n = getattr(self, "_lw", 0)
        while a0 < A:
            a1 = min(A, a0 + per)
            st = stg[n % len(stg)]
            view = st[:, 0:(a1 - a0) * N].rearrange("p (a n) -> p a n", n=N)
            K.dma(sems[n % len(stg)], view, src3[:, a0:a1, :], writes=[st.b()])
            self.cast(dst[:, a0:a1, :], view, [st.b()], [dst.b(key)])
            n += 1
            a0 = a1
        self._lw = n
        if cache is not None:
            scr = self.dram_tmp("wc_" + cache, [128, A * N], BF16)
            cb = Buf("wc_" + cache)
            self._wc[cache] = (scr, cb)
            K.dma(K.dsem("wcs%d" % (len(self._wc) % 2)), scr.rearrange("p (a n) -> p a n", n=N), dst[:, 0:A, :], reads=[dst.b(key)], writes=[cb])

    def rope_tables(self, S, pss, C32, S32):
        K = self.K
        pos = self.I["pos_prev" if pss == 0 else "pos_own"]
        TWO_PI = 2.0 * np.pi
        C1 = 6.28125
        C2 = TWO_PI - C1
        pi_ = S.sb("posi", [32, T], I32)
        ang = S.sb("ang", [32, T], F32)
        ki = S.sb("ki", [32, T], I32)
        kf = S.sb("kf", [32, T], F32)
        r = S.sb("rr", [32, T], F32)
        sem = K.dsem("misc")
        K.dma(sem, pi_[:], pos, writes=[pi_.b()])
        K.op("dve", lambda e: e.tensor_copy(out=ang[:], in_=pi_[:]), reads=[pi_.b()], writes=[ang.b()])
        K.op("dve", lambda e: e.tensor_scalar(out=ang[:], in0=ang[:], scalar1=self.ident[0:32, 128:129], scalar2=None, op0=ALU.mult),
             reads=[ang.b(), self.ident.b()], writes=[ang.b()])
        for which in range(2):
            off = 0.0 if which == 0 else 0.5 * np.pi
            K.op("dve", lambda e: e.tensor_scalar(out=ki[:], in0=ang[:], scalar1=float(off), scalar2=float(1.0 / TWO_PI), op0=ALU.add, op1=ALU.mult),
                 reads=[ang.b()], writes=[ki.b()])
            K.op("dve", lambda e: e.tensor_copy(out=kf[:], in_=ki[:]), reads=[ki.b()], writes=[kf.b()])
            K.op("dve", lambda e: e.scalar_tensor_tensor(out=r[:], in0=kf[:], scalar=float(-C1), in1=ang[:], op0=ALU.mult, op1=ALU.add),
                 reads=[kf.b(), ang.b()], writes=[r.b()])
            K.op("dve", lambda e: e.scalar_tensor_tensor(out=r[:], in0=kf[:], scalar=float(-C2), in1=r[:], op0=ALU.mult, op1=ALU.add),
                 reads=[kf.b(), r.b()], writes=[r.b()])
            K.op("dve", lambda e: e.tensor_scalar(out=r[:], in0=r[:], scalar1=float(off), scalar2=3.1415925, op0=ALU.add, op1=ALU.min),
                 reads=[r.b()], writes=[r.b()])
            K.op("dve", lambda e: e.tensor_scalar(out=r[:], in0=r[:], scalar1=-3.1415925, scalar2=None, op0=ALU.max),
                 reads=[r.b()], writes=[r.b()])
            if which == 0:
                for lo in (0, 64):
                    K.op("act", lambda e, lo=lo: e.activation(out=S32[lo:lo + 32, :], in_=r[:], func=AF.Sin, scale=self.ident[0:32, 129:130]),
                         reads=[r.b(), self.ident.b()], writes=[S32.b()])
            else:
                for lo in (0, 64):
                    K.op("act", lambda e, lo=lo: e.activation(out=C32[lo:lo + 32, :], in_=r[:], func=AF.Sin),
                         reads=[r.b()], writes=[C32.b()])

    def mla(self, pss, li, kv_only=False):
        K, I = self.K, self.I
        j = li // 3
        NKB_OWN = T // 128
        NK = T * (pss + 1)
        NKB = NK // 128
        prevb = self.flags[:, 0:1]
        wsem = [K.dsem("mw%d" % i) for i in range(2)]
        csems = [K.dsem("cache_st%d" % i) for i in range(4)]
        lsem = [K.dsem("cache_ld%d" % i) for i in range(2)]
        with self.sbuf_scope() as SM:
            cqn = SM.sb("cqn", [128, 3, T], BF16) if not kv_only else None
            C32 = SM.sb("C32", [96, T], BF16)
            S32 = SM.sb("S32", [96, T], BF16)
            ao = SM.sb("ao", [128, DC, T], BF16) if not kv_only else None
            with self.sbuf_scope() as SK:
                ckvn = SK.sb("ckvn", [128, 2, T], BF16)
                krope = SK.sb("krope", [96, T], BF16)
                with self.sbuf_scope() as S1:
                    with self.sbuf_scope() as SR:
                        self.rope_tables(SR, pss, C32, S32)
                    stg = [S1.sb("mstg%d" % i, [128, 1024], F32) for i in range(2)]
                    wdq = S1.sb("wdq", [128, DC, QL], BF16)
                    wdkv = S1.sb("wdkv", [128, DC, 320], BF16)
                    self.load_w_bf16(wdq, I["w_dq"][j].rearrange("(kc p) n -> p kc n", p=128), stg, wsem, cache="wdq%d" % j)
                    dkA = I["w_dkvA"][j].rearrange("(kc p) n -> p kc n", p=128)
                    dkB = I["w_dkvB"][j].rearrange("(kc p) n -> p kc n", p=128)
                    self.load_w_bf16(_Vn(wdkv, 0, 288), dkA, stg, wsem, cache="wdkvA%d" % j)
                    self.load_w_bf16(_Vn(wdkv, 288, 320), dkB, stg, wsem, cache="wdkvB%d" % j)
                    gq = self.sbv("qn_gT", [128, 6])
                    gkv = self.sbv("kvn_gT", [128, 4])
                    hTs = [S1.sb("mhT%d" % i, [128, DC, TB], BF16) for i in range(2)]
                    ws = self.norm_ws(S1, nset=1)
                    sqqs = [S1.sb("sqq%d" % i, [128, 3, TB], BF16) for i in range(1)] * 2
                    sqks = [S1.sb("sqk%d" % i, [128, 2, TB], BF16) for i in range(1)] * 2
                    kr1 = [S1.sb("kr1_%d" % i, [32, TB], F32) for i in range(1)] * 2
                    kr2 = [S1.sb("kr2_%d" % i, [32, TB], F32) for i in range(1)] * 2
                    for tb in range(NTB):
                        ts = slice(tb * TB, (tb + 1) * TB)
                        hT = hTs[tb % 2]
                        self.prenorm(ws, li, 0, tb, hT, 0, 0)
                        if True:
                            sqq = sqqs[tb % 2]
                            pq = []
                            for oc in range(0 if kv_only else 3):
                                pp = self.psum((0, 4))
                                pq.append(pp)
                                for kc in range(DC):
                                    K.op("pe", lambda e, pp=pp, oc=oc, kc=kc: e.matmul(pp[:], lhsT=wdq[:, kc, oc * 128:(oc + 1) * 128], rhs=hT[:, kc, :],
                                                                                         start=(kc == 0), stop=(kc == DC - 1)),
                                         reads=[wdq.b(), hT.b((0, kc))], writes=[pp.b()], inc=(kc == DC - 1))
                                self.sq_ops(sqq, oc, pp[:], [pp.b()], "act")
                            rstd = self.rstd_from_sq(self.ws_set(ws), sqq, 3, QL, TB) if not kv_only else None
                            for oc in range(0 if kv_only else 3):
                                K.op("dve", lambda e, oc=oc: e.scalar_tensor_tensor(out=cqn[:, oc, ts], in0=pq[oc][:], scalar=gq[:, j * 3 + oc:j * 3 + oc + 1],
                                                                                      in1=rstd[:], op0=ALU.mult, op1=ALU.mult),
                                     reads=[pq[oc].b(), rstd.b(), gq.b()], writes=[cqn.b(tb)])
                            sqk = sqks[tb % 2]
                            pk = []
                            for oc in range(2):
                                pp = self.psum((0, 4))
                                pk.append(pp)
                                for kc in range(DC):
                                    K.op("pe", lambda e, pp=pp, oc=oc, kc=kc: e.matmul(pp[:], lhsT=wdkv[:, kc, oc * 128:(oc + 1) * 128], rhs=hT[:, kc, :],
                                                                                         start=(kc == 0), stop=(kc == DC - 1)),
                                         reads=[wdkv.b(), hT.b((0, kc))], writes=[pp.b()], inc=(kc == DC - 1))
                                self.sq_ops(sqk, oc, pp[:], [pp.b()], "act")
                            rstd2 = self.rstd_from_sq(self.ws_set(ws), sqk, 2, KVL, TB)
                            for oc in range(2):
                                K.op("dve", lambda e, oc=oc: e.scalar_tensor_tensor(out=ckvn[:, oc, ts], in0=pk[oc][:], scalar=gkv[:, j * 2 + oc:j * 2 + oc + 1],
                                                                                      in1=rstd2[:], op0=ALU.mult, op1=ALU.mult),
                                     reads=[pk[oc].b(), rstd2.b(), gkv.b()], writes=[ckvn.b(tb)])
                            pr = []
                            for w in range(2):
                                pp = self.psum((4, 2))
                                pr.append(pp)
                                for kc in range(DC):
                                    K.op("pe", lambda e, pp=pp, w=w, kc=kc: e.matmul(pp[0:32, :], lhsT=wdkv[:, kc, 256 + 32 * w:288 + 32 * w], rhs=hT[:, kc, :],
                                                                                       start=(kc == 0), stop=(kc == DC - 1)),
                                         reads=[wdkv.b(), hT.b((0, kc))], writes=[pp.b()], inc=(kc == DC - 1))
                            t1 = kr1[tb % 2]
                            t2 = kr2[tb % 2]
                            K.op("dve", lambda e: e.tensor_tensor(out=t1[:], in0=pr[0][0:32, :], in1=C32[0:32, ts], op=ALU.mult),
                                 reads=[pr[0].b(), C32.b()], writes=[t1.b()])
                            K.op("dve", lambda e: e.tensor_tensor(out=t2[:], in0=pr[1][0:32, :], in1=S32[0:32, ts], op=ALU.mult),
                                 reads=[pr[1].b(), S32.b()], writes=[t2.b()])
                            K.op("dve", lambda e: e.tensor_tensor(out=krope[64:96, ts], in0=t1[:], in1=t2[:], op=ALU.add),
                                 reads=[t1.b(), t2.b()], writes=[krope.b(tb)])
                with self.sbuf_scope() as S1:
                    stg = [S1.sb("mstg%d" % i, [128, 1024], F32) for i in range(2)]
                    wk = S1.sb("wukvK", [128, 2, H * 64], BF16)
                    wv = S1.sb("wukvV", [128, 2, H * 64], BF16)
                    for hh in range(0, H, 8):
                        self.load_w_bf16(_Vn(wk, hh * 64, (hh + 8) * 64), I["w_ukvK"][j].rearrange("(kc p) n -> p kc n", p=128)[:, :, hh * 64:(hh + 8) * 64], stg, wsem, key=("k", hh), cache="wk%d_%d" % (j, hh))
                    for hh in range(0, H, 8):
                        self.load_w_bf16(_Vn(wv, hh * 64, (hh + 8) * 64), I["w_ukvV"][j].rearrange("(kc p) n -> p kc n", p=128)[:, :, hh * 64:(hh + 8) * 64], stg, wsem, key=("v", hh), cache="wv%d_%d" % (j, hh))
                    khs = [S1.sb("kh%d" % i, [96, T], BF16) for i in range(2)]
                    vhs = [S1.sb("vh%d" % i, [128, NKB_OWN, 128], BF16) for i in range(2)]
                    sqkhs = [S1.sb("sqkh%d" % i, [96, T], BF16) for i in range(2)]
                    mx4s = [S1.sb("mx4_%d" % i, [128, 4], F32) for i in range(2)]
                    for v in vhs:
                        K.op("pool", lambda e, v=v: e.memset(v[:, :, 64:128], 1.0), writes=[v.b("ones")])
                    for h in range(H):
                        kh, vh = khs[h % 2], vhs[h % 2]
                        sqkh, mx4 = sqkhs[h % 2], mx4s[h % 2]
                        K.op("act", lambda e, kh=kh: e.activation(out=kh[64:96, :], in_=krope[64:96, :], func=AF.Identity),
                             reads=krope.bs(range(NTB)), writes=[kh.b("r")])
                        for tb in range(NTB):
                            ts = slice(tb * TB, (tb + 1) * TB)
                            pp = self.psum((0, 4))
                            for c in range(2):
                                K.op("pe", lambda e, pp=pp, c=c: e.matmul(pp[0:64, :], lhsT=wk[:, c, h * 64:(h + 1) * 64], rhs=ckvn[:, c, ts],
                                                                            start=(c == 0), stop=(c == 1)),
                                     reads=[wk.b(("k", 0)), wk.b(("k", 8)), ckvn.b(tb)], writes=[pp.b()], inc=(c == 1))
                            if tb % 2 == 0:
                                K.op("dve", lambda e, pp=pp: e.tensor_copy(out=kh[0:64, ts], in_=pp[0:64, :]), reads=[pp.b()], writes=[kh.b(tb)])
                            else:
                                K.op("act", lambda e, pp=pp: e.activation(out=kh[0:64, ts], in_=pp[0:64, :], func=AF.Identity), reads=[pp.b()], writes=[kh.b(tb)])
                        K.dma(csems[(2 * h) % 4], self.kcache[j, h, :, pss * T:(pss + 1) * T], kh[:], reads=kh.bs(["r", 0, 1, 2, 3]), writes=[self.kcb(j, h, pss)])
                        for g in range(2):
                            pp = self.psum((0, 4))
                            for kb in range(8):
                                blk = g * 8 + kb
                                for c in range(2):
                                    K.op("pe", lambda e, pp=pp, kb=kb, blk=blk, c=c: e.matmul(pp[:, kb * 64:(kb + 1) * 64], lhsT=ckvn[:, c, blk * 128:(blk + 1) * 128],
                                                                                                rhs=wv[:, c, h * 64:(h + 1) * 64], start=(c == 0), stop=(c == 1)),
                                         reads=[wv.b(("v", 0)), wv.b(("v", 8)), ckvn.b(blk // 4)], writes=[pp.b()], inc=(kb == 7 and c == 1))
                            src = pp[:, :].rearrange("p (k d) -> p k d", d=64)
                            if g == 0:
                                K.op("dve", lambda e, src=src: e.tensor_copy(out=vh[:, 0:8, 0:64], in_=src), reads=[pp.b()], writes=[vh.b(0)])
                            else:
                                K.op("act", lambda e, src=src: e.activation(out=vh[:, 8:16, 0:64], in_=src, func=AF.Identity), reads=[pp.b()], writes=[vh.b(1)])
                        K.dma(csems[(2 * h + 1) % 4], self.vcache[j, h, :, pss * NKB_OWN:(pss + 1) * NKB_OWN, :], vh[:], reads=vh.bs(["ones", 0, 1]), writes=[self.vcb(j, h, pss)])
                        K.op("dve", lambda e: e.tensor_tensor(out=sqkh[:], in0=kh[:], in1=kh[:], op=ALU.mult), reads=kh.bs(["r", 0, 1, 2, 3]), writes=[sqkh.b()])
                        for tb in range(NTB):
                            ts = slice(tb * TB, (tb + 1) * TB)
                            pp = self.psum((6, 2))
                            K.op("pe", lambda e, pp=pp: e.matmul(pp[:], lhsT=self.onesb[0:96, :], rhs=sqkh[0:96, ts], start=True, stop=True),
                                 reads=[sqkh.b(), self.onesb.b()], writes=[pp.b()])
                            K.op("dve", lambda e, pp=pp, tb=tb: e.tensor_reduce(out=mx4[:, tb:tb + 1], in_=pp[:], axis=AX.X, op=ALU.max),
                                 reads=[pp.b()], writes=[mx4.b()])
                        kcol = (j * 2 + pss) * H + h
                        K.op("dve", lambda e: e.tensor_reduce(out=self.kmax2[:, kcol:kcol + 1], in_=mx4[:], axis=AX.X, op=ALU.max),
                             reads=[mx4.b()], writes=[self.kmax2.b()])
            if kv_only:
                return
            with self.sbuf_scope() as S1:
                wqa = S1.sb("wuqA", [128, 3, H * 96], BF16)
                wqb = S1.sb("wuqB", [128, 3, H * 32], BF16)
                with self.sbuf_scope() as SG:
                    stg = [SG.sb("mstg%d" % i, [128, 1024], F32) for i in range(2)]
                    for hh in range(0, H, 8):
                        self.load_w_bf16(_Vn(wqa, hh * 96, (hh + 8) * 96), I["w_uqA"][j].rearrange("(kc p) n -> p kc n", p=128)[:, :, hh * 96:(hh + 8) * 96], stg, wsem, cache="wqa%d_%d" % (j, hh))
                    self.load_w_bf16(wqb, I["w_uqB"][j].rearrange("(kc p) n -> p kc n", p=128), stg, wsem, cache="wqb%d" % j)
                khf = [S1.sb("khf%d" % i, [96, NK], BF16) for i in range(2)]
                vhf = [S1.sb("vhf%d" % i, [128, NKB, 128], BF16) for i in range(2)]
                qhs = [S1.sb("qh%d" % i, [96, T], BF16) for i in range(2)]
                sqqh = S1.sb("sqqh", [96, T], BF16)
                mxq = S1.sb("mxq", [128, 4], F32)
                sc = [S1.sb("attsc%d" % i, [128, 8], F32) for i in range(2)]
                pts = [S1.sb("pt%d" % i, [128, TB], BF16) for i in range(4)]
                ptd = [S1.sb("ptd%d" % i, [128, TB], BF16) for i in range(4)]
                r1 = [S1.sb("qr1_%d" % i, [96, TB], F32) for i in range(2)]
                r2 = [S1.sb("qr2_%d" % i, [96, TB], F32) for i in range(2)]
                rec = [S1.sb("rec%d" % i, [64, TB], F32) for i in range(2)]
                for dj in range(4):
                    K.op("pool", lambda e, dj=dj: e.memset(ptd[dj][:], 0.0), writes=[ptd[dj].b()])
                cnt = {"pt": 0, "rec": 0}

                def prep(h):
                    kf_, vf_, qh, s_ = khf[h % 2], vhf[h % 2], qhs[h % 2], sc[h % 2]
                    K.dma(lsem[h % 2], kf_[:], self.kcache[j, h, :, 0:NK], reads=[self.kcb(j, h, p_) for p_ in range(pss + 1)], writes=[kf_.b()], hold=True)
                    K.dma(lsem[h % 2], vf_[:], self.vcache[j, h, :, 0:NKB, :], reads=[self.vcb(j, h, p_) for p_ in range(pss + 1)], writes=[vf_.b()])
                    yield
                    for tb in range(NTB):
                        ts = slice(tb * TB, (tb + 1) * TB)
                        pa = self.psum((6, 1))
                        for c in range(3):
                            K.op("pe", lambda e, c=c: e.matmul(pa[0:96, :], lhsT=wqa[:, c, h * 96:(h + 1) * 96], rhs=cqn[:, c, ts], start=(c == 0), stop=(c == 2)),
                                 reads=[wqa.b(), cqn.b(tb)], writes=[pa.b()], inc=(c == 2))
                        pb = self.psum((7, 1))
                        for c in range(3):
                            K.op("pe", lambda e, c=c: e.matmul(pb[0:32, :], lhsT=wqb[:, c, h * 32:(h + 1) * 32], rhs=cqn[:, c, ts], start=(c == 0), stop=(c == 2)),
                                 reads=[wqb.b(), cqn.b(tb)], writes=[pb.b()], inc=(c == 2))
                        a1, a2 = r1[tb % 2], r2[tb % 2]
                        K.op("dve", lambda e: e.tensor_tensor(out=a1[64:96, :], in0=pa[64:96, :], in1=C32[64:96, ts], op=ALU.mult), reads=[pa.b(), C32.b()], writes=[a1.b()])
                        K.op("dve", lambda e: e.tensor_tensor(out=a2[64:96, :], in0=pb[0:32, :], in1=S32[0:32, ts], op=ALU.mult), reads=[pb.b(), S32.b()], writes=[a2.b()])
                        K.op("dve", lambda e: e.tensor_tensor(out=qh[64:96, ts], in0=a1[64:96, :], in1=a2[64:96, :], op=ALU.add), reads=[a1.b(), a2.b()], writes=[qh.b((tb, "r"))])
                        K.op("dve", lambda e: e.tensor_copy(out=qh[0:64, ts], in_=pa[0:64, :]), reads=[pa.b()], writes=[qh.b((tb, "n"))])
                        yield
                    K.op("dve", lambda e: e.tensor_tensor(out=sqqh[:], in0=qh[:], in1=qh[:], op=ALU.mult), reads=qh.bs([(t_, x_) for t_ in range(NTB) for x_ in "rn"]), writes=[sqqh.b()])
                    for tb in range(NTB):
                        ts = slice(tb * TB, (tb + 1) * TB)
                        pp = self.psum((6, 1))
                        K.op("pe", lambda e: e.matmul(pp[:], lhsT=self.onesb[0:96, :], rhs=sqqh[0:96, ts], start=True, stop=True),
                             reads=[sqqh.b(), self.onesb.b()], writes=[pp.b()])
                        K.op("dve", lambda e: e.tensor_reduce(out=mxq[:, tb:tb + 1], in_=pp[:], axis=AX.X, op=ALU.max), reads=[pp.b()], writes=[mxq.b()])
                    yield
                    K.op("dve", lambda e: e.tensor_reduce(out=s_[:, 0:1], in_=mxq[:], axis=AX.X, op=ALU.max), reads=[mxq.b()], writes=[s_.b()])
                    k0 = (j * 2 + 0) * H + h
                    k1 = (j * 2 + pss) * H + h
                    K.op("dve", lambda e: e.tensor_tensor(out=s_[:, 1:2], in0=self.kmax2[:, k0:k0 + 1], in1=self.kmax2[:, k1:k1 + 1], op=ALU.max),
                         reads=[self.kmax2.b()], writes=[s_.b()])
                    K.op("dve", lambda e: e.scalar_tensor_tensor(out=s_[:, 2:3], in0=s_[:, 0:1], scalar=1e-12, in1=s_[:, 1:2], op0=ALU.max, op1=ALU.mult),
                         reads=[s_.b()], writes=[s_.b()])
                    K.op("dve", lambda e: e.tensor_scalar(out=s_[:, 2:3], in0=s_[:, 2:3], scalar1=1e-12, scalar2=None, op0=ALU.max), reads=[s_.b()], writes=[s_.b()])
                    K.op("act", lambda e: e.activation(out=s_[:, 3:4], in_=s_[:, 2:3], func=AF.Ln), reads=[s_.b()], writes=[s_.b()])
                    K.op("act", lambda e: e.activation(out=s_[:, 4:5], in_=s_[:, 3:4], func=AF.Exp, scale=0.5), reads=[s_.b()], writes=[s_.b()])
                    K.op("dve", lambda e: e.tensor_scalar(out=s_[:, 5:6], in0=s_[:, 4:5], scalar1=float(-SCALE), scalar2=None, op0=ALU.mult), reads=[s_.b()], writes=[s_.b()])
                    K.op("dve", lambda e: e.tensor_tensor(out=s_[:, 6:7], in0=s_[:, 5:6], in1=prevb, op=ALU.add), reads=[s_.b(), self.flags.b()], writes=[s_.b()])

                LA = 3
                lazy = None
                lazy_layers = []
                if pss == 0:
                    lazy_layers = self.ada_take([li + 1])
                    if lazy_layers:
                        lazy = self.ada_gen(S1, lazy_layers, nbuf=3)

                def attn(h):
                    kf_, vf_, qh, s_ = khf[h % 2], vhf[h % 2], qhs[h % 2], sc[h % 2]
                    items = []
                    for qb in range(NTB):
                        blocks = []
                        if pss == 1:
                            for kb in range(NKB_OWN):
                                blocks.append((kb, 6, None))
                        for kbo in range(4 * (qb + 1)):
                            blocks.append((pss * NKB_OWN + kbo, 5, (kbo - 4 * qb) if kbo >= 4 * qb else None))
                        for i_, (blk, bcol, dj) in enumerate(blocks):
                            items.append((qb, blk, bcol, dj, i_ == 0, i_ == len(blocks) - 1))
                    n = len(items)
                    sts = [None] * n
                    accs = {}
                    gen = prep(h + 1) if h + 1 < H else iter(())
                    every = max(1, n // 8)
                    aevery = max(1, (n * H) // (48 * max(1, len(lazy_layers)) + 8)) if lazy is not None else 0

                    def qk(i):
                        qb, blk, bcol, dj, first, last = items[i]
                        c0 = 0 if dj is None else 128 * dj
                        st = self.psum((0, 4))
                        sts[i] = st
                        K.op("pe", lambda e: e.matmul(st[:, c0:TB], lhsT=kf_[0:96, blk * 128:(blk + 1) * 128], rhs=qh[0:96, qb * TB + c0:(qb + 1) * TB],
                                                      start=True, stop=True),
                             reads=[kf_.b(), qh.b((qb, "r")), qh.b((qb, "n"))], writes=[st.b()])
                    for i in range(min(LA, n)):
                        qk(i)
                    for i in range(n):
                        if i + LA < n:
                            qk(i + LA)
                        if i % every == every - 1:
                            next(gen, None)
                        if lazy is not None and i % aevery == aevery - 1:
                            next(lazy, None)
                        qb, blk, bcol, dj, first, last = items[i]
                        st = sts[i]
                        if first:
                            accs[qb] = self.psum((4, 2))
                        acc = accs[qb]
                        if dj is None:
                            pt = pts[cnt["pt"] % len(pts)]
                            cnt["pt"] += 1
                            c0 = 0
                            K.op("act", lambda e: e.activation(out=pt[:], in_=st[:], func=AF.Exp, bias=s_[:, bcol:bcol + 1], scale=float(SCALE)),
                                 reads=[st.b(), s_.b()], writes=[pt.b()])
                        else:
                            pt = ptd[dj]
                            c0 = 128 * dj
                            K.op("act", lambda e: e.activation(out=pt[:, c0:TB], in_=st[:, c0:TB], func=AF.Exp, bias=s_[:, bcol:bcol + 1], scale=float(SCALE)),
                                 reads=[st.b(), s_.b()], writes=[pt.b()])
                            K.op("pool", lambda e: e.memset(pt[64:128, c0:c0 + 64], 0.0), reads=[pt.b()], writes=[pt.b()])
                        K.op("pe", lambda e: e.matmul(acc[:, c0:TB], lhsT=vf_[:, blk, :], rhs=pt[:, c0:TB], start=first, stop=last),
                             reads=[vf_.b(), pt.b()], writes=[acc.b()], inc=last)
                        if self.warm and not last:
                            K.op("pe", lambda e: e.matmul(self.ps[7][64:128, :], lhsT=self.onesb[:, 0:64], rhs=self.onesb_w[:, :], start=True, stop=True),
                                 reads=[self.onesb.b(), self.onesb_w.b()], inc=False)
                        if last:
                            rc = rec[cnt["rec"] % 2]
                            cnt["rec"] += 1
                            K.op("dve", lambda e: e.reciprocal(out=rc[0:64, :], in_=acc[64:128, :]), reads=[acc.b()], writes=[rc.b()])
                            po = (h % 2) * 64
                            K.op("dve", lambda e: e.tensor_tensor(out=ao[po:po + 64, h // 2, qb * TB:(qb + 1) * TB], in0=acc[0:64, :], in1=rc[0:64, :], op=ALU.mult),
                                 reads=[acc.b(), rc.b()], writes=[ao.b((h // 2, qb))])
                    for _ in gen:
                        pass

                for _ in prep(0):
                    pass
                for h in range(H):
                    attn(h)
                if lazy is not None:
                    for _ in lazy:
                        pass
            with self.sbuf_scope() as S1:
                stg = [S1.sb("mstg%d" % i, [128, 1024], F32) for i in range(2)]
                wo = S1.sb("wo", [128, DC, D], BF16)
                self.load_w_bf16(wo, I["w_o"][j].rearrange("(kc p) n -> p kc n", p=128), stg, wsem, cache="wo%d" % j)
                ybs = [S1.sb("myb%d" % i, [128, DC, TB], F32) for i in range(2)]
                ws = self.norm_ws(S1, nset=2, pn=False)
                for tb in range(NTB):
                    yb = ybs[tb % 2]
                    for oc in range(DC):
                        pp = self.psum((0, 4))
                        for kc in range(DC):
                            K.op("pe", lambda e, pp=pp, oc=oc, kc=kc: e.matmul(pp[:], lhsT=wo[:, kc, oc * 128:(oc + 1) * 128], rhs=ao[:, kc, tb * TB:(tb + 1) * TB],
                                                                                 start=(kc == 0), stop=(kc == DC - 1)),
                                 reads=[wo.b(), ao.b((kc, tb))], writes=[pp.b()], inc=(kc == DC - 1))
                        self.evac("act" if oc % 2 == 0 else "dve", yb[:, oc, :], pp, [yb.b(oc)])
                    self.postnorm_residual(ws, li, 0, tb, yb)

    def kcb(self, j, h, p_):
        return self._cb("k", j, h, p_)

    def vcb(self, j, h, p_):
        return self._cb("v", j, h, p_)

    def _cb(self, kind, j, h, p_):
        if not hasattr(self, "_cbufs"):
            self._cbufs = {}
        key = (kind, j, h, p_)
        if key not in self._cbufs:
            self._cbufs[key] = Buf("cache%s" % (key,))
        return self._cbufs[key]

    def sbv(self, name, shape):
        if not hasattr(self, "_sbv"):
            self._sbv = {}
        if name not in self._sbv:
            assert shape is not None
            t = self.sb(name, shape, F32)
            self.K.dma(self.K.dsem("misc"), t[:], self.I[name], writes=[t.b()], hold=True)
            self._sbv[name] = t
        return self._sbv[name]


    def conv(self, pss, li):
        K, I = self.K, self.I
        wsem = [K.dsem("mw%d" % i) for i in range(2)]
        b1 = self.sbv("b_pw1T", None)
        cv = self.sbv("cv_vecT", None)
        wdw = self.sbv("w_dwT", None)
        uh = self.uhalo
        hflag = self.flags[:, 1:2]
        if pss == 0:
            K.op("dve", lambda e: e.memset(uh[:], 0.0), writes=[uh.b()])
        else:
            K.op("dve", lambda e: e.tensor_scalar(out=uh[:], in0=uh[:], scalar1=hflag, scalar2=None, op0=ALU.mult),
                 reads=[uh.b(), self.flags.b()], writes=[uh.b()])
        w1v = I["w_pw1"].rearrange("(kc p) n -> p kc n", p=128)
        w2v = I["w_pw2"].rearrange("(kc p) n -> p kc n", p=128)
        W = 2 * TB
        for sbk in range(2):
            with self.sbuf_scope() as SV:
                vT = SV.sb("vT", [128, DC, W], F32)
                with self.sbuf_scope() as SU:
                    uT = SU.sb("uT", [128, DC, 32 + W], BF16)
                    K.op("pool", lambda e: e.tensor_copy(out=uT[:, :, 0:32], in_=uh[:]), reads=[uh.b()], writes=[uT.b("h")])
                    with self.sbuf_scope() as S1:
                        w1 = S1.sb("wpw1", [128, DC, 2 * D], BF16)
                        with self.sbuf_scope() as SG:
                            stg = [SG.sb("cstg%d" % i, [128, 1024], F32) for i in range(2)]
                            for q4 in range(4):
                                self.load_w_bf16(_Vn(w1, q4 * 512, (q4 + 1) * 512), w1v[:, :, q4 * 512:(q4 + 1) * 512], stg, wsem, key=q4, cache="pw1_%d" % q4)
                        hT = S1.sb("chT", [128, DC, TB], BF16)
                        sg = [S1.sb("sig%d" % i, [128, TB], F32) for i in range(2)]
                        ws = self.norm_ws(S1, nset=1)
                        for t2 in range(2):
                            tb = sbk * 2 + t2
                            self.prenorm(ws, li, 0, tb, hT, 0, 0)
                            for oc in range(DC):
                                pa = self.psum((0, 4))
                                pb = self.psum((0, 4))
                                for (pp, off) in ((pa, 0), (pb, D)):
                                    for kc in range(DC):
                                        K.op("pe", lambda e, pp=pp, off=off, kc=kc: e.matmul(pp[:], lhsT=w1[:, kc, off + oc * 128:off + (oc + 1) * 128], rhs=hT[:, kc, :],
                                                                                               start=(kc == 0), stop=(kc == DC - 1)),
                                             reads=[w1.b((off + oc * 128) // 512), hT.b((0, kc))], writes=[pp.b()], inc=(kc == DC - 1))
                                sgt = sg[oc % 2]
                                K.op("act", lambda e: e.activation(out=sgt[:], in_=pb[:], func=AF.Sigmoid, bias=b1[:, DC + oc:DC + oc + 1]),
                                     reads=[pb.b(), b1.b()], writes=[sgt.b()])
                                K.op("dve", lambda e: e.scalar_tensor_tensor(out=uT[:, oc, 32 + t2 * TB:32 + (t2 + 1) * TB], in0=pa[:], scalar=b1[:, oc:oc + 1],
                                                                              in1=sgt[:], op0=ALU.add, op1=ALU.mult),
                                     reads=[pa.b(), sgt.b(), b1.b()], writes=[uT.b((oc, t2))])
                    K.op("pool", lambda e: e.tensor_copy(out=uh[:], in_=uT[:, :, W:W + 32]), reads=uT.bs([(c, 1) for c in range(DC)]), writes=[uh.b()])
                    with self.sbuf_scope() as S1:
                        dgs = [S1.sb("dg%d" % i, [128, CW, 128], BF16) for i in range(2)]
                        clazy = None
                        if pss == 0:
                            cl = self.ada_take([li + 1 + sbk])
                            if cl:
                                clazy = self.ada_gen(S1, cl, nbuf=4)
                        for c in range(DC):
                            dg = dgs[c % 2]
                            for jt in range(CW):
                                if jt % 2 == 0:
                                    K.op("act", lambda e, jt=jt: e.activation(out=dg[:, jt, :], in_=self.identb[:], func=AF.Copy, scale=wdw[:, c, jt:jt + 1]),
                                         reads=[self.identb.b(), wdw.b()], writes=[dg.b(jt % 2)])
                                else:
                                    K.op("dve", lambda e, jt=jt: e.tensor_scalar(out=dg[:, jt, :], in0=self.identb[:], scalar1=wdw[:, c, jt:jt + 1], scalar2=None, op0=ALU.mult),
                                         reads=[self.identb.b(), wdw.b()], writes=[dg.b(jt % 2)])
                            for t2 in range(2):
                                pp = self.psum((0, 4))
                                for jt in range(CW):
                                    c0 = t2 * TB + 2 + jt
                                    K.op("pe", lambda e, jt=jt, c0=c0: e.matmul(pp[:], lhsT=dg[:, jt, :], rhs=uT[:, c, c0:c0 + TB], start=(jt == 0), stop=(jt == CW - 1)),
                                         reads=[dg.b(jt % 2), uT.b("h"), uT.b((c, 0)), uT.b((c, 1))], writes=[pp.b()], inc=(jt == CW - 1))
                                    if clazy is not None and jt % 10 == 9:
                                        next(clazy, None)
                                if clazy is not None and c == DC - 1 and t2 == 1:
                                    for _ in clazy:
                                        pass
                                if (c + t2) % 2 == 0:
                                    K.op("act", lambda e: e.activation(out=vT[:, c, t2 * TB:(t2 + 1) * TB], in_=pp[:], func=AF.Identity, bias=cv[:, c:c + 1]),
                                         reads=[pp.b(), cv.b()], writes=[vT.b((c, t2))])
                                else:
                                    K.op("dve", lambda e: e.tensor_scalar(out=vT[:, c, t2 * TB:(t2 + 1) * TB], in0=pp[:], scalar1=cv[:, c:c + 1], scalar2=None, op0=ALU.add),
                                         reads=[pp.b(), cv.b()], writes=[vT.b((c, t2))])
                with self.sbuf_scope() as S1:
                    w2 = S1.sb("wpw2", [128, DC, D], BF16)
                    with self.sbuf_scope() as SG:
                        stg = [SG.sb("cstg%d" % i, [128, 1024], F32) for i in range(2)]
                        self.load_w_bf16(w2, w2v, stg, wsem, cache="pw2")
                    for t2 in range(2):
                        tb = sbk * 2 + t2
                        vs = slice(t2 * TB, (t2 + 1) * TB)
                        with self.sbuf_scope() as S2:
                            yb = S2.sb("cyb", [128, DC, TB], F32)
                            with self.sbuf_scope() as S3:
                                vb = S3.sb("vb", [128, DC, TB], BF16)
                                sqv = S3.sb("sqv", [128, DC, TB], BF16)
                                for c in range(DC):
                                    K.op("act", lambda e, c=c: e.activation(out=vb[:, c, :], in_=vT[:, c, vs], func=AF.Identity), reads=[vT.b((c, t2))], writes=[vb.b(c)])
                                    self.sq_ops(sqv, c, vT[:, c, vs], [vT.b((c, t2))], "dve")
                                p1 = self.psum((6, 2))
                                p2 = self.psum((6, 2))
                                for (pp, src) in ((p1, vb), (p2, sqv)):
                                    for c in range(DC):
                                        K.op("pe", lambda e, pp=pp, src=src, c=c: e.matmul(pp[:], lhsT=self.onesb[:], rhs=src[:, c, :], start=(c == 0), stop=(c == DC - 1)),
                                             reads=[src.b(c), self.onesb.b()], writes=[pp.b()], inc=(c == DC - 1))
                                m = S3.sb("lnm", [128, TB], F32)
                                msq = S3.sb("lnmsq", [128, TB], F32)
                                var = S3.sb("lnvar", [128, TB], F32)
                                lt = S3.sb("lnlt", [128, TB], F32)
                                rstd = S3.sb("lnrstd", [128, TB], F32)
                                nmr = S3.sb("lnnmr", [128, TB], F32)
                                K.op("act", lambda e: e.activation(out=m[:], in_=p1[:], func=AF.Identity, scale=1.0 / D), reads=[p1.b()], writes=[m.b()])
                                K.op("dve", lambda e: e.tensor_tensor(out=msq[:], in0=m[:], in1=m[:], op=ALU.mult), reads=[m.b()], writes=[msq.b()])
                                K.op("dve", lambda e: e.scalar_tensor_tensor(out=var[:], in0=p2[:], scalar=1.0 / D, in1=msq[:], op0=ALU.mult, op1=ALU.subtract),
                                     reads=[p2.b(), msq.b()], writes=[var.b()])
                                K.op("dve", lambda e: e.tensor_scalar(out=var[:], in0=var[:], scalar1=0.0, scalar2=None, op0=ALU.max), reads=[var.b()], writes=[var.b()])
                                K.op("act", lambda e: e.activation(out=lt[:], in_=var[:], func=AF.Ln, bias=self.epsc[:, 0:1]), reads=[var.b(), self.epsc.b()], writes=[lt.b()])
                                K.op("act", lambda e: e.activation(out=rstd[:], in_=lt[:], func=AF.Exp, scale=-0.5), reads=[lt.b()], writes=[rstd.b()])
                                K.op("dve", lambda e: e.scalar_tensor_tensor(out=nmr[:], in0=m[:], scalar=-1.0, in1=rstd[:], op0=ALU.mult, op1=ALU.mult),
                                     reads=[m.b(), rstd.b()], writes=[nmr.b()])
                                sT = S3.sb("sT", [128, DC, TB], BF16)
                                tt = [S3.sb("lntt%d" % i, [128, TB], F32) for i in range(4)]
                                for c in range(DC):
                                    ta, tb_ = tt[(2 * c) % 4], tt[(2 * c + 1) % 4]
                                    K.op("dve", lambda e, c=c, ta=ta: e.tensor_tensor(out=ta[:], in0=vT[:, c, vs], in1=rstd[:], op=ALU.mult),
                                         reads=[vT.b((c, t2)), rstd.b()], writes=[ta.b()])
                                    K.op("dve", lambda e, ta=ta, tb_=tb_: e.tensor_tensor(out=tb_[:], in0=ta[:], in1=nmr[:], op=ALU.add),
                                         reads=[ta.b(), nmr.b()], writes=[tb_.b()])
                                    K.op("act", lambda e, c=c, tb_=tb_: e.activation(out=sT[:, c, :], in_=tb_[:], func=AF.Silu, scale=cv[:, DC + c:DC + c + 1], bias=cv[:, 2 * DC + c:2 * DC + c + 1]),
                                         reads=[tb_.b(), cv.b()], writes=[sT.b(c)])
                                for oc in range(DC):
                                    pp = self.psum((0, 4))
                                    for kc in range(DC):
                                        K.op("pe", lambda e, pp=pp, oc=oc, kc=kc: e.matmul(pp[:], lhsT=w2[:, kc, oc * 128:(oc + 1) * 128], rhs=sT[:, kc, :], start=(kc == 0), stop=(kc == DC - 1)),
                                             reads=[w2.b(), sT.b(kc)], writes=[pp.b()], inc=(kc == DC - 1))
                                    if oc % 2 == 0:
                                        K.op("act", lambda e, pp=pp, oc=oc: e.activation(out=yb[:, oc, :], in_=pp[:], func=AF.Identity, bias=cv[:, 3 * DC + oc:3 * DC + oc + 1]),
                                             reads=[pp.b(), cv.b()], writes=[yb.b(oc)])
                                    else:
                                        K.op("dve", lambda e, pp=pp, oc=oc: e.tensor_scalar(out=yb[:, oc, :], in0=pp[:], scalar1=cv[:, 3 * DC + oc:3 * DC + oc + 1], scalar2=None, op0=ALU.add),
                                             reads=[pp.b(), cv.b()], writes=[yb.b(oc)])
                            with self.sbuf_scope() as S3:
                                self.postnorm_residual(self.norm_ws(S3, nset=1, pn=False), li, 0, tb, yb)

    def pool(self, pss, li):
        K, I = self.K, self.I
        wsem = [K.dsem("mw%d" % i) for i in range(2)]
        pv = self.sbv("pl_vecT", None)
        pf = self.sbv("poolfac", None)
        hh = self.hhalo
        hflag = self.flags[:, 1:2]
        if pss == 0:
            K.op("dve", lambda e: e.memset(hh[:], 0.0), writes=[hh.b()])
        else:
            K.op("dve", lambda e: e.tensor_scalar(out=hh[:], in0=hh[:], scalar1=hflag, scalar2=None, op0=ALU.mult),
                 reads=[hh.b(), self.flags.b()], writes=[hh.b()])
        with self.sbuf_scope() as S0:
            pw = S0.sb("poolw", [128, 8, 256], BF16)
            with self.sbuf_scope() as SG:
                stg = [SG.sb("pstg%d" % i, [128, 1024], F32) for i in range(2)]
                self.load_w_bf16(pw, I["pool_w"].rearrange("g (kc p) n -> p (g kc) n", p=128), stg, wsem, cache="poolw")
            WW = 16 + TB
            pws = self.norm_ws(S0, nset=2)
            for tb in range(NTB):
                with self.sbuf_scope() as S1:
                    hp = S1.sb("hp", [128, DC, WW], F32)
                    K.op("pool", lambda e: e.tensor_copy(out=hp[:, :, 0:16], in_=hh[:]), reads=[hh.b()], writes=[hp.b("h")])
                    self.prenorm(pws, li, 0, tb, _Vc(hp, 16), 0, 0)
                    K.op("pool", lambda e: e.tensor_copy(out=hh[:], in_=hp[:, :, TB:TB + 16]), reads=hp.bs([(0, c) for c in range(DC)]), writes=[hh.b()])
                    pT = S1.sb("ppT", [128, DC, TB], BF16)
                    yb = S1.sb("pyb", [128, DC, TB], F32)
                    sa = [S1.sb("psa%d" % i, [128, WW], F32) for i in range(4)]
                    for c in range(DC):
                        g = c // 2
                        eng = "dve"
                        bufs = sa[0:2] if c % 2 == 0 else sa[2:4]
                        cur_ap = lambda lo, hi, c=c: hp[:, c, lo:hi]
                        cur_b = [hp.b("h"), hp.b((0, c))]
                        lo = 0
                        for lvl in range(g + 1):
                            sh = 1 << lvl
                            dst = bufs[lvl % 2]
                            nlo = lo + sh
                            K.op(eng, lambda e, dst=dst, cur_ap=cur_ap, nlo=nlo, sh=sh: e.tensor_tensor(out=dst[:, nlo:WW], in0=cur_ap(nlo, WW), in1=cur_ap(nlo - sh, WW - sh), op=ALU.add),
                                 reads=cur_b, writes=[dst.b()])
                            cur_ap = (lambda lo_, hi_, dst=dst: dst[:, lo_:hi_])
                            cur_b = [dst.b()]
                            lo = nlo
                        if tb == 0:
                            K.op(eng, lambda e, cur_ap=cur_ap: e.tensor_tensor(out=cur_ap(16, 32), in0=cur_ap(16, 32), in1=pf[:, pss, g, :], op=ALU.mult),
                                 reads=cur_b + [pf.b()], writes=cur_b)
                        K.op("dve", lambda e, cur_ap=cur_ap, c=c, g=g: e.scalar_tensor_tensor(out=pT[:, c, :], in0=cur_ap(16, WW), scalar=1.0 / PWIN[g], in1=hp[:, c, 16:WW],
                                                                                               op0=ALU.mult, op1=ALU.subtract),
                             reads=cur_b + [hp.b((0, c))], writes=[pT.b(c)])
                    for g in range(4):
                        for o2 in range(2):
                            oc = 2 * g + o2
                            pp = self.psum((0, 4))
                            for k2 in range(2):
                                K.op("pe", lambda e, pp=pp, g=g, o2=o2, k2=k2: e.matmul(pp[:], lhsT=pw[:, 2 * g + k2, o2 * 128:(o2 + 1) * 128], rhs=pT[:, 2 * g + k2, :],
                                                                                          start=(k2 == 0), stop=(k2 == 1)),
                                     reads=[pw.b(), pT.b(2 * g + k2)], writes=[pp.b()], inc=(k2 == 1))
                            K.op("dve", lambda e, pp=pp, oc=oc: e.tensor_scalar(out=yb[:, oc, :], in0=pp[:], scalar1=pv[:, oc:oc + 1], scalar2=pv[:, DC + oc:DC + oc + 1],
                                                                                 op0=ALU.add, op1=ALU.mult),
                                 reads=[pp.b(), pv.b()], writes=[yb.b(oc)])
                    self.postnorm_residual(pws, li, 0, tb, yb)


class _Vc:
    def __init__(self, tl, off):
        self.tl, self.off = tl, off

    def __getitem__(self, idx):
        p, c, n = idx
        return self.tl.t[p, c, n.start + self.off:n.stop + self.off]

    def b(self, key=0):
        return self.tl.b(key)


class _Vn:
    def __init__(self, tl, lo, hi):
        self.tl, self.lo, self.hi = tl, lo, hi

    def __getitem__(self, idx):
        p, a, n = idx
        return self.tl.t[p, a, self.lo:self.hi]

    def b(self, key=0):
        return self.tl.b(key)


class _Scope:
    def __init__(self, prog):
        self.p = prog
        self.cms = []

    def __enter__(self):
        return self

    def sb(self, name, shape, dt=F32):
        self.p.uid += 1
        nm = "%s_%d" % (name, self.p.uid)
        cm = self.p.nc.sbuf_tensor(nm, list(shape), dt)
        t = cm.__enter__()
        self.cms.append(cm)
        return Tl(t, nm)

    def __exit__(self, *a):
        self.p.K.barrier()
        for cm in reversed(self.cms):
            cm.__exit__(None, None, None)
        return False


def _t128(v, n):
    return np.ascontiguousarray(np.asarray(v, np.float32).reshape(n, 128).T)


def make_in_maps(inp):
    x = np.asarray(inp["x"], np.float32)
    B = x.shape[0]
    pos = np.asarray(inp["positions"]).astype(np.int32)
    shared = {}
    shared["ada_wT"] = np.ascontiguousarray(np.asarray(inp["ada_w"], np.float32).transpose(0, 2, 1))
    shared["ada_bT"] = np.concatenate([_t128(inp["ada_b"][i], 48) for i in range(DEPTH)], axis=1)
    shared["norm_gT"] = np.concatenate([_t128(inp["norm_g"][i, j], DC) for i in range(DEPTH) for j in range(4)], axis=1)
    shared["w_dq"] = np.ascontiguousarray(inp["mla_w_dq"], np.float32)
    shared["qn_gT"] = np.concatenate([_t128(inp["mla_q_norm_g"][j], 3) for j in range(2)], axis=1)
    wuq = np.asarray(inp["mla_w_uq"], np.float32).reshape(2, QL, H, 96)
    shared["w_uqA"] = np.ascontiguousarray(wuq.reshape(2, QL, H * 96))
    shared["w_uqB"] = np.ascontiguousarray(np.concatenate([wuq[..., 80:96], wuq[..., 64:80]], axis=-1).reshape(2, QL, H * 32))
    wdkv = np.asarray(inp["mla_w_dkv"], np.float32)
    shared["w_dkvA"] = np.ascontiguousarray(wdkv)
    shared["w_dkvB"] = np.ascontiguousarray(np.concatenate([wdkv[..., 272:288], wdkv[..., 256:272]], axis=-1))
    shared["kvn_gT"] = np.concatenate([_t128(inp["mla_kv_norm_g"][j], 2) for j in range(2)], axis=1)
    wukv = np.asarray(inp["mla_w_ukv"], np.float32).reshape(2, KVL, H, 128)
    shared["w_ukvK"] = np.ascontiguousarray(wukv[..., 0:64].reshape(2, KVL, H * 64))
    shared["w_ukvV"] = np.ascontiguousarray(wukv[..., 64:128].reshape(2, KVL, H * 64))
    shared["w_o"] = np.ascontiguousarray(inp["mla_w_o"], np.float32)
    shared["w_pw1"] = np.ascontiguousarray(inp["conv_w_pw1"][0], np.float32)
    shared["b_pw1T"] = _t128(inp["conv_b_pw1"][0], 16)
    shared["w_dwT"] = np.ascontiguousarray(np.asarray(inp["conv_w_dw"][0], np.float32).T.reshape(DC, 128, CW).transpose(1, 0, 2))
    shared["cv_vecT"] = np.concatenate([_t128(inp[k][0], DC) for k in ("conv_b_dw", "conv_ln_g", "conv_ln_b", "conv_b_pw2")], axis=1)
    shared["w_pw2"] = np.ascontiguousarray(inp["conv_w_pw2"][0], np.float32)
    shared["pool_w"] = np.ascontiguousarray(inp["pool_w"][0], np.float32)
    shared["pl_vecT"] = np.concatenate([_t128(np.asarray(inp["pool_b"][0]).reshape(-1), DC), _t128(inp["pool_scale"][0], DC)], axis=1)
    shared["ffn_w1"] = np.ascontiguousarray(inp["ffn_w1"], np.float32)
    shared["ffn_w2"] = np.ascontiguousarray(inp["ffn_w2"], np.float32)
    consts = np.zeros((128, 136), np.float32)
    consts[:, :128] = np.eye(128, dtype=np.float32)
    invf = (10000.0 ** (-np.arange(0, 32, 2, dtype=np.float32) / np.float32(32.0))).astype(np.float32)
    consts[0:16, 128] = invf
    consts[16:32, 128] = invf
    consts[0:16, 129] = -1.0
    consts[16:32, 129] = 1.0
    shared["consts"] = consts
    fac_start = np.ones((4, 16), np.float32)
    for g, w in enumerate(PWIN):
        for t in range(16):
            fac_start[g, t] = float(w) / float(min(t + 1, w))
    maps = []
    for core in range(2 * B):
        b, half = core // 2, core % 2
        m = dict(shared)
        own = x[b, half * T:(half + 1) * T]
        m["xT_own"] = np.ascontiguousarray(own.T)
        m["pos_own"] = np.ascontiguousarray(np.broadcast_to(pos[b, half * T:(half + 1) * T].reshape(1, T), (32, T)))
        if half == 1:
            m["xT_prev"] = np.ascontiguousarray(x[b, 0:T].T)
            m["pos_prev"] = np.ascontiguousarray(np.broadcast_to(pos[b, 0:T].reshape(1, T), (32, T)))
        else:
            m["xT_prev"] = np.zeros((D, T), np.float32)
            m["pos_prev"] = np.zeros((32, T), np.int32)
        m["c"] = _t128(inp["c"][b], DC)
        m["c_rep"] = np.ascontiguousarray(np.broadcast_to(np.asarray(inp["c"][b], np.float32).reshape(1, D), (128, D)))
        fl = np.zeros((128, 4), np.float32)
        fl[:, 0] = 0.0 if half == 1 else NEG
        fl[:, 1] = 1.0 if half == 1 else 0.0
        m["flags"] = fl
        pf = np.ones((2, 4, 16), np.float32)
        pf[0] = fac_start
        if half == 0:
            pf[1] = fac_start
        m["poolfac"] = np.ascontiguousarray(np.broadcast_to(pf[None], (128, 2, 4, 16)))
        maps.append(m)
    return maps


_PROG = {}


def get_prog(**kw):
    key = tuple(sorted((k, str(v)) for k, v in kw.items()))
    if key not in _PROG:
        _PROG[key] = Prog(**kw)
    return _PROG[key]


def kernel(**inputs):
    x = np.asarray(inputs["x"])
    B, S, _ = x.shape
    prog = get_prog()
    maps = make_in_maps(inputs)
    res = run_bass_kernel_spmd(prog.nc, maps, core_ids=list(range(8)))
    out = np.empty((B, S, D), np.float32)
    for core in range(8):
        b, half = core // 2, core % 2
        out[b, half * T:(half + 1) * T] = np.asarray(res.results[core]["outT"]).T
    return out
```
